# Optimizing a Trainium2 kernel written in Bass

```python
import jax, jax.numpy as jnp
from jax import lax
import numpy as np

D_MODEL = 1024
BATCH = 4
SEQ = 4096
DEPTH = 2
DEC_BATCH = 8
DEC_SEQ = 16
PAST_LEN = 1024

CHUNK = 64
D_A = 512
G_A = 4
A_CHUNK = 128
D_B = 512
H_B = 8
DH_B = D_B // H_B
CONV_W = 4
LRU_C = 8.0
H_C = 8
D_NOPE = 64
D_ROPE = 32
D_V = 64
Q_LORA = 256
KV_LORA = 128
D_C = H_C * D_V
ROPE_BASE = 10000.0
Q_BLOCK = 128
N_MEM = 256
H_M = 4
DH_M = 64
D_M = H_M * DH_M
N_BRANCH = 4
ALPHA = (2.0 * DEPTH) ** 0.25
BETA = (8.0 * DEPTH) ** -0.25
EPS = 1e-6
SPLITS = (D_A, D_A, D_A, D_B, D_B, Q_LORA, KV_LORA, D_ROPE, D_C, D_M, N_BRANCH * D_MODEL)
D_IN = 3 * D_A + 2 * D_B + Q_LORA + KV_LORA + D_ROPE + D_C + D_M + N_BRANCH * D_MODEL
D_BR = D_A + D_B + D_C + D_M

kernel_name = 'hybrid_gmlp_rglru_mla_stream_step'


def layer_norm(x, g, b):
    xf = x.astype(jnp.float32)
    mu = jnp.mean(xf, -1, keepdims=True)
    var = jnp.mean(jnp.square(xf - mu), -1, keepdims=True)
    return ((xf - mu) * lax.rsqrt(var + EPS) * g + b).astype(x.dtype)


def rms_norm(x, g):
    xf = x.astype(jnp.float32)
    return (xf * lax.rsqrt(jnp.mean(jnp.square(xf), -1, keepdims=True) + EPS) * g).astype(x.dtype)


def split_cols(z):
    parts, start = [], 0
    for n in SPLITS:
        parts.append(z[..., start:start + n])
        start += n
    return parts


def rope(x, pos):
    half = D_ROPE // 2
    freq = ROPE_BASE ** (-jnp.arange(half, dtype=jnp.float32) / half)
    ang = pos.astype(jnp.float32)[:, None] * freq[None, :]
    ang = ang.reshape((1, pos.shape[0]) + (1,) * (x.ndim - 3) + (half,))
    cos, sin = jnp.cos(ang), jnp.sin(ang)
    xf = x.astype(jnp.float32)
    x1, x2 = xf[..., :half], xf[..., half:]
    return jnp.concatenate([x1 * cos - x2 * sin, x1 * sin + x2 * cos], -1).astype(x.dtype)


def gmlp_branch(u, v, ln_g, ln_b, w_s, b_s):
    B, T, _ = v.shape
    u = jax.nn.gelu(u)
    v = layer_norm(jax.nn.gelu(v), ln_g, ln_b)
    L = min(T, A_CHUNK)
    vc = v.reshape(B, T // L, L, G_A, D_A // G_A)
    w = jnp.tril(w_s[:, :L, :L])
    s = jnp.einsum('gpq,bnqgc->bnpgc', w, vc) + jnp.transpose(b_s[:, :L])[None, None, :, :, None]
    return u * s.reshape(B, T, D_A), v


def _lin_combine(left, right):
    a_l, b_l = left
    a_r, b_r = right
    return a_l * a_r, a_r * b_l + b_r


def rglru_branch(xb, conv0, h0, conv_w, conv_b, w_r, b_r, w_i, b_i, lam):
    B, T, _ = xb.shape
    xp = jnp.concatenate([conv0.astype(xb.dtype), xb], axis=1)
    xc = conv_b
    for k in range(CONV_W):
        xc = xc + xp[:, k:k + T] * conv_w[k]
    xh = xc.reshape(B, T, H_B, DH_B)
    r = jax.nn.sigmoid((jnp.einsum('bthi,hij->bthj', xh, w_r).reshape(B, T, D_B) + b_r).astype(jnp.float32))
    i = jax.nn.sigmoid((jnp.einsum('bthi,hij->bthj', xh, w_i).reshape(B, T, D_B) + b_i).astype(jnp.float32))
    log_a = -LRU_C * r * jax.nn.softplus(-lam.astype(jnp.float32))
    a = jnp.exp(log_a)
    bval = jnp.sqrt(-jnp.expm1(2.0 * log_a)) * (i * xc.astype(jnp.float32))
    acc_a, acc_b = lax.associative_scan(_lin_combine, (a, bval), axis=1)
    h = acc_a * h0.astype(jnp.float32)[:, None, :] + acc_b
    return h.astype(xb.dtype), xp[:, -(CONV_W - 1):], h[:, -1].astype(xb.dtype)


def _mla_block(q_nope, q_rope, k_nope, k_rope, v, q_pos, k_pos):
    scale = (D_NOPE + D_ROPE) ** -0.5
    s = (jnp.einsum('bqhd,bkhd->bhqk', q_nope, k_nope)
         + jnp.einsum('bqhd,bkd->bhqk', q_rope, k_rope)).astype(jnp.float32) * scale
    mask = (k_pos[None, :] // CHUNK) <= (q_pos[:, None] // CHUNK)
    s = jnp.where(mask[None, None], s, -1e30)
    p = jax.nn.softmax(s, axis=-1).astype(v.dtype)
    return jnp.einsum('bhqk,bkhd->bqhd', p, v)


def chunk_causal_mla(q_nope, q_rope, k_nope, k_rope, v, q_pos, k_pos):
    B, T = q_nope.shape[:2]
    if T <= Q_BLOCK:
        return _mla_block(q_nope, q_rope, k_nope, k_rope, v, q_pos, k_pos)
    nb = T // Q_BLOCK
    qn = jnp.swapaxes(q_nope.reshape(B, nb, Q_BLOCK, H_C, D_NOPE), 0, 1)
    qr = jnp.swapaxes(q_rope.reshape(B, nb, Q_BLOCK, H_C, D_ROPE), 0, 1)
    qp = q_pos.reshape(nb, Q_BLOCK)
    out = lax.map(lambda a: _mla_block(a[0], a[1], k_nope, k_rope, v, a[2], k_pos), (qn, qr, qp))
    return jnp.swapaxes(out, 0, 1).reshape(B, T, H_C, D_V)


def mla_branch(c_q, c_kv, kr_raw, q_norm, w_uq, kv_norm, w_ukv, q_pos, past_ckv, past_kr):
    B, T, _ = c_q.shape
    q = (rms_norm(c_q, q_norm) @ w_uq).reshape(B, T, H_C, D_NOPE + D_ROPE)
    q_nope, q_rope = q[..., :D_NOPE], rope(q[..., D_NOPE:], q_pos)
    ckv_new = rms_norm(c_kv, kv_norm)
    kr_new = rope(kr_raw, q_pos)
    if past_ckv is None:
        ckv_all, kr_all, k_pos = ckv_new, kr_new, q_pos
    else:
        ckv_all = jnp.concatenate([past_ckv, ckv_new], axis=1)
        kr_all = jnp.concatenate([past_kr, kr_new], axis=1)
        k_pos = jnp.arange(ckv_all.shape[1])
    Tk = ckv_all.shape[1]
    kv = (ckv_all @ w_ukv).reshape(B, Tk, H_C, D_NOPE + D_V)
    o = chunk_causal_mla(q_nope, q_rope, kv[..., :D_NOPE], kr_all, kv[..., D_NOPE:], q_pos, k_pos)
    return o.reshape(B, T, D_C), ckv_new, kr_new


def mem_attend(q, mem_k, mem_v):
    s = jnp.einsum('bqhd,bkhd->bhqk', q, mem_k).astype(jnp.float32) * (DH_M ** -0.5)
    p = jax.nn.softmax(s, axis=-1).astype(mem_v.dtype)
    return jnp.einsum('bhqk,bkhd->bqhd', p, mem_v)


def trunk_layer(x, pos, mem_k, mem_v, past_ckv, past_kr, conv0, h0,
                w_in, gmlp_ln_g, gmlp_ln_b, gmlp_ws, gmlp_bs,
                lru_conv_w, lru_conv_b, lru_w_r, lru_b_r, lru_w_i, lru_b_i, lru_lambda,
                mla_q_norm, mla_w_uq, mla_kv_norm, mla_w_ukv, w_br, w_out, ln_g, ln_b):
    B, T, _ = x.shape
    (a_u, a_v, a_g, b_x, b_g, c_q, c_kv, c_kr, c_g, m_q, gate_logits) = split_cols(x @ w_in)
    oa, v_rows = gmlp_branch(a_u, a_v, gmlp_ln_g, gmlp_ln_b, gmlp_ws, gmlp_bs)
    oa = oa * jax.nn.silu(a_g)
    ob, conv_new, h_new = rglru_branch(b_x, conv0, h0, lru_conv_w, lru_conv_b,
                                       lru_w_r, lru_b_r, lru_w_i, lru_b_i, lru_lambda)
    ob = ob * jax.nn.silu(b_g)
    oc, ckv_new, kr_new = mla_branch(c_q, c_kv, c_kr, mla_q_norm, mla_w_uq, mla_kv_norm, mla_w_ukv,
                                     pos, past_ckv, past_kr)
    oc = oc * jax.nn.silu(c_g)
    om = mem_attend(m_q.reshape(B, T, H_M, DH_M), mem_k, mem_v).reshape(B, T, D_M)
    g = jax.nn.sigmoid(gate_logits.astype(jnp.float32)).astype(x.dtype).reshape(B, T, N_BRANCH, D_MODEL)
    ya = oa @ w_br[:D_A]
    yb = ob @ w_br[D_A:D_A + D_B]
    yc = oc @ w_br[D_A + D_B:D_A + D_B + D_C]
    ym = om @ w_br[D_A + D_B + D_C:]
    merged = g[:, :, 0] * ya + g[:, :, 1] * yb + g[:, :, 2] * yc + g[:, :, 3] * ym
    y = merged @ w_out
    x_new = layer_norm(ALPHA * x + y, ln_g, ln_b)
    return x_new, v_rows, conv_new, h_new, ckv_new, kr_new


def setup_inputs(seed: int = 0) -> dict:
    key = jax.random.key(seed)
    ks = iter(jax.random.split(key, 48))
    def nrm(shape, scale):
        return jax.random.normal(next(ks), shape, jnp.float32) * scale
    a0 = jax.random.uniform(next(ks), (DEPTH, D_B), jnp.float32, minval=0.9, maxval=0.999)
    p0 = a0 ** (1.0 / LRU_C)
    lru_lambda = jnp.log(p0) - jnp.log1p(-p0)
    w_br = jnp.concatenate([nrm((DEPTH, D_A, D_MODEL), BETA * D_A ** -0.5),
                            nrm((DEPTH, D_B, D_MODEL), BETA * D_B ** -0.5),
                            nrm((DEPTH, D_C, D_MODEL), BETA * D_C ** -0.5),
                            nrm((DEPTH, D_M, D_MODEL), BETA * D_M ** -0.5)], axis=1)
    return {
        'x_prompt': nrm((BATCH, SEQ, D_MODEL), 1.0),
        'x_sample': nrm((DEC_BATCH, DEC_SEQ, D_MODEL), 1.0),
        'mem_prompt': nrm((BATCH, N_MEM, D_MODEL), 1.0),
        'cache_mla_ckv': nrm((DEPTH, DEC_BATCH, PAST_LEN, KV_LORA), 1.0),
        'cache_mla_krope': nrm((DEPTH, DEC_BATCH, PAST_LEN, D_ROPE), 1.0),
        'cache_mem_k': nrm((DEPTH, DEC_BATCH, N_MEM, H_M, DH_M), 1.0),
        'cache_mem_v': nrm((DEPTH, DEC_BATCH, N_MEM, H_M, DH_M), 1.0),
        'state_lru_h': nrm((DEPTH, DEC_BATCH, D_B), 0.5),
        'state_lru_conv': nrm((DEPTH, DEC_BATCH, CONV_W - 1, D_B), 1.0),
        'w_in': nrm((DEPTH, D_MODEL, D_IN), D_MODEL ** -0.5),
        'gmlp_ln_g': 1.0 + nrm((DEPTH, D_A), 0.02),
        'gmlp_ln_b': nrm((DEPTH, D_A), 0.02),
        'gmlp_ws': nrm((DEPTH, G_A, A_CHUNK, A_CHUNK), 0.5 * A_CHUNK ** -0.5),
        'gmlp_bs': 1.0 + nrm((DEPTH, G_A, A_CHUNK), 0.1),
        'lru_conv_w': nrm((DEPTH, CONV_W, D_B), CONV_W ** -0.5),
        'lru_conv_b': nrm((DEPTH, D_B), 0.02),
        'lru_w_r': nrm((DEPTH, H_B, DH_B, DH_B), DH_B ** -0.5),
        'lru_b_r': nrm((DEPTH, D_B), 0.02),
        'lru_w_i': nrm((DEPTH, H_B, DH_B, DH_B), DH_B ** -0.5),
        'lru_b_i': nrm((DEPTH, D_B), 0.02),
        'lru_lambda': lru_lambda,
        'mla_q_norm': 1.0 + nrm((DEPTH, Q_LORA), 0.02),
        'mla_w_uq': nrm((DEPTH, Q_LORA, H_C * (D_NOPE + D_ROPE)), Q_LORA ** -0.5),
        'mla_kv_norm': 1.0 + nrm((DEPTH, KV_LORA), 0.02),
        'mla_w_ukv': nrm((DEPTH, KV_LORA, H_C * (D_NOPE + D_V)), KV_LORA ** -0.5),
        'mem_w_k': nrm((DEPTH, D_MODEL, D_M), D_MODEL ** -0.5),
        'mem_w_v': nrm((DEPTH, D_MODEL, D_M), D_MODEL ** -0.5),
        'w_br': w_br,
        'w_out': nrm((DEPTH, D_MODEL, D_MODEL), BETA * D_MODEL ** -0.5),
        'ln_g': 1.0 + nrm((DEPTH, D_MODEL), 0.02),
        'ln_b': nrm((DEPTH, D_MODEL), 0.02),
    }


def reference(x_prompt, x_sample, mem_prompt, cache_mla_ckv, cache_mla_krope, cache_mem_k, cache_mem_v,
              state_lru_h, state_lru_conv, w_in, gmlp_ln_g, gmlp_ln_b, gmlp_ws, gmlp_bs,
              lru_conv_w, lru_conv_b, lru_w_r, lru_b_r, lru_w_i, lru_b_i, lru_lambda,
              mla_q_norm, mla_w_uq, mla_kv_norm, mla_w_ukv, mem_w_k, mem_w_v, w_br, w_out, ln_g, ln_b):
    Bp, Tp, _ = x_prompt.shape
    Bs, Ts, _ = x_sample.shape
    pos_p = jnp.arange(Tp)
    pos_s = PAST_LEN + jnp.arange(Ts)
    xp, xs = x_prompt, x_sample
    p_ckv, p_kr, p_mk, p_mv, p_h, p_conv = [], [], [], [], [], []
    s_ckv, s_kr, s_h, s_conv, s_v = [], [], [], [], []
    for l in range(DEPTH):
        lw = (w_in[l], gmlp_ln_g[l], gmlp_ln_b[l], gmlp_ws[l], gmlp_bs[l],
              lru_conv_w[l], lru_conv_b[l], lru_w_r[l], lru_b_r[l], lru_w_i[l], lru_b_i[l], lru_lambda[l],
              mla_q_norm[l], mla_w_uq[l], mla_kv_norm[l], mla_w_ukv[l], w_br[l], w_out[l], ln_g[l], ln_b[l])
        mk = (mem_prompt @ mem_w_k[l]).reshape(Bp, N_MEM, H_M, DH_M)
        mv = (mem_prompt @ mem_w_v[l]).reshape(Bp, N_MEM, H_M, DH_M)
        conv0 = jnp.zeros((Bp, CONV_W - 1, D_B), xp.dtype)
        h0 = jnp.zeros((Bp, D_B), xp.dtype)
        xp, _, conv_n, h_n, ckv_n, kr_n = trunk_layer(xp, pos_p, mk, mv, None, None, conv0, h0, *lw)
        p_ckv.append(ckv_n); p_kr.append(kr_n); p_mk.append(mk); p_mv.append(mv)
        p_h.append(h_n); p_conv.append(conv_n)
        xs, v_n, conv_n, h_n, ckv_n, kr_n = trunk_layer(
            xs, pos_s, cache_mem_k[l], cache_mem_v[l], cache_mla_ckv[l], cache_mla_krope[l],
            state_lru_conv[l], state_lru_h[l], *lw)
        s_ckv.append(ckv_n); s_kr.append(kr_n); s_h.append(h_n); s_conv.append(conv_n); s_v.append(v_n)
    return (xp, xs,
            jnp.stack(p_ckv), jnp.stack(p_kr), jnp.stack(p_mk), jnp.stack(p_mv), jnp.stack(p_h), jnp.stack(p_conv),
            jnp.stack(s_ckv), jnp.stack(s_kr), jnp.stack(s_h), jnp.stack(s_conv), jnp.stack(s_v))
```

```python
import contextlib
import numpy as np
import concourse.bass as bass
import concourse.mybir as mybir
from concourse.bass_utils import run_bass_kernel_spmd

F32 = mybir.dt.float32
BF16 = mybir.dt.bfloat16
AF = mybir.ActivationFunctionType
ALU = mybir.AluOpType
PE, ACT, DVE, POOL, SP = "pe", "act", "dve", "pool", "sp"

NL = 2
D = 1024
TP = 4096
TS = 16
TT = TP + TS
PAST = 1024
KC = TP + PAST + TS
NPV = 51
D_IN = 7840
SM_SCALE = 96.0 ** -0.5
ALPHA = (2.0 * NL) ** 0.25
EPS = 1e-6
GELU_C = 0.7978845608028654
BLOCKS = [(i * 512, 512, 0) for i in range(8)] + [(TP, TS, 1)]
NSLOT_A = 6
NSLOT_B = 2
SEM_LIMIT = 30000


class Dep:
    __slots__ = ("w", "r", "name")

    def __init__(self, name=""):
        self.w = None
        self.r = {}
        self.name = name


class Prog:
    def __init__(self, nc):
        self.nc = nc
        self.q = {e: [] for e in (PE, ACT, DVE, POOL, SP)}
        self.eng = {PE: nc.tensor, ACT: nc.scalar, DVE: nc.vector, POOL: nc.gpsimd, SP: nc.sync}
        self.csem = {}
        self.cnt = {}
        self.nsem = 0
        for e in (PE, ACT, DVE, POOL):
            self._new_csem(e)
        self.seen = {e: {} for e in self.q}
        self.dsem_val = {}

    def _new_csem(self, e):
        self.nsem += 1
        self.csem[e] = self.nc.alloc_semaphore("c%s%d" % (e, self.nsem))
        self.cnt[e] = 0

    def dma_sem(self, name):
        s = self.nc.alloc_semaphore(name)
        self.dsem_val[s] = 0
        return s

    def _need(self, engine, ev, waits):
        if ev is None:
            return
        sem, val, src = ev
        if src == engine and engine == PE:
            return
        if src == "dma":
            val = self.dsem_val[sem]
        if self.seen[engine].get(sem, 0) >= val:
            return
        if waits.get(sem, 0) < val:
            waits[sem] = val

    def _waits(self, engine, reads, writes):
        waits = {}
        for d in reads:
            self._need(engine, d.w, waits)
        for d in writes:
            self._need(engine, d.w, waits)
            for e2, ev in d.r.items():
                if e2 == engine:
                    continue
                self._need(engine, ev, waits)
        q = self.q[engine]
        eng = self.eng[engine]
        for sem, val in waits.items():
            q.append((lambda eng=eng, sem=sem, val=val: eng.wait_ge(sem, val)))
            self.seen[engine][sem] = val

    def op(self, engine, fn, reads=(), writes=()):
        self._waits(engine, reads, writes)
        if self.cnt[engine] >= SEM_LIMIT:
            self._new_csem(engine)
        self.cnt[engine] += 1
        val = self.cnt[engine]
        sem = self.csem[engine]
        eng = self.eng[engine]
        self.q[engine].append((lambda eng=eng, fn=fn, sem=sem: fn(eng).then_inc(sem, 1)))
        ev = (sem, val, engine)
        for d in reads:
            d.r[engine] = ev
        for d in writes:
            d.w = ev
            d.r = {}
        return ev

    def dma(self, engine, out, in_, sem, reads=(), writes=()):
        self._waits(engine, reads, writes)
        self.dsem_val[sem] += 16
        val = self.dsem_val[sem]
        eng = self.eng[engine]
        self.q[engine].append(
            (lambda eng=eng, out=out, in_=in_, sem=sem: eng.dma_start(out=out, in_=in_).then_inc(sem, 16)))
        ev = (sem, val, "dma")
        for d in reads:
            d.r["dma:%d" % id(sem)] = ev
        for d in writes:
            d.w = ev
            d.r = {}
        return ev

    def barrier(self):
        targets = [(self.csem[e], self.cnt[e]) for e in (PE, ACT, DVE, POOL) if self.cnt[e] > 0]
        targets += [(s, v) for s, v in self.dsem_val.items() if v > 0]
        for e in (PE, ACT, DVE, POOL, SP):
            eng = self.eng[e]
            for sem, val in targets:
                if self.seen[e].get(sem, 0) >= val:
                    continue
                self.q[e].append((lambda eng=eng, sem=sem, val=val: eng.wait_ge(sem, val)))
                self.seen[e][sem] = val

    def run(self):
        nc = self.nc
        with nc.Block() as block:
            @block.tensor
            def _(e):
                for f in self.q[PE]:
                    f()

            @block.scalar
            def _(e):
                for f in self.q[ACT]:
                    f()

            @block.vector
            def _(e):
                for f in self.q[DVE]:
                    f()

            @block.gpsimd
            def _(e):
                for f in self.q[POOL]:
                    f()

            @block.sync
            def _(e):
                for f in self.q[SP]:
                    f()


def w3_tile_cols():
    tiles = []
    for c in range(4):
        tiles.append(np.arange(0 + c * 128, 0 + (c + 1) * 128))
        tiles.append(np.arange(1024 + c * 128, 1024 + (c + 1) * 128))
    for c in range(4):
        tiles.append(np.arange(2048 + c * 128, 2048 + (c + 1) * 128))
        tiles.append(np.arange(1536 + c * 128, 1536 + (c + 1) * 128))
    for c in range(4):
        tiles.append(np.arange(2976 + c * 128, 2976 + (c + 1) * 128))
    for c in range(2):
        tiles.append(np.arange(3488 + c * 128, 3488 + (c + 1) * 128))
    for j in range(8):
        for br in range(4):
            tiles.append(np.arange(3744 + br * 1024 + j * 128, 3744 + br * 1024 + (j + 1) * 128))
    return tiles


N_W3 = 54


def build_program(debug=False, sched=None):
    nc = bass.Bass("TRN2", target_bir_lowering=False)
    P = Prog(nc)

    def din(name, shape):
        return nc.dram_tensor(name, list(shape), F32, kind="ExternalInput").ap()

    def dout(name, shape):
        return nc.dram_tensor(name, list(shape), F32, kind="ExternalOutput").ap()

    def dscr(name, shape, dt):
        return nc.dram_tensor(name, list(shape), dt).ap()

    xT = din("xT", [128, 8, TT])
    cosT = din("cosT", [32, TT])
    sinT = din("sinT", [32, TT])
    maskT = din("maskT", [128, 128])
    memT = din("memT", [128, 8, 256])
    cmkT = din("cmkT", [NL, 128, 2, 256])
    cmv = din("cmv", [NL, 128, 2, 256])
    cckvT = din("cckvT", [NL, 128, PAST])
    ckrT = din("ckrT", [NL, 32, PAST])
    shin = din("shin", [NL, 128, 4])
    scin = din("scin", [NL, 128, 4, 3])
    W1 = din("W1", [NL, 5, 128, 1024])
    WAV = din("WAV", [NL, 128, 4096])
    W3 = din("W3", [NL, N_W3, 128, 1024])
    WBR = din("WBR", [NL, 8, 128, 1792])
    WOUT = din("WOUT", [NL, 8, 128, 1024])
    WUQ = din("WUQ", [NL, 128, 1536])
    WUQS = din("WUQS", [NL, 128, 1536])
    WUKV = din("WUKV", [NL, 128, 1024])
    MWK = din("MWK", [NL, 128, 2048])
    MWV = din("MWV", [NL, 128, 2048])
    WR = din("WR", [NL, 128, 512])
    WI = din("WI", [NL, 128, 512])
    WST = din("WST", [NL, 128, 512])
    PVin = din("PV", [NL, 128, NPV])
    GLN = din("GLN", [NL, 2, 512])
    BSin = din("BS", [NL, 1, 512])

    yT = dout("yT", [128, 8, TT])
    ckvo = dout("ckvo", [NL, 128, TT])
    kro = dout("kro", [NL, 32, TT])
    pmk = dout("pmk", [NL, 128, 2, 256])
    pmv = dout("pmv", [NL, 128, 2, 256])
    lruh = dout("lruh", [NL, 2, 128, 4])
    lruc = dout("lruc", [NL, 2, 128, 4, 3])
    sgv = dout("sgv", [NL, 16, 512])
    dbg = {}
    if debug:
        dbg["ocraw"] = dout("dbg_ocraw", [NL, 128, 4, TT])
        dbg["br"] = dout("dbg_br", [NL, 128, 14, TT])
        dbg["x1"] = dout("dbg_x1", [128, 8, TT])
        dbg["mg"] = dout("dbg_mg", [NL, 128, 8, TT])
        dbg["t"] = dout("dbg_t", [NL, 128, 8, TT])

    xres_d = dscr("xres_d", [128, 8, TT], F32)
    xbf_d = [dscr("xbf0_d", [128, 8, TT], BF16), dscr("xbf1_d", [128, 8, TT], BF16)]
    W1b = dscr("W1b", [NL, 5, 128, 1024], BF16)
    WAVb = dscr("WAVb", [NL, 128, 4096], BF16)
    W3b = dscr("W3b", [NL, N_W3, 128, 1024], BF16)
    WBRb = dscr("WBRb", [NL, 8, 128, 1792], BF16)
    WOUTb = dscr("WOUTb", [NL, 8, 128, 1024], BF16)
    WUQb = dscr("WUQb", [NL, 128, 1536], BF16)
    WUQSb = dscr("WUQSb", [NL, 128, 1536], BF16)
    WUKVb = dscr("WUKVb", [NL, 128, 1024], BF16)
    MWKb = dscr("MWKb", [NL, 128, 2048], BF16)
    MWVb = dscr("MWVb", [NL, 128, 2048], BF16)
    WRb = dscr("WRb", [NL, 128, 512], BF16)
    WIb = dscr("WIb", [NL, 128, 512], BF16)

    st = contextlib.ExitStack()

    uid = {"i": 0}

    def S(name, shape, dt=F32, stack=None):
        uid["i"] += 1
        t = (stack or st).enter_context(nc.sbuf_tensor("%s_%d" % (name, uid["i"]), list(shape), dt))
        return t

    def act(out, in_, func, reads, writes, bias=None, scale=None, accum=None):
        kw = {}
        if bias is not None:
            kw["bias"] = bias
        if scale is not None:
            kw["scale"] = scale
        if accum is not None:
            kw["accum_out"] = accum
        P.op(ACT, lambda e: e.activation(out=out, in_=in_, func=func, **kw), reads, writes)

    def tt(eng, out, in0, in1, op, reads, writes):
        P.op(eng, lambda e: e.tensor_tensor(out=out, in0=in0, in1=in1, op=op), reads, writes)

    def tsc(eng, out, in0, s1, s2, op0, op1, reads, writes):
        if s2 is None:
            P.op(eng, lambda e: e.tensor_scalar(out=out, in0=in0, scalar1=s1, scalar2=None, op0=op0), reads, writes)
        else:
            P.op(eng, lambda e: e.tensor_scalar(out=out, in0=in0, scalar1=s1, scalar2=s2, op0=op0, op1=op1), reads, writes)

    def stt(out, in0, scalar, in1, op0, op1, reads, writes, accum=None):
        if accum is None:
            P.op(DVE, lambda e: e.scalar_tensor_tensor(out=out, in0=in0, scalar=scalar, in1=in1, op0=op0, op1=op1), reads, writes)
        else:
            P.op(DVE, lambda e: e.scalar_tensor_tensor(out=out, in0=in0, scalar=scalar, in1=in1, op0=op0, op1=op1,
                                                       accum_out=accum), reads, writes)

    def cp(eng, out, in_, reads, writes):
        P.op(eng, lambda e: e.tensor_copy(out, in_), reads, writes)

    def memset(eng, ap, val, writes):
        P.op(eng, lambda e: e.memset(ap, val), (), writes)

    def mmg(mms, reads, writes):
        def fn(e, mms=mms):
            r = None
            for (o, l, rh, s0, s1) in mms:
                r = e.matmul(o, l, rh, start=s0, stop=s1)
            return r
        P.op(PE, fn, reads, writes)

    banks = [st.enter_context(nc.psum_tensor("bank%d" % i, [128, 512], F32)) for i in range(8)]
    dbank = [Dep("bank%d" % i) for i in range(8)]
    rr = {"i": 0}

    def nb():
        i = rr["i"] % 6
        rr["i"] += 1
        return banks[i], dbank[i]

    ones_bf = S("ones_bf", [128, 128], BF16)
    ones32 = S("ones32", [128, 128])
    maskS = S("maskS", [128, 128])
    d_const = Dep("const")
    ocraw = S("ocraw", [128, 4, TT], BF16)
    d_ocraw = Dep("ocraw")
    wav = S("wav", [128, 8, 512], BF16)
    wuq = S("wuq", [128, 2, 768], BF16)
    wuqs = S("wuqs", [128, 2, 768], BF16)
    wukv = S("wukv", [128, 1024], BF16)
    wr = S("wr", [128, 4, 128], BF16)
    wi = S("wi", [128, 4, 128], BF16)
    wst = S("wst", [128, 4, 128], BF16)
    gbc = S("gbc", [128, 512])
    bbc = S("bbc", [128, 512])
    bsrow = S("bsrow", [1, 512], BF16)
    pv = S("pv", [128, NPV])
    pvd = S("pvd", [128, 16])
    mkT = [S("mkT%d" % s, [128, 2, 256], BF16) for s in range(2)]
    mvx = [S("mvx%d" % s, [128, 2, 4, 128], BF16) for s in range(2)]
    hst = [S("hst%d" % s, [128, 4]) for s in range(2)]
    carry = [S("carry%d" % s, [128, 4, 3]) for s in range(2)]
    d_lw = Dep("layer_weights")
    d_mem = Dep("memkv")
    d_hst = [Dep("hst0"), Dep("hst1")]
    d_carry = [Dep("carry0"), Dep("carry1")]

    s_misc = P.dma_sem("s_misc")
    s_misc_sp = P.dma_sem("s_misc_sp")
    s_xst = P.dma_sem("s_xst")
    s_x = [P.dma_sem("s_x%d" % i) for i in range(5)]
    s_out = [P.dma_sem("s_out%d" % i) for i in range(4)]
    oi = {"i": 0}

    def out_dma(dst, src, reads):
        s = s_out[oi["i"] % 4]
        oi["i"] += 1
        P.dma(POOL, dst, src, s, reads=reads)

    s_conv = {}
    d_conv = {}
    conv_lists = {}
    for l in range(NL):
        g0 = [(W1b[l, i], W1[l, i]) for i in range(5)]
        g0 += [(WAVb[l][:, i * 1024:(i + 1) * 1024], WAV[l][:, i * 1024:(i + 1) * 1024]) for i in range(4)]
        g0 += [(WUQb[l], WUQ[l]), (WUQSb[l], WUQS[l]), (WUKVb[l], WUKV[l]), (WRb[l], WR[l]), (WIb[l], WI[l]),
               (MWKb[l], MWK[l]), (MWVb[l], MWV[l])]
        g1 = [(W3b[l, i], W3[l, i]) for i in range(22)]
        g2 = [(W3b[l, i], W3[l, i]) for i in range(22, N_W3)]
        g2 += [(WBRb[l, j], WBR[l, j]) for j in range(8)]
        g2 += [(WOUTb[l, j], WOUT[l, j]) for j in range(8)]
        for gi, g in enumerate((g0, g1, g2)):
            s_conv[(l, gi)] = P.dma_sem("s_conv%d_%d" % (l, gi))
            d_conv[(l, gi)] = Dep("conv%d_%d" % (l, gi))
            conv_lists[(l, gi)] = list(g)

    def emit_conv(l, gi, n=None):
        lst = conv_lists[(l, gi)]
        k = len(lst) if n is None else min(n, len(lst))
        for _ in range(k):
            dst, src = lst.pop(0)
            P.dma(POOL, dst, src, s_conv[(l, gi)])
        sem = s_conv[(l, gi)]
        d_conv[(l, gi)].w = (sem, P.dsem_val[sem], "dma")

    def conv_pending():
        return [(k, len(v)) for k, v in conv_lists.items() if v]

    def tile_src(key):
        kind, l_, i_ = key
        if kind == "W1":
            return W1b[l_, i_], 1024, d_conv[(l_, 0)]
        if kind == "W3":
            return W3b[l_, i_], 1024, d_conv[(l_, 1 if i_ < 22 else 2)]
        if kind == "WBR":
            return WBRb[l_, i_], 1792, d_conv[(l_, 2)]
        return WOUTb[l_, i_], 1024, d_conv[(l_, 2)]

    class Ring:
        def __init__(self, name, nslot, width, sched_):
            self.buf = [S("%s%d" % (name, i), [128, width], BF16) for i in range(nslot)]
            self.dep = [Dep() for _ in range(nslot)]
            self.sem = [P.dma_sem("s_%s%d" % (name, i)) for i in range(nslot)]
            self.n = nslot
            self.sched = sched_
            self.rec = []
            self.issued = 0
            self.consumed = 0
            self.low = 0
            self.done = set()

        def _issue(self, upto):
            if self.sched is None:
                return
            while self.issued < min(upto, len(self.sched)):
                i = self.issued
                sl = i % self.n
                ap, ncol, dc = tile_src(self.sched[i])
                if dc.w is None or conv_lists[[k for k, v in d_conv.items() if v is dc][0]]:
                    break
                P.dma(SP, self.buf[sl][:, 0:ncol], ap, self.sem[sl], reads=[dc], writes=[self.dep[sl]])
                self.issued += 1

        def get(self, key):
            i = self.consumed
            self.rec.append(key)
            if self.sched is not None:
                assert self.sched[i] == key, (i, self.sched[i], key)
                assert i < self.low + self.n
                self._issue(self.low + self.n)
                assert self.issued > i
            self.consumed += 1
            return self.buf[i % self.n], self.dep[i % self.n], i

        def rel(self, i):
            self.done.add(i)
            while self.low in self.done:
                self.done.discard(self.low)
                self.low += 1
            self._issue(self.low + self.n)

    ringA = Ring("rA", NSLOT_A, 1024, None if sched is None else sched[0])
    ringB = Ring("rB", NSLOT_B, 1792, None if sched is None else sched[1])

    memset(POOL, ones_bf[:], 1.0, [d_const])
    memset(POOL, ones32[:], 1.0, [d_const])
    P.dma(POOL, maskS[:], maskT, s_misc, writes=[d_const])
    emit_conv(0, 0)
    s_conv[(0, 9)] = P.dma_sem("s_xc")
    d_conv[(0, 9)] = Dep("xbf")
    conv_lists[(0, 9)] = [(xbf_d[0][:, :, n0_:n0_ + N_], xT[:, :, n0_:n0_ + N_]) for (n0_, N_, _sg) in BLOCKS]
    d_xbf_first = d_xbf_rest = d_conv[(0, 9)]
    conv_order = [(0, 9), (0, 1), (0, 2)]
    conv_order_l1 = [(1, 0), (1, 1), (1, 2)]

    def emit_conv_some(budget, order=None):
        for key in (order or conv_order):
            if budget <= 0:
                break
            n_ = min(budget, len(conv_lists[key]))
            if n_ > 0:
                emit_conv(key[0], key[1], n_)
                budget -= n_

    for l in range(NL):
        xsrc = xT if l == 0 else xres_d
        d_xsrc = Dep("xsrc")
        ydst = xres_d if l == 0 else yT

        with contextlib.ExitStack() as ph:
            memTb = S("memTb", [128, 8, 256], BF16, ph)
            d_memTb = Dep("memTb")
            P.dma(POOL, memTb[:], memT, s_misc, writes=[d_memTb])
            mwk = S("mwk", [128, 8, 256], BF16, ph)
            mwv = S("mwv", [128, 8, 256], BF16, ph)
            mk32 = S("mk32", [128, 2, 256], F32, ph)
            mv32 = S("mv32", [128, 2, 256], F32, ph)
            cm32 = S("cm32", [128, 2, 256], F32, ph)
            cv32 = S("cv32", [128, 2, 256], F32, ph)
            wst32 = S("wst32", [128, 4, 128], F32, ph)
            bsrow32 = S("bsrow32", [1, 512], F32, ph)
            d_p0 = Dep("p0")
            dcv = d_conv[(l, 0)]
            P.dma(SP, wav[:], WAVb[l].rearrange("p (k n) -> p k n", k=8), s_misc_sp, reads=[dcv], writes=[d_lw])
            P.dma(SP, wuq[:], WUQb[l].rearrange("p (k n) -> p k n", k=2), s_misc_sp, reads=[dcv], writes=[d_lw])
            P.dma(SP, wuqs[:], WUQSb[l].rearrange("p (k n) -> p k n", k=2), s_misc_sp, reads=[dcv], writes=[d_lw])
            P.dma(SP, wukv[:], WUKVb[l], s_misc_sp, reads=[dcv], writes=[d_lw])
            P.dma(SP, wr[:], WRb[l].rearrange("p (k n) -> p k n", k=4), s_misc_sp, reads=[dcv], writes=[d_lw])
            P.dma(SP, wi[:], WIb[l].rearrange("p (k n) -> p k n", k=4), s_misc_sp, reads=[dcv], writes=[d_lw])
            P.dma(SP, mwk[:], MWKb[l].rearrange("p (k n) -> p k n", k=8), s_misc_sp, reads=[dcv], writes=[d_p0])
            P.dma(SP, mwv[:], MWVb[l].rearrange("p (k n) -> p k n", k=8), s_misc_sp, reads=[dcv], writes=[d_p0])
            P.dma(SP, wst32[:], WST[l].rearrange("p (k n) -> p k n", k=4), s_misc_sp, writes=[d_lw])
            P.dma(SP, pv[:], PVin[l], s_misc_sp, writes=[d_lw])
            P.dma(SP, gbc[:], GLN[l][0:1, :].partition_broadcast(128), s_misc_sp, writes=[d_lw])
            P.dma(SP, bbc[:], GLN[l][1:2, :].partition_broadcast(128), s_misc_sp, writes=[d_lw])
            P.dma(SP, bsrow32[:], BSin[l], s_misc_sp, writes=[d_lw])
            P.dma(SP, cm32[:], cmkT[l], s_misc_sp, writes=[d_p0])
            P.dma(SP, cv32[:], cmv[l], s_misc_sp, writes=[d_p0])
            P.dma(SP, hst[1][:], shin[l], s_misc_sp, writes=[d_hst[1]])
            P.dma(SP, carry[1][:], scin[l], s_misc_sp, writes=[d_carry[1]])
            memset(POOL, hst[0][:], 0.0, [d_hst[0]])
            memset(POOL, carry[0][:], 0.0, [d_carry[0]])
            tt(DVE, wst[:], wst32[:], maskS[:].unsqueeze(1).to_broadcast([128, 4, 128]), ALU.mult,
               [d_lw, d_const], [d_lw])
            cp(DVE, bsrow[:], bsrow32[:], [d_lw], [d_lw])
            tsc(DVE, pvd[:, 0:4], pv[:, 20:24], 0.5, None, ALU.mult, None, [d_lw], [d_lw])
            tsc(DVE, pvd[:, 4:8], pv[:, 24:28], 0.5, None, ALU.mult, None, [d_lw], [d_lw])
            act(pvd[:, 12:16], pv[:, 28:32], AF.Exp, [d_lw], [d_lw], scale=-1.0)
            act(pvd[:, 12:16], pvd[:, 12:16], AF.Ln, [d_lw], [d_lw], bias=1.0)
            tsc(DVE, pvd[:, 8:12], pvd[:, 12:16], -4.0, None, ALU.mult, None, [d_lw], [d_lw])
            tsc(DVE, pvd[:, 12:16], pvd[:, 12:16], -8.0, None, ALU.mult, None, [d_lw], [d_lw])
            memset(POOL, mvx[0][:], 1.0, [d_mem])
            memset(POOL, mvx[1][:], 1.0, [d_mem])
            for c in range(2):
                bk, db = nb()
                mmg([(bk[:, 0:256], mwk[:, k, c * 128:(c + 1) * 128], memTb[:, k, :], k == 0, k == 7) for k in range(8)],
                    [d_p0, d_memTb], [db])
                act(mk32[:, c, :], bk[:, 0:256], AF.Copy, [db], [d_p0])
            cp(DVE, mkT[0][:], mk32[:], [d_p0], [d_mem])
            out_dma(pmk[l], mk32[:], [d_p0])
            for mt in range(2):
                bk, db = nb()
                mmg([(bk[:, 0:256], memTb[:, k, mt * 128:(mt + 1) * 128], mwv[:, k, :], k == 0, k == 7) for k in range(8)],
                    [d_p0, d_memTb], [db])
                act(mv32[:, mt, :], bk[:, 0:256], AF.Copy, [db], [d_p0])
            out_dma(pmv[l], mv32[:], [d_p0])
            for mt in range(2):
                cp(DVE, mvx[0][:, mt, :, 0:64], mv32[:, mt, :].rearrange("p (h d) -> p h d", h=4), [d_p0], [d_mem])
                cp(DVE, mvx[1][:, mt, :, 0:64], cv32[:, mt, :].rearrange("p (h d) -> p h d", h=4), [d_p0], [d_mem])
            cp(DVE, mkT[1][:], cm32[:], [d_p0], [d_mem])
            P.barrier()

        with contextlib.ExitStack() as ph:
            cosS = S("cosS", [128, TT], F32, ph)
            sinS = S("sinS", [128, TT], F32, ph)
            d_tab = Dep("tab")
            P.dma(POOL, cosS[64:96, :], cosT, s_misc, writes=[d_tab])
            P.dma(POOL, sinS[64:96, :], sinT, s_misc, writes=[d_tab])
            cqn = S("cqn", [128, 2, TT], BF16, ph)
            ckvall = S("ckvall", [128, KC], BF16, ph)
            Kb = S("Kb", [96, KC], BF16, ph)
            tmp = [S("t1_%d" % i, [128, 512], F32, ph) for i in range(4)]
            p1s = contextlib.ExitStack()
            xb = [S("xb1_%d" % i, [128, 8, 512], BF16, p1s) for i in range(2)]
            sq = [S("sq%d" % i, [128, 512], BF16, p1s) for i in range(3)]
            xst = S("xst", [128, 8, 512], F32, p1s) if l == 0 else None
            d_xst = Dep()
            d_cqn, d_ckvall, d_Kn, d_Kr, d_Q, d_V = Dep(), Dep(), Dep(), Dep(), Dep(), Dep()
            d_xb = [Dep(), Dep()]
            d_PT = [Dep() for _ in range(4)]
            d_tmp = [Dep() for _ in range(4)]
            d_sq = [Dep(), Dep(), Dep()]
            d_rsum = [Dep(), Dep()]
            ti = {"i": 0}

            def T1():
                i = ti["i"] % 4
                ti["i"] += 1
                return tmp[i], d_tmp[i]

            P.dma(POOL, ckvall[:, TP:TP + PAST], cckvT[l], s_misc, writes=[d_ckvall])
            P.dma(POOL, Kb[64:96, TP:TP + PAST], ckrT[l], s_misc, writes=[d_Kr])
            def p1_load(bj):
                n0_, N_, _ = BLOCKS[bj]
                if l == 0:
                    P.dma(ACT, xst[:, :, 0:N_], xT[:, :, n0_:n0_ + N_], s_xst, writes=[d_xst])
                else:
                    P.dma(ACT, xb[bj % 2][:, :, 0:N_], xbf_d[l][:, :, n0_:n0_ + N_], s_x[bj % 2], writes=[d_xb[bj % 2]])

            def p1_cast(bj):
                _, N_, _ = BLOCKS[bj]
                act(xb[bj % 2][:, 0:4, 0:N_], xst[:, 0:4, 0:N_], AF.Copy, [d_xst], [d_xb[bj % 2]])
                cp(DVE, xb[bj % 2][:, 4:8, 0:N_], xst[:, 4:8, 0:N_], [d_xst], [d_xb[bj % 2]])

            p1_load(0)
            for bi, (n0, N, seg) in enumerate(BLOCKS):
                cc0 = n0 if seg == 0 else TP + PAST
                x_b, dx = xb[bi % 2], d_xb[bi % 2]
                if l == 0:
                    p1_cast(bi)
                if bi + 1 < len(BLOCKS):
                    p1_load(bi + 1)
                pz = []
                for i in range(5):
                    wt_, dw, ri = ringA.get(("W1", l, i))
                    bk, db = nb()
                    mmg([(bk[:, 0:N], wt_[:, k * 128:(k + 1) * 128], x_b[:, k, 0:N], k == 0, k == 7) for k in range(8)],
                        [dw, dx], [db])
                    ringA.rel(ri)
                    pz.append((bk, db))
                k1, dk1 = T1()
                k2, dk2 = T1()
                tt(DVE, k1[64:96, 0:N], pz[3][0][64:96, 0:N], cosS[64:96, n0:n0 + N], ALU.mult, [pz[3][1], d_tab], [dk1])
                tt(DVE, k2[64:96, 0:N], pz[4][0][64:96, 0:N], sinS[64:96, n0:n0 + N], ALU.mult, [pz[4][1], d_tab], [dk2])
                tt(DVE, k1[64:96, 0:N], k1[64:96, 0:N], k2[64:96, 0:N], ALU.add, [dk1, dk2], [dk1])
                bs_, dbs = banks[6], dbank[6]
                bs2, dbs2 = banks[7], dbank[7]
                for c in range(3):
                    act(sq[c][:, 0:N], pz[c][0][:, 0:N], AF.Square, [pz[c][1]], [d_sq[c]])
                mmg([(bs_[:, 0:N], ones_bf[:], sq[c][:, 0:N], c == 0, c == 1) for c in range(2)],
                    [d_const, d_sq[0], d_sq[1]], [dbs])
                mmg([(bs2[:, 0:N], ones_bf[:], sq[2][:, 0:N], True, True)], [d_const, d_sq[2]], [dbs2])
                r_, dr = T1()
                r2, dr2 = T1()
                tsc(DVE, r_[:, 0:N], bs_[:, 0:N], 1.0 / 256.0, EPS, ALU.mult, ALU.add, [dbs], [dr])
                tsc(DVE, r2[:, 0:N], bs2[:, 0:N], 1.0 / 128.0, EPS, ALU.mult, ALU.add, [dbs2], [dr2])
                act(r_[:, 0:N], r_[:, 0:N], AF.Ln, [dr], [dr])
                act(r2[:, 0:N], r2[:, 0:N], AF.Ln, [dr2], [dr2])
                act(r_[:, 0:N], r_[:, 0:N], AF.Exp, [dr], [dr], scale=-0.5)
                act(r2[:, 0:N], r2[:, 0:N], AF.Exp, [dr2], [dr2], scale=-0.5)
                for c in range(2):
                    stt(cqn[:, c, n0:n0 + N], pz[c][0][:, 0:N], pv[:, 32 + c:33 + c], r_[:, 0:N], ALU.mult, ALU.mult,
                        [pz[c][1], d_lw, dr], [d_cqn])
                stt(r2[:, 0:N], pz[2][0][:, 0:N], pv[:, 34:35], r2[:, 0:N], ALU.mult, ALU.mult, [pz[2][1], d_lw, dr2], [dr2])
                out_dma(ckvo[l][:, n0:n0 + N], r2[:, 0:N], [dr2])
                act(ckvall[:, cc0:cc0 + N], r2[:, 0:N], AF.Copy, [dr2], [d_ckvall])
                out_dma(kro[l][:, n0:n0 + N], k1[64:96, 0:N], [dk1])
                act(Kb[64:96, cc0:cc0 + N], k1[64:96, 0:N], AF.Copy, [dk1], [d_Kr])

            P.barrier()
            p1s.close()
            Qb = S("Qb", [96, TT], BF16, ph)
            Vb = S("Vb", [128, 41, 128], BF16, ph)
            PT = [S("PT%d" % i, [128, 512], BF16, ph) for i in range(4)]
            rsum = [S("rsum%d" % i, [128, 512], F32, ph) for i in range(2)]
            memset(POOL, Vb[:], 1.0, [d_V])
            Kbs = [Kb, S("Kb1", [96, KC], BF16, ph)]
            Qbs = [Qb, S("Qb1", [96, TT], BF16, ph)]
            d_Kns = [d_Kn, Dep()]
            d_Qs = [d_Q, Dep()]
            cp(DVE, Kbs[1][64:96, :], Kb[64:96, :], [d_Kr], [d_Kr])
            obank = [(banks[6], dbank[6]), (banks[7], dbank[7])]

            def kq_pieces(h):
                Kh, dKh, Qh, dQh = Kbs[h % 2], d_Kns[h % 2], Qbs[h % 2], d_Qs[h % 2]
                out = []
                c0 = 0
                while c0 < KC:
                    n = min(512, KC - c0)

                    def kp(c0=c0, n=n):
                        bk, db = nb()
                        mmg([(bk[0:64, 0:n], wukv[:, h * 128:h * 128 + 64], ckvall[:, c0:c0 + n], True, True)],
                            [d_lw, d_ckvall], [db])
                        cp(DVE, Kh[0:64, c0:c0 + n], bk[0:64, 0:n], [db], [dKh])
                    out.append(kp)
                    c0 += n
                for (n0, N, seg) in BLOCKS:
                    def qp(n0=n0, N=N):
                        qa, dqa = nb()
                        qs, dqs = nb()
                        mmg([(qa[0:96, 0:N], wuq[:, k, h * 96:(h + 1) * 96], cqn[:, k, n0:n0 + N], k == 0, k == 1) for k in range(2)],
                            [d_lw, d_cqn], [dqa])
                        mmg([(qs[0:96, 0:N], wuqs[:, k, h * 96:(h + 1) * 96], cqn[:, k, n0:n0 + N], k == 0, k == 1) for k in range(2)],
                            [d_lw, d_cqn], [dqs])
                        cp(DVE, Qh[0:64, n0:n0 + N], qa[0:64, 0:N], [dqa], [dQh])
                        k1, dk1 = T1()
                        k2, dk2 = T1()
                        tt(DVE, k1[64:96, 0:N], qa[64:96, 0:N], cosS[64:96, n0:n0 + N], ALU.mult, [dqa, d_tab], [dk1])
                        tt(DVE, k2[64:96, 0:N], qs[64:96, 0:N], sinS[64:96, n0:n0 + N], ALU.mult, [dqs, d_tab], [dk2])
                        tt(DVE, Qh[64:96, n0:n0 + N], k1[64:96, 0:N], k2[64:96, 0:N], ALU.add, [dk1, dk2], [dQh])
                    out.append(qp)
                return out

            for f_ in kq_pieces(0):
                f_()
            for h in range(8):
                Kh, dKh, Qh, dQh = Kbs[h % 2], d_Kns[h % 2], Qbs[h % 2], d_Qs[h % 2]
                pieces = kq_pieces(h + 1) if h + 1 < 8 else []
                if l == 0:
                    emit_conv_some(11 if h < 7 else 1000)
                for b8 in range(5):
                    bk, db = nb()
                    mmg([(bk[:, i * 64:(i + 1) * 64], ckvall[:, (b8 * 8 + i) * 128:(b8 * 8 + i + 1) * 128],
                          wukv[:, h * 128 + 64:h * 128 + 128], True, True) for i in range(8)],
                        [d_lw, d_ckvall], [db])
                    cp(DVE, Vb[:, b8 * 8:b8 * 8 + 8, 0:64], bk[:, :].rearrange("p (t d) -> p t d", t=8), [db], [d_V])
                bk, db = nb()
                mmg([(bk[0:16, 0:64], ckvall[:, TP + PAST:KC], wukv[:, h * 128 + 64:h * 128 + 128], True, True)],
                    [d_lw, d_ckvall], [db])
                cp(DVE, Vb[0:16, 40, 0:64], bk[0:16, 0:64], [db], [d_V])
                items = []
                for qb in range(8):
                    nk = 4 * qb + 4
                    for kt in range(nk):
                        j = kt - 4 * qb
                        cs = 128 * j if j > 0 else 0
                        items.append(dict(q0=qb * 512, nq=512, cs=cs, kcol=kt * 128, kn=128, vt=kt, diag=(j >= 0),
                                          first=(kt == 0), last=(kt == nk - 1), ob=qb % 2))
                for i in range(8):
                    items.append(dict(q0=TP, nq=TS, cs=0, kcol=TP + 128 * i, kn=128, vt=32 + i, diag=False,
                                      first=(i == 0), last=False, ob=0))
                items.append(dict(q0=TP, nq=TS, cs=0, kcol=TP + PAST, kn=TS, vt=40, diag=False, first=False, last=True, ob=0))
                LOOK = 3
                inflight = []
                norms = []
                pti = 0
                for ii in range(len(items) + LOOK + 9):
                    while norms and norms[0][0] <= ii:
                        norms.pop(0)[1]()
                    if pieces and ii >= 12 and ii % 5 == 0:
                        pieces.pop(0)()
                    if ii < len(items):
                        it = items[ii]
                        sb_, dsb = nb()
                        pt_, dpt = PT[pti % 4], d_PT[pti % 4]
                        pti += 1
                        kn, cs, nq, q0 = it["kn"], it["cs"], it["nq"], it["q0"]
                        mmg([(sb_[0:kn, cs:nq], Kh[0:96, it["kcol"]:it["kcol"] + kn], Qh[0:96, q0 + cs:q0 + nq], True, True)],
                            [dKh, d_Kr, dQh], [dsb])
                        act(pt_[0:kn, cs:nq], sb_[0:kn, cs:nq], AF.Exp, [dsb], [dpt], scale=SM_SCALE)
                        if it["diag"]:
                            memset(POOL, pt_[64:128, cs:cs + 64], 0.0, [dpt])
                        inflight.append((it, pt_, dpt))
                    if ii >= LOOK and inflight:
                        it, pt_, dpt = inflight.pop(0)
                        ob_, dob = obank[it["ob"]]
                        kn, cs, nq, q0 = it["kn"], it["cs"], it["nq"], it["q0"]
                        mmg([(ob_[0:128, cs:nq], Vb[0:kn, it["vt"], 0:128], pt_[0:kn, cs:nq], it["first"], it["last"])],
                            [d_V, dpt], [dob])
                        if it["last"]:
                            rs_, drs = rsum[it["ob"]], d_rsum[it["ob"]]
                            P.op(DVE, (lambda e, o=rs_[64:128, 0:nq], i_=ob_[64:128, 0:nq]: e.reciprocal(o, i_)), [dob], [drs])
                            p0 = (h % 2) * 64
                            tt(DVE, ocraw[p0:p0 + 64, h // 2, q0:q0 + nq], ob_[0:64, 0:nq], rs_[64:128, 0:nq], ALU.mult,
                               [dob, drs], [d_ocraw])
                while pieces:
                    pieces.pop(0)()
            if debug:
                t32 = tmp[0]
                for c in range(4):
                    for (n0, N, seg) in BLOCKS:
                        cp(DVE, t32[:, 0:N], ocraw[:, c, n0:n0 + N], [d_ocraw], [d_tmp[0]])
                        out_dma(dbg["ocraw"][l][:, c, n0:n0 + N], t32[:, 0:N], [d_tmp[0]])
            P.barrier()

        with contextlib.ExitStack() as ph:
            xres = [S("xres%d" % i, [128, 8, 512], F32, ph) for i in range(1)]
            xb3 = [S("xb3_%d" % i, [128, 8, 512], BF16, ph) for i in range(2)]
            vb = S("vb", [128, 4, 512], BF16, ph)
            uag = S("uag", [128, 4, 512], BF16, ph)
            oas = [S("oa%d" % i, [128, 4, 512], BF16, ph) for i in range(2)]
            d_oas = [Dep(), Dep()]
            obs = [S("ob%d" % i, [128, 4, 512], BF16, ph) for i in range(2)]
            ocbs = [S("ocb%d" % i, [128, 4, 512], BF16, ph) for i in range(2)]
            mq = S("mq", [128, 2, 512], BF16, ph)
            oms = [S("om%d" % i, [128, 2, 512], BF16, ph) for i in range(2)]
            mg = S("mg", [128, 8, 512], BF16, ph)
            bxs = [S("bx%d" % i, [128, 515], F32, ph) for i in range(2)]
            xcs = [S("xc%d" % i, [128, 512], F32, ph) for i in range(2)]
            xcbs = [S("xcb%d" % i, [128, 512], BF16, ph) for i in range(2)]
            sgbs = [S("sgb%d" % i, [128, 512], F32, ph) for i in range(2)]
            d_sgbs = [Dep(), Dep()]
            macc = [S("macc%d" % i, [128, 512], F32, ph) for i in range(2)]
            tmp = [S("t3_%d" % i, [128, 512], F32, ph) for i in range(5)]
            tb16 = [S("tb16_%d" % i, [128, 512], BF16, ph) for i in range(2)]
            gv = S("gv", [128, 4, 512], F32, ph)
            d_gv = Dep()
            tmn = [S("tmn%d" % i, [128, 512], F32, ph) for i in range(2)]
            d_tmn = [Dep(), Dep()]
            PT3 = [S("PT3_%d" % i, [128, 512], BF16, ph) for i in range(4)]
            stt_ = S("stat", [128, 20], F32, ph)
            d_xres = [Dep()]
            d_xb3 = [Dep(), Dep()]
            d_vb, d_uag, d_mq, d_mg = [Dep() for _ in range(4)]
            d_obs, d_ocbs, d_oms = [Dep(), Dep()], [Dep(), Dep()], [Dep(), Dep()]
            d_stat = Dep()
            d_bxs, d_xcs, d_xcbs = [Dep(), Dep()], [Dep(), Dep()], [Dep(), Dep()]
            d_macc = [Dep(), Dep()]
            d_tmp = [Dep() for _ in range(5)]
            d_tb16 = [Dep() for _ in range(2)]
            d_PT3 = [Dep() for _ in range(4)]
            ti = {"i": 0}

            def T3():
                i = ti["i"] % 5
                ti["i"] += 1
                return tmp[i], d_tmp[i]

            def silu2(dst, ddst, pz, dpz, N):
                act(dst[:, 0:N], pz[:, 0:N], AF.Tanh, [dpz], [ddst], scale=0.5)
                stt(dst[:, 0:N], dst[:, 0:N], 1.0, pz[:, 0:N], ALU.add, ALU.mult, [ddst, dpz], [ddst])

            def gelu2(dst, ddst, pz, dpz, m, n):
                xs, dxs = T3()
                act(xs[0:m, 0:n], pz[0:m, 0:n], AF.Copy, [dpz], [dxs])
                act(dst[0:m, 0:n], pz[0:m, 0:n], AF.Square, [dpz], [ddst])
                tsc(DVE, dst[0:m, 0:n], dst[0:m, 0:n], 0.044715, 1.0, ALU.mult, ALU.add, [ddst], [ddst])
                tt(DVE, dst[0:m, 0:n], dst[0:m, 0:n], xs[0:m, 0:n], ALU.mult, [ddst, dxs], [ddst])
                act(dst[0:m, 0:n], dst[0:m, 0:n], AF.Tanh, [ddst], [ddst], scale=GELU_C)
                stt(dst[0:m, 0:n], dst[0:m, 0:n], 1.0, xs[0:m, 0:n], ALU.add, ALU.mult, [ddst, dxs], [ddst])

            def p3_load(bj):
                n0_, N_, _ = BLOCKS[bj]
                rd = [] if l == 1 else [d_xbf_first if bj == 0 else d_xbf_rest]
                P.dma(ACT, xb3[bj % 2][:, :, 0:N_], xbf_d[l][:, :, n0_:n0_ + N_], s_x[2 + bj % 2], reads=rd, writes=[d_xb3[bj % 2]])

            def p3_load_res(bj):
                n0_, N_, _ = BLOCKS[bj]
                P.dma(ACT, xres[0][:, :, 0:N_], xsrc[:, :, n0_:n0_ + N_], s_x[4], reads=[d_xsrc], writes=[d_xres[0]])

            def zmm_b(key, bj):
                _, N_, _ = BLOCKS[bj]
                xbj, dxbj = xb3[bj % 2], d_xb3[bj % 2]
                wt_, dw, ri = ringA.get(key)
                bk, db = nb()
                mmg([(bk[:, 0:N_], wt_[:, k * 128:(k + 1) * 128], xbj[:, k, 0:N_], k == 0, k == 7) for k in range(8)],
                    [dw, dxbj], [db])
                ringA.rel(ri)
                return bk, db

            def au_ag(bj):
                _, N_, _ = BLOCKS[bj]
                for c in range(4):
                    pu, dpu = zmm_b(("W3", l, 2 * c), bj)
                    u2, du2 = T3()
                    act(u2[:, 0:N_], pu[:, 0:N_], AF.Gelu_apprx_tanh, [dpu], [du2])
                    pg, dpg = zmm_b(("W3", l, 2 * c + 1), bj)
                    s2, ds2 = T3()
                    silu2(s2, ds2, pg, dpg, N_)
                    stt(uag[:, c, 0:N_], s2[:, 0:N_], 0.5, u2[:, 0:N_], ALU.mult, ALU.mult, [ds2, du2], [d_uag])

            def v_pass1(bj):
                n0_, N_, _ = BLOCKS[bj]
                xbj, dxbj = xb3[bj % 2], d_xb3[bj % 2]
                memset(POOL, stt_[:, :], 1.0, [d_stat])
                for t_ in range((N_ + 127) // 128):
                    m = min(128, N_ - t_ * 128)
                    bk, db = nb()
                    mmg([(bk[0:m, :], xbj[:, k, t_ * 128:t_ * 128 + m], wav[:, k, :], k == 0, k == 7) for k in range(8)],
                        [dxbj, d_lw], [db])
                    act(gv[0:m, t_, :], bk[0:m, :], AF.Gelu_apprx_tanh, [db], [d_gv, d_stat], accum=stt_[0:m, t_:t_ + 1])
                    act(PT3[3][0:m, :], gv[0:m, t_, :], AF.Square, [d_gv], [d_PT3[3], d_stat], accum=stt_[0:m, 4 + t_:5 + t_])

            def lru_z(bj, c):
                pg, dpg = zmm_b(("W3", l, 8 + 2 * c), bj)
                px, dpx = zmm_b(("W3", l, 8 + 2 * c + 1), bj)
                return pg, dpg, px, dpx

            def cg_tile(bj, c):
                n0_, N_, _ = BLOCKS[bj]
                pc, dpc = zmm_b(("W3", l, 16 + c), bj)
                sg, dsg = T3()
                act(sg[:, 0:N_], pc[:, 0:N_], AF.Silu, [dpc], [dsg])
                tt(DVE, ocbs[bj % 2][:, c, 0:N_], ocraw[:, c, n0_:n0_ + N_], sg[:, 0:N_], ALU.mult, [d_ocraw, dsg], [d_ocbs[bj % 2]])

            def lru_front(bj, c, zc):
                _, N_, seg_ = BLOCKS[bj]
                pg, dpg, px, dpx = zc
                bx, d_bx, xc, d_xc, xcb, d_xcb = bxs[c % 2], d_bxs[c % 2], xcs[c % 2], d_xcs[c % 2], xcbs[c % 2], d_xcbs[c % 2]
                sg, dsg = sgbs[c % 2], d_sgbs[c % 2]
                act(sg[:, 0:N_], pg[:, 0:N_], AF.Silu, [dpg], [dsg])
                cp(POOL, bx[:, 0:3], carry[seg_][:, c, :], [d_carry[seg_]], [d_bx])
                act(bx[:, 3:3 + N_], px[:, 0:N_], AF.Copy, [dpx], [d_bx])
                cp(POOL, carry[seg_][:, c, :], bx[:, N_:N_ + 3], [d_bx], [d_carry[seg_]])
                tsc(DVE, xc[:, 0:N_], bx[:, 0:N_], pv[:, 4 * c:4 * c + 1], pv[:, 16 + c:17 + c], ALU.mult, ALU.add,
                    [d_bx, d_lw], [d_xc])
                for k in range(1, 3):
                    stt(xc[:, 0:N_], bx[:, k:k + N_], pv[:, 4 * c + k:4 * c + k + 1], xc[:, 0:N_], ALU.mult, ALU.add,
                        [d_bx, d_lw, d_xc], [d_xc])
                stt(xcb[:, 0:N_], bx[:, 3:3 + N_], pv[:, 4 * c + 3:4 * c + 4], xc[:, 0:N_], ALU.mult, ALU.add,
                    [d_bx, d_lw, d_xc], [d_xcb])
                stt(xc[:, 0:N_], bx[:, 3:3 + N_], pv[:, 4 * c + 3:4 * c + 4], xc[:, 0:N_], ALU.mult, ALU.add,
                    [d_bx, d_lw, d_xc], [d_xc])
                return sg, dsg

            def lru_step(bj, c, stl):
                n0_, N_, seg_ = BLOCKS[bj]
                frs = stl["frs"]
                if c == 0:
                    frs[0] = lru_front(bj, 0, lru_z(bj, 0))
                sg, dsg = frs[c]
                xc, d_xc, xcb, d_xcb = xcs[c % 2], d_xcs[c % 2], xcbs[c % 2], d_xcbs[c % 2]
                pr, dpr = nb()
                pi_, dpi = nb()
                mmg([(pr[:, 0:N_], wr[:, c, :], xcb[:, 0:N_], True, True)], [d_lw, d_xcb], [dpr])
                mmg([(pi_[:, 0:N_], wi[:, c, :], xcb[:, 0:N_], True, True)], [d_lw, d_xcb], [dpi])
                cg_tile(bj, c)
                ta, dta = T3()
                tc_, dtc = T3()
                td, dtd = T3()
                act(ta[:, 0:N_], pr[:, 0:N_], AF.Tanh, [dpr, d_lw], [dta], scale=0.5, bias=pvd[:, c:c + 1])
                act(tc_[:, 0:N_], pi_[:, 0:N_], AF.Tanh, [dpi, d_lw], [dtc], scale=0.5, bias=pvd[:, 4 + c:5 + c])
                act(td[:, 0:N_], ta[:, 0:N_], AF.Exp, [dta, d_lw], [dtd], scale=pvd[:, 12 + c:13 + c], bias=pvd[:, 12 + c:13 + c])
                act(ta[:, 0:N_], ta[:, 0:N_], AF.Exp, [dta, d_lw], [dta], scale=pvd[:, 8 + c:9 + c], bias=pvd[:, 8 + c:9 + c])
                act(td[:, 0:N_], td[:, 0:N_], AF.Ln, [dtd], [dtd], scale=-1.0, bias=1.0)
                act(td[:, 0:N_], td[:, 0:N_], AF.Exp, [dtd], [dtd], scale=0.5)
                stt(tc_[:, 0:N_], tc_[:, 0:N_], 1.0, xc[:, 0:N_], ALU.add, ALU.mult, [dtc, d_xc], [dtc])
                stt(tc_[:, 0:N_], tc_[:, 0:N_], 0.5, td[:, 0:N_], ALU.mult, ALU.mult, [dtc, dtd], [dtc])
                P.op(DVE, (lambda e, o=td[:, 0:N_], a=ta[:, 0:N_], b=tc_[:, 0:N_], i0=hst[seg_][:, c:c + 1]:
                           e.tensor_tensor_scan(out=o, data0=a, data1=b, initial=i0, op0=ALU.mult, op1=ALU.add)),
                     [dta, dtc, d_hst[seg_]], [dtd])
                cp(DVE, hst[seg_][:, c:c + 1], td[:, N_ - 1:N_], [dtd], [d_hst[seg_]])
                tt(DVE, obs[bj % 2][:, c, 0:N_], td[:, 0:N_], sg[:, 0:N_], ALU.mult, [dtd, dsg], [d_obs[bj % 2]])
                if c + 1 < 4:
                    frs[c + 1] = lru_front(bj, c + 1, lru_z(bj, c + 1))
                if c == 3 and (bj == 7 or seg_ == 1):
                    out_dma(lruh[l, seg_], hst[seg_][:], [d_hst[seg_]])
                    out_dma(lruc[l, seg_], carry[seg_][:], [d_carry[seg_]])

            def mq_tiles(bj):
                _, N_, _ = BLOCKS[bj]
                for c in range(2):
                    pm, dpm = zmm_b(("W3", l, 20 + c), bj)
                    act(mq[:, c, 0:N_], pm[:, 0:N_], AF.Copy, [dpm], [d_mq])

            def mem_pair(bj, hp):
                _, N_, seg_ = BLOCKS[bj]
                c = hp
                om_, dom_ = oms[bj % 2], d_oms[bj % 2]
                sbs = []
                for hh in range(2):
                    p0 = hh * 64
                    for mt in range(2):
                        sb_, dsb = nb()
                        mmg([(sb_[:, 0:N_], mkT[seg_][p0:p0 + 64, c, mt * 128:(mt + 1) * 128], mq[p0:p0 + 64, c, 0:N_], True, True)],
                            [d_mem, d_mq], [dsb])
                        sbs.append((sb_, dsb))
                for hh in range(2):
                    h = 2 * hp + hh
                    po, dpo = banks[6 + hh], dbank[6 + hh]
                    for mt in range(2):
                        sb_, dsb = sbs[2 * hh + mt]
                        pt_, dpt = PT3[2 * hh + mt], d_PT3[2 * hh + mt]
                        act(pt_[:, 0:N_], sb_[:, 0:N_], AF.Exp, [dsb], [dpt], scale=0.125)
                        mmg([(po[0:128, 0:N_], mvx[seg_][:, mt, h, :], pt_[:, 0:N_], mt == 0, mt == 1)], [d_mem, dpt], [dpo])
                rss = []
                for hh in range(2):
                    po, dpo = banks[6 + hh], dbank[6 + hh]
                    rs3, d_rs3 = T3()
                    P.op(DVE, (lambda e, o=rs3[64:128, 0:N_], i_=po[64:128, 0:N_]: e.reciprocal(o, i_)), [dpo], [d_rs3])
                    rss.append((rs3, d_rs3))
                for hh in range(2):
                    po, dpo = banks[6 + hh], dbank[6 + hh]
                    rs3, d_rs3 = rss[hh]
                    tt(DVE, om_[hh * 64:hh * 64 + 64, c, 0:N_], po[0:64, 0:N_], rs3[64:128, 0:N_], ALU.mult, [dpo, d_rs3], [dom_])

            def spatial_part(bj):
                n0, N, seg = BLOCKS[bj]
                ntile = (N + 127) // 128
                oa, d_oa = oas[bj % 2], d_oas[bj % 2]
                for g in range(4):
                    bk, db = nb()
                    mms = []
                    for t_ in range(ntile):
                        m = min(128, N - t_ * 128)
                        mms.append((bk[:, t_ * 128:t_ * 128 + m], vb[0:m, t_, g * 128:(g + 1) * 128], wst[0:m, g, 0:m], True, False))
                        mms.append((bk[:, t_ * 128:t_ * 128 + m], ones_bf[0:1, 0:128], bsrow[0:1, g * 128:g * 128 + m], False, True))
                    mmg(mms, [d_vb, d_lw, d_const], [db])
                    tt(DVE, oa[:, g, 0:N], bk[:, 0:N], uag[:, g, 0:N], ALU.mult, [db, d_uag], [d_oa])

            def v_rest_spatial(bj, part=None):
                n0, N, seg = BLOCKS[bj]
                ntile = (N + 127) // 128
                oa, d_oa = oas[bj % 2], d_oas[bj % 2]
                if part == 1:
                    return spatial_part(bj)
                mt_ = [min(128, N - t_ * 128) for t_ in range(ntile)]
                mm_ = mt_[0]
                nt = ntile
                tsc(DVE, stt_[0:mm_, 8:8 + nt], stt_[0:mm_, 0:nt], 1.0 / 512.0, None, ALU.mult, None, [d_stat], [d_stat])
                tt(DVE, stt_[0:mm_, 12:12 + nt], stt_[0:mm_, 8:8 + nt], stt_[0:mm_, 8:8 + nt], ALU.mult, [d_stat], [d_stat])
                stt(stt_[0:mm_, 12:12 + nt], stt_[0:mm_, 4:4 + nt], 1.0 / 512.0, stt_[0:mm_, 12:12 + nt], ALU.mult, ALU.subtract,
                    [d_stat], [d_stat])
                tsc(DVE, stt_[0:mm_, 12:12 + nt], stt_[0:mm_, 12:12 + nt], EPS, None, ALU.add, None, [d_stat], [d_stat])
                act(stt_[0:mm_, 12:12 + nt], stt_[0:mm_, 12:12 + nt], AF.Ln, [d_stat], [d_stat])
                act(stt_[0:mm_, 12:12 + nt], stt_[0:mm_, 12:12 + nt], AF.Exp, [d_stat], [d_stat], scale=-0.5)
                stt(stt_[0:mm_, 16:16 + nt], stt_[0:mm_, 8:8 + nt], -1.0, stt_[0:mm_, 12:12 + nt], ALU.mult, ALU.mult, [d_stat], [d_stat])
                d_gvt = [Dep() for _ in range(ntile)]
                for t_ in range(ntile):
                    m = mt_[t_]
                    act(gv[0:m, t_, :], gv[0:m, t_, :], AF.Identity, [d_gv, d_stat], [d_gvt[t_]],
                        scale=stt_[0:m, 12 + t_:13 + t_], bias=stt_[0:m, 16 + t_:17 + t_])
                for t_ in range(ntile):
                    m = mt_[t_]
                    tt(DVE, gv[0:m, t_, :], gv[0:m, t_, :], gbc[0:m, :], ALU.mult, [d_gvt[t_], d_lw], [d_gvt[t_]])
                for t_ in range(ntile):
                    m = mt_[t_]
                    d_gv_t = d_gvt[t_]
                    if seg == 1:
                        tt(DVE, gv[0:m, t_, :], gv[0:m, t_, :], bbc[0:m, :], ALU.add, [d_gv_t, d_lw], [d_gv_t, d_gv])
                        out_dma(sgv[l], gv[0:m, t_, :], [d_gv_t, d_gv])
                        act(vb[0:m, t_, :], gv[0:m, t_, :], AF.Copy, [d_gv_t], [d_vb])
                    else:
                        tt(DVE, vb[0:m, t_, :], gv[0:m, t_, :], bbc[0:m, :], ALU.add, [d_gv_t, d_lw], [d_vb, d_gv])
                if part == 0:
                    return
                spatial_part(bj)

            p3_load(0)
            v_pass1(0)
            au_ag(0)
            stl0 = {"zs": {}, "frs": {}}
            for c in range(4):
                lru_step(0, c, stl0)
            mq_tiles(0)
            mem_pair(0, 0)
            mem_pair(0, 1)
            v_rest_spatial(0)

            pending_ln = []
            late_cast = []
            d_yblk = Dep("yblk")
            for bi, (n0, N, seg) in enumerate(BLOCKS):
                ntile = (N + 127) // 128
                xr_, dxr = xres[0], d_xres[0]
                nxt = bi + 1 if bi + 1 < len(BLOCKS) else None
                if nxt is not None:
                    p3_load(nxt)
                ob, d_ob, ocb, d_ocb, om, d_om = obs[bi % 2], d_obs[bi % 2], ocbs[bi % 2], d_ocbs[bi % 2], oms[bi % 2], d_oms[bi % 2]
                oa, d_oa = oas[bi % 2], d_oas[bi % 2]
                if debug:
                    srcs = [(oa, d_oa, 4, 0), (ob, d_ob, 4, 4), (ocb, d_ocb, 4, 8), (om, d_om, 2, 12)]
                    for (tb_, dtb, nch, k0) in srcs:
                        for c in range(nch):
                            t32, dt32 = T3()
                            cp(DVE, t32[:, 0:N], tb_[:, c, 0:N], [dtb], [dt32])
                            out_dma(dbg["br"][l][:, k0 + c, n0:n0 + N], t32[:, 0:N], [dt32])
                branches = [(oa, d_oa, 0, 4), (ob, d_ob, 4, 4), (ocb, d_ocb, 8, 4), (om, d_om, 12, 2)]
                stl = {"zs": {}, "frs": {}}
                for j in range(8):
                    wb_, dwb, rib = ringB.get(("WBR", l, j))
                    ma, dma_ = macc[j % 2], d_macc[j % 2]
                    for br in range(4):
                        pgt, dpgt = zmm_b(("W3", l, 22 + j * 4 + br), bi)
                        tg, dtg = T3()
                        act(tg[:, 0:N], pgt[:, 0:N], AF.Tanh, [dpgt], [dtg], scale=0.5)
                        src, dsrc, k0, nk = branches[br]
                        py, dpy = nb()
                        mmg([(py[:, 0:N], wb_[:, (k0 + k) * 128:(k0 + k + 1) * 128], src[:, k, 0:N], k == 0, k == nk - 1)
                             for k in range(nk)], [dwb, dsrc], [dpy])
                        if br == 0:
                            stt(ma[:, 0:N], tg[:, 0:N], 1.0, py[:, 0:N], ALU.add, ALU.mult, [dtg, dpy], [dma_])
                        else:
                            stt(tg[:, 0:N], tg[:, 0:N], 1.0, py[:, 0:N], ALU.add, ALU.mult, [dtg, dpy], [dtg])
                            tt(POOL, ma[:, 0:N], ma[:, 0:N], tg[:, 0:N], ALU.add, [dma_, dtg], [dma_])
                    ringB.rel(rib)
                    act(mg[:, j, 0:N], ma[:, 0:N], AF.Copy, [dma_], [d_mg])
                    if debug:
                        out_dma(dbg["mg"][l][:, j, n0:n0 + N], ma[:, 0:N], [dma_])
                    if nxt is not None:
                        if j < 4:
                            lru_step(nxt, j, stl)
                        elif j == 4:
                            mq_tiles(nxt)
                            mem_pair(nxt, 0)
                        elif j == 5:
                            mem_pair(nxt, 1)
                            v_pass1(nxt)
                        elif j == 6:
                            au_ag(nxt)
                            v_rest_spatial(nxt, part=0)
                        elif j == 7:
                            v_rest_spatial(nxt, part=1)
                    if pending_ln and j < 3:
                        pending_ln.pop(0)()
                    if l == 0 and j == 3:
                        emit_conv_some(11 if bi + 1 < len(BLOCKS) else 1000, conv_order_l1)
                    if j == 5:
                        p3_load_res(bi)
                    if j == 6 and late_cast:
                        late_cast.pop(0)()
                act(xr_[:, :, 0:N], xr_[:, :, 0:N], AF.Copy, [dxr], [dxr], scale=ALPHA)
                s1, ds1 = banks[6], dbank[6]
                s2b, ds2b = banks[7], dbank[7]
                pend = None
                for j in range(8):
                    wo_, dwo, rio = ringA.get(("WOUT", l, j))
                    pyo, dpyo = nb()
                    mmg([(pyo[:, 0:N], wo_[:, k * 128:(k + 1) * 128], mg[:, k, 0:N], k == 0, k == 7) for k in range(8)],
                        [dwo, d_mg], [dpyo])
                    ringA.rel(rio)
                    stt(xr_[:, j, 0:N], pyo[:, 0:N], 0.5, xr_[:, j, 0:N], ALU.mult, ALU.add, [dpyo, dxr], [dxr])
                    ta_, dta_ = tb16[0], d_tb16[0]
                    tq_, dtq_ = tb16[1], d_tb16[1]
                    if pend is not None:
                        pend()
                    act(ta_[:, 0:N], xr_[:, j, 0:N], AF.Copy, [dxr], [dta_])
                    act(tq_[:, 0:N], xr_[:, j, 0:N], AF.Square, [dxr], [dtq_])

                    def stats(j=j, ta_=ta_, dta_=dta_, tq_=tq_, dtq_=dtq_):
                        mmg([(s1[:, 0:N], ones_bf[:], ta_[:, 0:N], j == 0, j == 7)], [d_const, dta_], [ds1])
                        mmg([(s2b[:, 0:N], ones_bf[:], tq_[:, 0:N], j == 0, j == 7)], [d_const, dtq_], [ds2b])
                    pend = stats
                pend()
                if debug:
                    out_dma(dbg["t"][l][:, :, n0:n0 + N], xr_[:, :, 0:N], [dxr])
                tm, dtm = tmn[0], d_tmn[0]
                tn, dtn = tmn[1], d_tmn[1]
                tsc(DVE, tm[:, 0:N], s1[:, 0:N], 1.0 / 1024.0, None, ALU.mult, None, [ds1], [dtm])
                tt(DVE, tn[:, 0:N], tm[:, 0:N], tm[:, 0:N], ALU.mult, [dtm], [dtn])
                stt(tn[:, 0:N], s2b[:, 0:N], 1.0 / 1024.0, tn[:, 0:N], ALU.mult, ALU.subtract, [ds2b, dtn], [dtn])
                tsc(DVE, tn[:, 0:N], tn[:, 0:N], EPS, None, ALU.add, None, [dtn], [dtn])
                act(tn[:, 0:N], tn[:, 0:N], AF.Ln, [dtn], [dtn])
                act(tn[:, 0:N], tn[:, 0:N], AF.Exp, [dtn], [dtn], scale=-0.5)
                stt(tm[:, 0:N], tm[:, 0:N], -1.0, tn[:, 0:N], ALU.mult, ALU.mult, [dtm, dtn], [dtm])

                def ln_b1(N=N, xr_=xr_, dxr=dxr, tn=tn, dtn=dtn):
                    tt(DVE, xr_[:, :, 0:N], xr_[:, :, 0:N], tn[:, 0:N].unsqueeze(1).to_broadcast([128, 8, N]), ALU.mult,
                       [dxr, dtn], [dxr])

                def ln_b2(N=N, xr_=xr_, dxr=dxr, tm=tm, dtm=dtm):
                    tt(DVE, xr_[:, :, 0:N], xr_[:, :, 0:N], tm[:, 0:N].unsqueeze(1).to_broadcast([128, 8, N]), ALU.add,
                       [dxr, dtm], [dxr])

                def ln_c(N=N, n0=n0, xr_=xr_, dxr=dxr):
                    for j in range(8):
                        tsc(POOL, xr_[:, j, 0:N], xr_[:, j, 0:N], pv[:, 35 + j:36 + j], pv[:, 43 + j:44 + j], ALU.mult, ALU.add,
                            [dxr, d_lw], [dxr])
                    s_ = s_out[oi["i"] % 4]
                    oi["i"] += 1
                    P.dma(POOL, ydst[:, :, n0:n0 + N], xr_[:, :, 0:N], s_, reads=[dxr], writes=[d_yblk])

                def cast_late(N=N, n0=n0):
                    P.dma(POOL, xbf_d[l + 1][:, :, n0:n0 + N], xres_d[:, :, n0:n0 + N], s_out[oi["i"] % 4], reads=[d_yblk])
                    oi["i"] += 1

                if nxt is not None:
                    pending_ln.extend([ln_b1, ln_b2, ln_c])
                    if l + 1 < NL:
                        late_cast.append(cast_late)
                else:
                    ln_b1()
                    ln_b2()
                    ln_c()
                    if l + 1 < NL:
                        late_cast.append(cast_late)
                    while late_cast:
                        late_cast.pop(0)()
            P.barrier()

    if debug:
        out_dma(dbg["x1"], xres_d, [])
    for s in s_out:
        if P.dsem_val[s] > 0:
            P.q[POOL].append((lambda s=s, v=P.dsem_val[s]: nc.gpsimd.wait_ge(s, v)))
    assert not conv_pending(), conv_pending()
    if sched is None:
        st.close()
        return [ringA.rec, ringB.rec]
    assert ringA.consumed == len(ringA.sched) and ringB.consumed == len(ringB.sched)
    P.run()
    st.close()
    return nc


def _fm(a):
    T, F = a.shape
    return np.ascontiguousarray(a.T.reshape(F // 128, 128, T).transpose(1, 0, 2))


def _wtile(w):
    n = w.shape[1]
    t = np.zeros((128, 8, 128), np.float32)
    t[:, :, :n] = w.reshape(8, 128, n).transpose(1, 0, 2)
    return t.reshape(128, 1024)


def _prep_shared(inp):
    f32 = np.float32
    w_in = inp["w_in"]
    sw = np.concatenate([np.arange(16, 32), np.arange(0, 16)])
    W1 = np.zeros((NL, 5, 128, 1024), f32)
    WAV = np.zeros((NL, 128, 4096), f32)
    W3 = np.zeros((NL, N_W3, 128, 1024), f32)
    WBR = np.zeros((NL, 8, 128, 1792), f32)
    WOUT = np.zeros((NL, 8, 128, 1024), f32)
    WUQ = np.zeros((NL, 128, 1536), f32)
    WUQS = np.zeros((NL, 128, 1536), f32)
    WUKV = np.zeros((NL, 128, 1024), f32)
    MWK = np.zeros((NL, 128, 2048), f32)
    MWV = np.zeros((NL, 128, 2048), f32)
    WR = np.zeros((NL, 128, 512), f32)
    WI = np.zeros((NL, 128, 512), f32)
    WST = np.zeros((NL, 128, 512), f32)
    PV = np.zeros((NL, 128, NPV), f32)
    GLN = np.zeros((NL, 2, 512), f32)
    BS = np.zeros((NL, 1, 512), f32)
    cols3 = w3_tile_cols()
    for l in range(NL):
        w = w_in[l]
        W1[l, 0] = _wtile(w[:, 2560:2688])
        W1[l, 1] = _wtile(w[:, 2688:2816])
        W1[l, 2] = _wtile(w[:, 2816:2944])
        kr = np.zeros((1024, 128), f32)
        kr[:, 64:96] = w[:, 2944:2976]
        W1[l, 3] = _wtile(kr)
        krs = np.zeros((1024, 128), f32)
        krs[:, 64:96] = w[:, 2944 + sw]
        W1[l, 4] = _wtile(krs)
        WAV[l] = w[:, 512:1024].reshape(8, 128, 512).transpose(1, 0, 2).reshape(128, 4096)
        for i, cc in enumerate(cols3):
            W3[l, i] = _wtile(w[:, cc])
        wbr = inp["w_br"][l]
        for j in range(8):
            WBR[l, j] = wbr[:, j * 128:(j + 1) * 128].reshape(14, 128, 128).transpose(1, 0, 2).reshape(128, 1792)
            WOUT[l, j] = _wtile(inp["w_out"][l][:, j * 128:(j + 1) * 128])
        wuq = inp["mla_w_uq"][l]
        WUQ[l] = wuq.reshape(2, 128, 768).transpose(1, 0, 2).reshape(128, 1536)
        wuqs = np.zeros_like(wuq)
        for h in range(8):
            wuqs[:, h * 96 + 64:h * 96 + 96] = wuq[:, h * 96 + 64 + sw]
        WUQS[l] = wuqs.reshape(2, 128, 768).transpose(1, 0, 2).reshape(128, 1536)
        WUKV[l] = inp["mla_w_ukv"][l]
        MWK[l] = inp["mem_w_k"][l].reshape(8, 128, 256).transpose(1, 0, 2).reshape(128, 2048)
        MWV[l] = inp["mem_w_v"][l].reshape(8, 128, 256).transpose(1, 0, 2).reshape(128, 2048)
        for c in range(4):
            for hh in range(2):
                hb = 2 * c + hh
                WR[l, hh * 64:(hh + 1) * 64, c * 128 + hh * 64:c * 128 + (hh + 1) * 64] = inp["lru_w_r"][l][hb]
                WI[l, hh * 64:(hh + 1) * 64, c * 128 + hh * 64:c * 128 + (hh + 1) * 64] = inp["lru_w_i"][l][hb]
        WST[l] = inp["gmlp_ws"][l].transpose(2, 0, 1).reshape(128, 512)
        fmv = lambda v: v.reshape(-1, 128).T
        PV[l, :, 0:16] = inp["lru_conv_w"][l].reshape(4, 4, 128).transpose(2, 1, 0).reshape(128, 16)
        PV[l, :, 16:20] = fmv(inp["lru_conv_b"][l])
        PV[l, :, 20:24] = fmv(inp["lru_b_r"][l])
        PV[l, :, 24:28] = fmv(inp["lru_b_i"][l])
        PV[l, :, 28:32] = fmv(inp["lru_lambda"][l])
        PV[l, :, 32:34] = fmv(inp["mla_q_norm"][l])
        PV[l, :, 34:35] = fmv(inp["mla_kv_norm"][l])
        PV[l, :, 35:43] = fmv(inp["ln_g"][l])
        PV[l, :, 43:51] = fmv(inp["ln_b"][l])
        GLN[l, 0] = inp["gmlp_ln_g"][l]
        GLN[l, 1] = inp["gmlp_ln_b"][l]
        BS[l, 0] = inp["gmlp_bs"][l].reshape(512)
    pos = np.concatenate([np.arange(TP), PAST + np.arange(TS)]).astype(np.float32)
    freq = (np.float32(10000.0) ** (-np.arange(16, dtype=np.float32) / np.float32(16))).astype(np.float32)
    ang = pos[None, :] * freq[:, None]
    cosT = np.concatenate([np.cos(ang), np.cos(ang)], 0).astype(f32)
    sinT = np.concatenate([-np.sin(ang), np.sin(ang)], 0).astype(f32)
    maskT = (np.arange(128)[:, None] <= np.arange(128)[None, :]).astype(f32)
    return dict(W1=W1, WAV=WAV, W3=W3, WBR=WBR, WOUT=WOUT, WUQ=WUQ, WUQS=WUQS, WUKV=WUKV, MWK=MWK, MWV=MWV,
                WR=WR, WI=WI, WST=WST, PV=PV, GLN=GLN, BS=BS, cosT=cosT, sinT=sinT, maskT=maskT)


def _prep_core(inp, c):
    b = c % 4
    f32 = np.float32
    xtok = np.concatenate([inp["x_prompt"][b], inp["x_sample"][c]], 0)
    m = {"xT": _fm(xtok)}
    m["memT"] = _fm(inp["mem_prompt"][b])
    cmk = inp["cache_mem_k"][:, c].reshape(NL, 256, 256)
    m["cmkT"] = np.ascontiguousarray(cmk.transpose(0, 2, 1).reshape(NL, 2, 128, 256).transpose(0, 2, 1, 3))
    cmvv = inp["cache_mem_v"][:, c].reshape(NL, 256, 256)
    m["cmv"] = np.ascontiguousarray(cmvv.reshape(NL, 2, 128, 256).transpose(0, 2, 1, 3))
    m["cckvT"] = np.ascontiguousarray(inp["cache_mla_ckv"][:, c].transpose(0, 2, 1))
    m["ckrT"] = np.ascontiguousarray(inp["cache_mla_krope"][:, c].transpose(0, 2, 1))
    m["shin"] = np.ascontiguousarray(inp["state_lru_h"][:, c].reshape(NL, 4, 128).transpose(0, 2, 1))
    m["scin"] = np.ascontiguousarray(inp["state_lru_conv"][:, c].reshape(NL, 3, 4, 128).transpose(0, 3, 2, 1))
    return {k: np.ascontiguousarray(v, dtype=f32) for k, v in m.items()}


_CACHE = {}


def _run(inp, debug=False):
    key = "dbg" if debug else "prog"
    if key not in _CACHE:
        rec = build_program(debug=debug, sched=None)
        _CACHE[key] = build_program(debug=debug, sched=rec)
    nc = _CACHE[key]
    shared = _prep_shared(inp)
    in_maps = []
    for c in range(8):
        m = dict(shared)
        m.update(_prep_core(inp, c))
        in_maps.append(m)
    res = run_bass_kernel_spmd(nc, in_maps, core_ids=list(range(8)))
    return res.results


def _tok(a):
    p, c, t = a.shape
    return np.ascontiguousarray(a.transpose(2, 1, 0).reshape(t, c * p))


def kernel(**inputs):
    inp = {k: np.asarray(v, dtype=np.float32) for k, v in inputs.items()}
    r = _run(inp)
    f32 = np.float32
    y_prompt = np.stack([_tok(r[b]["yT"][:, :, :TP]) for b in range(4)]).astype(f32)
    y_sample = np.stack([_tok(r[c]["yT"][:, :, TP:]) for c in range(8)]).astype(f32)
    p_ckv = np.stack([np.stack([r[b]["ckvo"][l][:, :TP].T for b in range(4)]) for l in range(NL)]).astype(f32)
    p_kr = np.stack([np.stack([r[b]["kro"][l][:, :TP].T for b in range(4)]) for l in range(NL)]).astype(f32)
    p_mk = np.stack([np.stack([_tok(r[b]["pmk"][l]).reshape(256, 4, 64) for b in range(4)]) for l in range(NL)]).astype(f32)
    p_mv = np.stack([np.stack([r[b]["pmv"][l].transpose(1, 0, 2).reshape(256, 4, 64) for b in range(4)])
                     for l in range(NL)]).astype(f32)
    p_h = np.stack([np.stack([r[b]["lruh"][l, 0].T.reshape(512) for b in range(4)]) for l in range(NL)]).astype(f32)
    p_conv = np.stack([np.stack([r[b]["lruc"][l, 0].transpose(2, 1, 0).reshape(3, 512) for b in range(4)])
                       for l in range(NL)]).astype(f32)
    s_ckv = np.stack([np.stack([r[c]["ckvo"][l][:, TP:].T for c in range(8)]) for l in range(NL)]).astype(f32)
    s_kr = np.stack([np.stack([r[c]["kro"][l][:, TP:].T for c in range(8)]) for l in range(NL)]).astype(f32)
    s_h = np.stack([np.stack([r[c]["lruh"][l, 1].T.reshape(512) for c in range(8)]) for l in range(NL)]).astype(f32)
    s_conv = np.stack([np.stack([r[c]["lruc"][l, 1].transpose(2, 1, 0).reshape(3, 512) for c in range(8)])
                       for l in range(NL)]).astype(f32)
    s_v = np.stack([np.stack([r[c]["sgv"][l] for c in range(8)]) for l in range(NL)]).astype(f32)
    return (y_prompt, y_sample, p_ckv, p_kr, p_mk, p_mv, p_h, p_conv, s_ckv, s_kr, s_h, s_conv, s_v)
```

```python
import contextlib
import numpy as np
import concourse.bass as bass
import concourse.mybir as mybir
from concourse.bass_utils import run_bass_kernel_spmd

F32 = mybir.dt.float32
BF16 = mybir.dt.bfloat16
AF = mybir.ActivationFunctionType
ALU = mybir.AluOpType
PE, ACT, DVE, POOL, SP = "pe", "act", "dve", "pool", "sp"

NL = 2
D = 1024
TP = 4096
TS = 16
TT = TP + TS
PAST = 1024
KC = TP + PAST + TS
NPV = 51
D_IN = 7840
SM_SCALE = 96.0 ** -0.5
ALPHA = (2.0 * NL) ** 0.25
EPS = 1e-6
GELU_C = 0.7978845608028654
BLOCKS = [(i * 512, 512, 0) for i in range(8)] + [(TP, TS, 1)]
NSLOT_A = 7
NSLOT_B = 2
SEM_LIMIT = 30000


class Dep:
    __slots__ = ("w", "r", "name")

    def __init__(self, name=""):
        self.w = None
        self.r = {}
        self.name = name


class Prog:
    def __init__(self, nc):
        self.nc = nc
        self.q = {e: [] for e in (PE, ACT, DVE, POOL, SP)}
        self.eng = {PE: nc.tensor, ACT: nc.scalar, DVE: nc.vector, POOL: nc.gpsimd, SP: nc.sync}
        self.csem = {}
        self.cnt = {}
        self.nsem = 0
        for e in (PE, ACT, DVE, POOL):
            self._new_csem(e)
        self.seen = {e: {} for e in self.q}
        self.dsem_val = {}

    def _new_csem(self, e):
        self.nsem += 1
        self.csem[e] = self.nc.alloc_semaphore("c%s%d" % (e, self.nsem))
        self.cnt[e] = 0

    def dma_sem(self, name):
        s = self.nc.alloc_semaphore(name)
        self.dsem_val[s] = 0
        return s

    def _need(self, engine, ev, waits):
        if ev is None:
            return
        sem, val, src = ev
        if src == engine and engine == PE:
            return
        if src == "dma":
            val = self.dsem_val[sem]
        if self.seen[engine].get(sem, 0) >= val:
            return
        if waits.get(sem, 0) < val:
            waits[sem] = val

    def _waits(self, engine, reads, writes):
        waits = {}
        for d in reads:
            self._need(engine, d.w, waits)
        for d in writes:
            self._need(engine, d.w, waits)
            for e2, ev in d.r.items():
                if e2 == engine:
                    continue
                self._need(engine, ev, waits)
        q = self.q[engine]
        eng = self.eng[engine]
        for sem, val in waits.items():
            q.append((lambda eng=eng, sem=sem, val=val: eng.wait_ge(sem, val)))
            self.seen[engine][sem] = val

    def op(self, engine, fn, reads=(), writes=()):
        self._waits(engine, reads, writes)
        if self.cnt[engine] >= SEM_LIMIT:
            self._new_csem(engine)
        self.cnt[engine] += 1
        val = self.cnt[engine]
        sem = self.csem[engine]
        eng = self.eng[engine]
        self.q[engine].append((lambda eng=eng, fn=fn, sem=sem: fn(eng).then_inc(sem, 1)))
        ev = (sem, val, engine)
        for d in reads:
            d.r[engine] = ev
        for d in writes:
            d.w = ev
            d.r = {}
        return ev

    def dma(self, engine, out, in_, sem, reads=(), writes=()):
        self._waits(engine, reads, writes)
        self.dsem_val[sem] += 16
        val = self.dsem_val[sem]
        eng = self.eng[engine]
        self.q[engine].append(
            (lambda eng=eng, out=out, in_=in_, sem=sem: eng.dma_start(out=out, in_=in_).then_inc(sem, 16)))
        ev = (sem, val, "dma")
        for d in reads:
            d.r["dma:%d" % id(sem)] = ev
        for d in writes:
            d.w = ev
            d.r = {}
        return ev

    def barrier(self):
        targets = [(self.csem[e], self.cnt[e]) for e in (PE, ACT, DVE, POOL) if self.cnt[e] > 0]
        targets += [(s, v) for s, v in self.dsem_val.items() if v > 0]
        for e in (PE, ACT, DVE, POOL, SP):
            eng = self.eng[e]
            for sem, val in targets:
                if self.seen[e].get(sem, 0) >= val:
                    continue
                self.q[e].append((lambda eng=eng, sem=sem, val=val: eng.wait_ge(sem, val)))
                self.seen[e][sem] = val

    def run(self):
        nc = self.nc
        with nc.Block() as block:
            @block.tensor
            def _(e):
                for f in self.q[PE]:
                    f()

            @block.scalar
            def _(e):
                for f in self.q[ACT]:
                    f()

            @block.vector
            def _(e):
                for f in self.q[DVE]:
                    f()

            @block.gpsimd
            def _(e):
                for f in self.q[POOL]:
                    f()

            @block.sync
            def _(e):
                for f in self.q[SP]:
                    f()


def w3_tile_cols():
    tiles = []
    for c in range(4):
        tiles.append(np.arange(0 + c * 128, 0 + (c + 1) * 128))
        tiles.append(np.arange(1024 + c * 128, 1024 + (c + 1) * 128))
    for c in range(4):
        tiles.append(np.arange(2048 + c * 128, 2048 + (c + 1) * 128))
        tiles.append(np.arange(1536 + c * 128, 1536 + (c + 1) * 128))
    for c in range(4):
        tiles.append(np.arange(2976 + c * 128, 2976 + (c + 1) * 128))
    for c in range(2):
        tiles.append(np.arange(3488 + c * 128, 3488 + (c + 1) * 128))
    for j in range(8):
        for br in range(4):
            tiles.append(np.arange(3744 + br * 1024 + j * 128, 3744 + br * 1024 + (j + 1) * 128))
    return tiles


N_W3 = 54


def build_program(debug=False, sched=None):
    nc = bass.Bass("TRN2", target_bir_lowering=False)
    P = Prog(nc)

    def din(name, shape):
        return nc.dram_tensor(name, list(shape), F32, kind="ExternalInput").ap()

    def dout(name, shape):
        return nc.dram_tensor(name, list(shape), F32, kind="ExternalOutput").ap()

    def dscr(name, shape, dt):
        return nc.dram_tensor(name, list(shape), dt).ap()

    xT = din("xT", [128, 8, TT])
    cosT = din("cosT", [32, TT])
    sinT = din("sinT", [32, TT])
    maskT = din("maskT", [128, 128])
    memT = din("memT", [128, 8, 256])
    cmkT = din("cmkT", [NL, 128, 2, 256])
    cmv = din("cmv", [NL, 128, 2, 256])
    cckvT = din("cckvT", [NL, 128, PAST])
    ckrT = din("ckrT", [NL, 32, PAST])
    shin = din("shin", [NL, 128, 4])
    scin = din("scin", [NL, 128, 4, 3])
    W1 = din("W1", [NL, 5, 128, 1024])
    WAV = din("WAV", [NL, 128, 4096])
    W3 = din("W3", [NL, N_W3, 128, 1024])
    WBR = din("WBR", [NL, 8, 128, 1792])
    WOUT = din("WOUT", [NL, 8, 128, 1024])
    WUQ = din("WUQ", [NL, 128, 1536])
    WUQS = din("WUQS", [NL, 128, 1536])
    WUKV = din("WUKV", [NL, 128, 1024])
    MWK = din("MWK", [NL, 128, 2048])
    MWV = din("MWV", [NL, 128, 2048])
    WR = din("WR", [NL, 128, 512])
    WI = din("WI", [NL, 128, 512])
    WST = din("WST", [NL, 128, 512])
    PVin = din("PV", [NL, 128, NPV])
    GLN = din("GLN", [NL, 2, 512])
    BSin = din("BS", [NL, 1, 512])

    yT = dout("yT", [128, 8, TT])
    ckvo = dout("ckvo", [NL, 128, TT])
    kro = dout("kro", [NL, 32, TT])
    pmk = dout("pmk", [NL, 128, 2, 256])
    pmv = dout("pmv", [NL, 128, 2, 256])
    lruh = dout("lruh", [NL, 2, 128, 4])
    lruc = dout("lruc", [NL, 2, 128, 4, 3])
    sgv = dout("sgv", [NL, 16, 512])
    dbg = {}
    if debug:
        dbg["ocraw"] = dout("dbg_ocraw", [NL, 128, 4, TT])
        dbg["br"] = dout("dbg_br", [NL, 128, 14, TT])
        dbg["x1"] = dout("dbg_x1", [128, 8, TT])
        dbg["mg"] = dout("dbg_mg", [NL, 128, 8, TT])
        dbg["t"] = dout("dbg_t", [NL, 128, 8, TT])

    xres_d = dscr("xres_d", [128, 8, TT], F32)
    xbf_d = [dscr("xbf0_d", [128, 8, TT], BF16), dscr("xbf1_d", [128, 8, TT], BF16)]
    W1b = dscr("W1b", [NL, 5, 128, 1024], BF16)
    WAVb = dscr("WAVb", [NL, 128, 4096], BF16)
    W3b = dscr("W3b", [NL, N_W3, 128, 1024], BF16)
    WBRb = dscr("WBRb", [NL, 8, 128, 1792], BF16)
    WOUTb = dscr("WOUTb", [NL, 8, 128, 1024], BF16)
    WUQb = dscr("WUQb", [NL, 128, 1536], BF16)
    WUQSb = dscr("WUQSb", [NL, 128, 1536], BF16)
    WUKVb = dscr("WUKVb", [NL, 128, 1024], BF16)
    MWKb = dscr("MWKb", [NL, 128, 2048], BF16)
    MWVb = dscr("MWVb", [NL, 128, 2048], BF16)
    WRb = dscr("WRb", [NL, 128, 512], BF16)
    WIb = dscr("WIb", [NL, 128, 512], BF16)

    st = contextlib.ExitStack()

    uid = {"i": 0}

    def S(name, shape, dt=F32, stack=None):
        uid["i"] += 1
        t = (stack or st).enter_context(nc.sbuf_tensor("%s_%d" % (name, uid["i"]), list(shape), dt))
        return t

    def act(out, in_, func, reads, writes, bias=None, scale=None, accum=None):
        kw = {}
        if bias is not None:
            kw["bias"] = bias
        if scale is not None:
            kw["scale"] = scale
        if accum is not None:
            kw["accum_out"] = accum
        P.op(ACT, lambda e: e.activation(out=out, in_=in_, func=func, **kw), reads, writes)

    def tt(eng, out, in0, in1, op, reads, writes):
        P.op(eng, lambda e: e.tensor_tensor(out=out, in0=in0, in1=in1, op=op), reads, writes)

    def tsc(eng, out, in0, s1, s2, op0, op1, reads, writes):
        if s2 is None:
            P.op(eng, lambda e: e.tensor_scalar(out=out, in0=in0, scalar1=s1, scalar2=None, op0=op0), reads, writes)
        else:
            P.op(eng, lambda e: e.tensor_scalar(out=out, in0=in0, scalar1=s1, scalar2=s2, op0=op0, op1=op1), reads, writes)

    def stt(out, in0, scalar, in1, op0, op1, reads, writes, accum=None):
        if accum is None:
            P.op(DVE, lambda e: e.scalar_tensor_tensor(out=out, in0=in0, scalar=scalar, in1=in1, op0=op0, op1=op1), reads, writes)
        else:
            P.op(DVE, lambda e: e.scalar_tensor_tensor(out=out, in0=in0, scalar=scalar, in1=in1, op0=op0, op1=op1,
                                                       accum_out=accum), reads, writes)

    def cp(eng, out, in_, reads, writes):
        P.op(eng, lambda e: e.tensor_copy(out, in_), reads, writes)

    def memset(eng, ap, val, writes):
        P.op(eng, lambda e: e.memset(ap, val), (), writes)

    def mmg(mms, reads, writes):
        def fn(e, mms=mms):
            r = None
            for (o, l, rh, s0, s1) in mms:
                r = e.matmul(o, l, rh, start=s0, stop=s1)
            return r
        P.op(PE, fn, reads, writes)

    banks = [st.enter_context(nc.psum_tensor("bank%d" % i, [128, 512], F32)) for i in range(8)]
    dbank = [Dep("bank%d" % i) for i in range(8)]
    rr = {"i": 0}

    def nb():
        i = rr["i"] % 6
        rr["i"] += 1
        return banks[i], dbank[i]

    ones_bf = S("ones_bf", [128, 128], BF16)
    ones32 = S("ones32", [128, 128])
    maskS = S("maskS", [128, 128])
    d_const = Dep("const")
    ocraw = S("ocraw", [128, 4, TT], BF16)
    d_ocraw = Dep("ocraw")
    wav = S("wav", [128, 8, 512], BF16)
    wuq = S("wuq", [128, 2, 768], BF16)
    wuqs = S("wuqs", [128, 2, 768], BF16)
    wukv = S("wukv", [128, 1024], BF16)
    wr = S("wr", [128, 4, 128], BF16)
    wi = S("wi", [128, 4, 128], BF16)
    wst = S("wst", [128, 4, 128], BF16)
    gbc = S("gbc", [128, 512])
    bbc = S("bbc", [128, 512])
    bsrow = S("bsrow", [1, 512], BF16)
    pv = S("pv", [128, NPV])
    pvd = S("pvd", [128, 16])
    mkT = [S("mkT%d" % s, [128, 2, 256], BF16) for s in range(2)]
    mvx = [S("mvx%d" % s, [128, 2, 4, 128], BF16) for s in range(2)]
    hst = [S("hst%d" % s, [128, 4]) for s in range(2)]
    carry = [S("carry%d" % s, [128, 4, 3]) for s in range(2)]
    d_lw = Dep("layer_weights")
    d_mem = Dep("memkv")
    d_hst = [Dep("hst0"), Dep("hst1")]
    d_carry = [Dep("carry0"), Dep("carry1")]

    s_misc = P.dma_sem("s_misc")
    s_misc_sp = P.dma_sem("s_misc_sp")
    s_xst = P.dma_sem("s_xst")
    s_x = [P.dma_sem("s_x%d" % i) for i in range(5)]
    s_out = [P.dma_sem("s_out%d" % i) for i in range(4)]
    oi = {"i": 0}

    def out_dma(dst, src, reads):
        s = s_out[oi["i"] % 4]
        oi["i"] += 1
        P.dma(POOL, dst, src, s, reads=reads)

    s_conv = {}
    d_conv = {}
    conv_lists = {}
    for l in range(NL):
        g0 = [(W1b[l, i], W1[l, i]) for i in range(5)]
        g0 += [(WAVb[l][:, i * 1024:(i + 1) * 1024], WAV[l][:, i * 1024:(i + 1) * 1024]) for i in range(4)]
        g0 += [(WUQb[l], WUQ[l]), (WUQSb[l], WUQS[l]), (WUKVb[l], WUKV[l]), (WRb[l], WR[l]), (WIb[l], WI[l]),
               (MWKb[l], MWK[l]), (MWVb[l], MWV[l])]
        g1 = [(W3b[l, i], W3[l, i]) for i in range(22)]
        g2 = [(W3b[l, i], W3[l, i]) for i in range(22, N_W3)]
        g2 += [(WBRb[l, j], WBR[l, j]) for j in range(8)]
        g2 += [(WOUTb[l, j], WOUT[l, j]) for j in range(8)]
        for gi, g in enumerate((g0, g1, g2)):
            s_conv[(l, gi)] = P.dma_sem("s_conv%d_%d" % (l, gi))
            d_conv[(l, gi)] = Dep("conv%d_%d" % (l, gi))
            conv_lists[(l, gi)] = list(g)

    def emit_conv(l, gi, n=None):
        lst = conv_lists[(l, gi)]
        k = len(lst) if n is None else min(n, len(lst))
        for _ in range(k):
            dst, src = lst.pop(0)
            P.dma(POOL, dst, src, s_conv[(l, gi)])
        sem = s_conv[(l, gi)]
        d_conv[(l, gi)].w = (sem, P.dsem_val[sem], "dma")

    def conv_pending():
        return [(k, len(v)) for k, v in conv_lists.items() if v]

    def tile_src(key):
        kind, l_, i_ = key
        if kind == "W1":
            return W1b[l_, i_], 1024, d_conv[(l_, 0)]
        if kind == "W3":
            return W3b[l_, i_], 1024, d_conv[(l_, 1 if i_ < 22 else 2)]
        if kind == "WBR":
            return WBRb[l_, i_], 1792, d_conv[(l_, 2)]
        return WOUTb[l_, i_], 1024, d_conv[(l_, 2)]

    class Ring:
        def __init__(self, name, nslot, width, sched_):
            self.buf = [S("%s%d" % (name, i), [128, width], BF16) for i in range(nslot)]
            self.dep = [Dep() for _ in range(nslot)]
            self.sem = [P.dma_sem("s_%s%d" % (name, i)) for i in range(nslot)]
            self.n = nslot
            self.sched = sched_
            self.rec = []
            self.issued = 0
            self.consumed = 0
            self.low = 0
            self.done = set()

        def _issue(self, upto):
            if self.sched is None:
                return
            while self.issued < min(upto, len(self.sched)):
                i = self.issued
                sl = i % self.n
                ap, ncol, dc = tile_src(self.sched[i])
                if dc.w is None or conv_lists[[k for k, v in d_conv.items() if v is dc][0]]:
                    break
                P.dma(SP, self.buf[sl][:, 0:ncol], ap, self.sem[sl], reads=[dc], writes=[self.dep[sl]])
                self.issued += 1

        def get(self, key):
            i = self.consumed
            self.rec.append(key)
            if self.sched is not None:
                assert self.sched[i] == key, (i, self.sched[i], key)
                assert i < self.low + self.n
                self._issue(self.low + self.n)
                assert self.issued > i
            self.consumed += 1
            return self.buf[i % self.n], self.dep[i % self.n], i

        def rel(self, i):
            self.done.add(i)
            while self.low in self.done:
                self.done.discard(self.low)
                self.low += 1
            self._issue(self.low + self.n)

    ringA = Ring("rA", NSLOT_A, 1024, None if sched is None else sched[0])
    ringB = Ring("rB", NSLOT_B, 1792, None if sched is None else sched[1])

    memset(POOL, ones_bf[:], 1.0, [d_const])
    memset(POOL, ones32[:], 1.0, [d_const])
    P.dma(POOL, maskS[:], maskT, s_misc, writes=[d_const])
    emit_conv(0, 0)
    s_conv[(0, 9)] = P.dma_sem("s_xc")
    d_conv[(0, 9)] = Dep("xbf")
    conv_lists[(0, 9)] = [(xbf_d[0][:, :, n0_:n0_ + N_], xT[:, :, n0_:n0_ + N_]) for (n0_, N_, _sg) in BLOCKS]
    d_xbf_first = d_xbf_rest = d_conv[(0, 9)]
    conv_order = [(0, 9), (0, 1), (0, 2)]
    conv_order_l1 = [(1, 0), (1, 1), (1, 2)]

    def emit_conv_some(budget, order=None):
        for key in (order or conv_order):
            if budget <= 0:
                break
            n_ = min(budget, len(conv_lists[key]))
            if n_ > 0:
                emit_conv(key[0], key[1], n_)
                budget -= n_

    for l in range(NL):
        xsrc = xT if l == 0 else xres_d
        d_xsrc = Dep("xsrc")
        ydst = xres_d if l == 0 else yT

        with contextlib.ExitStack() as ph:
            memTb = S("memTb", [128, 8, 256], BF16, ph)
            d_memTb = Dep("memTb")
            P.dma(POOL, memTb[:], memT, s_misc, writes=[d_memTb])
            mwk = S("mwk", [128, 8, 256], BF16, ph)
            mwv = S("mwv", [128, 8, 256], BF16, ph)
            mk32 = S("mk32", [128, 2, 256], F32, ph)
            mv32 = S("mv32", [128, 2, 256], F32, ph)
            cm32 = S("cm32", [128, 2, 256], F32, ph)
            cv32 = S("cv32", [128, 2, 256], F32, ph)
            wst32 = S("wst32", [128, 4, 128], F32, ph)
            bsrow32 = S("bsrow32", [1, 512], F32, ph)
            d_p0 = Dep("p0")
            dcv = d_conv[(l, 0)]
            P.dma(SP, wav[:], WAVb[l].rearrange("p (k n) -> p k n", k=8), s_misc_sp, reads=[dcv], writes=[d_lw])
            P.dma(SP, wuq[:], WUQb[l].rearrange("p (k n) -> p k n", k=2), s_misc_sp, reads=[dcv], writes=[d_lw])
            P.dma(SP, wuqs[:], WUQSb[l].rearrange("p (k n) -> p k n", k=2), s_misc_sp, reads=[dcv], writes=[d_lw])
            P.dma(SP, wukv[:], WUKVb[l], s_misc_sp, reads=[dcv], writes=[d_lw])
            P.dma(SP, wr[:], WRb[l].rearrange("p (k n) -> p k n", k=4), s_misc_sp, reads=[dcv], writes=[d_lw])
            P.dma(SP, wi[:], WIb[l].rearrange("p (k n) -> p k n", k=4), s_misc_sp, reads=[dcv], writes=[d_lw])
            P.dma(SP, mwk[:], MWKb[l].rearrange("p (k n) -> p k n", k=8), s_misc_sp, reads=[dcv], writes=[d_p0])
            P.dma(SP, mwv[:], MWVb[l].rearrange("p (k n) -> p k n", k=8), s_misc_sp, reads=[dcv], writes=[d_p0])
            P.dma(SP, wst32[:], WST[l].rearrange("p (k n) -> p k n", k=4), s_misc_sp, writes=[d_lw])
            P.dma(SP, pv[:], PVin[l], s_misc_sp, writes=[d_lw])
            P.dma(SP, gbc[:], GLN[l][0:1, :].partition_broadcast(128), s_misc_sp, writes=[d_lw])
            P.dma(SP, bbc[:], GLN[l][1:2, :].partition_broadcast(128), s_misc_sp, writes=[d_lw])
            P.dma(SP, bsrow32[:], BSin[l], s_misc_sp, writes=[d_lw])
            P.dma(SP, cm32[:], cmkT[l], s_misc_sp, writes=[d_p0])
            P.dma(SP, cv32[:], cmv[l], s_misc_sp, writes=[d_p0])
            P.dma(SP, hst[1][:], shin[l], s_misc_sp, writes=[d_hst[1]])
            P.dma(SP, carry[1][:], scin[l], s_misc_sp, writes=[d_carry[1]])
            memset(POOL, hst[0][:], 0.0, [d_hst[0]])
            memset(POOL, carry[0][:], 0.0, [d_carry[0]])
            tt(DVE, wst[:], wst32[:], maskS[:].unsqueeze(1).to_broadcast([128, 4, 128]), ALU.mult,
               [d_lw, d_const], [d_lw])
            cp(DVE, bsrow[:], bsrow32[:], [d_lw], [d_lw])
            tsc(DVE, pvd[:, 0:4], pv[:, 20:24], 0.5, None, ALU.mult, None, [d_lw], [d_lw])
            tsc(DVE, pvd[:, 4:8], pv[:, 24:28], 0.5, None, ALU.mult, None, [d_lw], [d_lw])
            act(pvd[:, 12:16], pv[:, 28:32], AF.Exp, [d_lw], [d_lw], scale=-1.0)
            act(pvd[:, 12:16], pvd[:, 12:16], AF.Ln, [d_lw], [d_lw], bias=1.0)
            tsc(DVE, pvd[:, 8:12], pvd[:, 12:16], -4.0, None, ALU.mult, None, [d_lw], [d_lw])
            tsc(DVE, pvd[:, 12:16], pvd[:, 12:16], -8.0, None, ALU.mult, None, [d_lw], [d_lw])
            memset(POOL, mvx[0][:], 1.0, [d_mem])
            memset(POOL, mvx[1][:], 1.0, [d_mem])
            for c in range(2):
                bk, db = nb()
                mmg([(bk[:, 0:256], mwk[:, k, c * 128:(c + 1) * 128], memTb[:, k, :], k == 0, k == 7) for k in range(8)],
                    [d_p0, d_memTb], [db])
                act(mk32[:, c, :], bk[:, 0:256], AF.Copy, [db], [d_p0])
            cp(DVE, mkT[0][:], mk32[:], [d_p0], [d_mem])
            out_dma(pmk[l], mk32[:], [d_p0])
            for mt in range(2):
                bk, db = nb()
                mmg([(bk[:, 0:256], memTb[:, k, mt * 128:(mt + 1) * 128], mwv[:, k, :], k == 0, k == 7) for k in range(8)],
                    [d_p0, d_memTb], [db])
                act(mv32[:, mt, :], bk[:, 0:256], AF.Copy, [db], [d_p0])
            out_dma(pmv[l], mv32[:], [d_p0])
            for mt in range(2):
                cp(DVE, mvx[0][:, mt, :, 0:64], mv32[:, mt, :].rearrange("p (h d) -> p h d", h=4), [d_p0], [d_mem])
                cp(DVE, mvx[1][:, mt, :, 0:64], cv32[:, mt, :].rearrange("p (h d) -> p h d", h=4), [d_p0], [d_mem])
            cp(DVE, mkT[1][:], cm32[:], [d_p0], [d_mem])
            P.barrier()

        with contextlib.ExitStack() as ph:
            cosS = S("cosS", [128, TT], F32, ph)
            sinS = S("sinS", [128, TT], F32, ph)
            d_tab = Dep("tab")
            P.dma(POOL, cosS[64:96, :], cosT, s_misc, writes=[d_tab])
            P.dma(POOL, sinS[64:96, :], sinT, s_misc, writes=[d_tab])
            cqn = S("cqn", [128, 2, TT], BF16, ph)
            ckvall = S("ckvall", [128, KC], BF16, ph)
            Kb = S("Kb", [96, KC], BF16, ph)
            tmp = [S("t1_%d" % i, [128, 512], F32, ph) for i in range(4)]
            p1s = contextlib.ExitStack()
            xb = [S("xb1_%d" % i, [128, 8, 512], BF16, p1s) for i in range(2)]
            sq = [S("sq%d" % i, [128, 512], BF16, p1s) for i in range(3)]
            xst = S("xst", [128, 8, 512], F32, p1s) if l == 0 else None
            d_xst = Dep()
            d_cqn, d_ckvall, d_Kn, d_Kr, d_Q, d_V = Dep(), Dep(), Dep(), Dep(), Dep(), Dep()
            d_xb = [Dep(), Dep()]
            d_PT = [Dep() for _ in range(4)]
            d_tmp = [Dep() for _ in range(4)]
            d_sq = [Dep(), Dep(), Dep()]
            d_rsum = [Dep(), Dep()]
            ti = {"i": 0}

            def T1():
                i = ti["i"] % 4
                ti["i"] += 1
                return tmp[i], d_tmp[i]

            P.dma(POOL, ckvall[:, TP:TP + PAST], cckvT[l], s_misc, writes=[d_ckvall])
            P.dma(POOL, Kb[64:96, TP:TP + PAST], ckrT[l], s_misc, writes=[d_Kr])
            def p1_load(bj):
                n0_, N_, _ = BLOCKS[bj]
                if l == 0:
                    P.dma(ACT, xst[:, :, 0:N_], xT[:, :, n0_:n0_ + N_], s_xst, writes=[d_xst])
                else:
                    P.dma(ACT, xb[bj % 2][:, :, 0:N_], xbf_d[l][:, :, n0_:n0_ + N_], s_x[bj % 2], writes=[d_xb[bj % 2]])

            def p1_cast(bj):
                _, N_, _ = BLOCKS[bj]
                act(xb[bj % 2][:, 0:4, 0:N_], xst[:, 0:4, 0:N_], AF.Copy, [d_xst], [d_xb[bj % 2]])
                cp(DVE, xb[bj % 2][:, 4:8, 0:N_], xst[:, 4:8, 0:N_], [d_xst], [d_xb[bj % 2]])

            p1_load(0)
            for bi, (n0, N, seg) in enumerate(BLOCKS):
                cc0 = n0 if seg == 0 else TP + PAST
                x_b, dx = xb[bi % 2], d_xb[bi % 2]
                if l == 0:
                    p1_cast(bi)
                if bi + 1 < len(BLOCKS):
                    p1_load(bi + 1)
                pz = []
                for i in range(5):
                    wt_, dw, ri = ringA.get(("W1", l, i))
                    bk, db = nb()
                    mmg([(bk[:, 0:N], wt_[:, k * 128:(k + 1) * 128], x_b[:, k, 0:N], k == 0, k == 7) for k in range(8)],
                        [dw, dx], [db])
                    ringA.rel(ri)
                    pz.append((bk, db))
                k1, dk1 = T1()
                k2, dk2 = T1()
                tt(DVE, k1[64:96, 0:N], pz[3][0][64:96, 0:N], cosS[64:96, n0:n0 + N], ALU.mult, [pz[3][1], d_tab], [dk1])
                tt(DVE, k2[64:96, 0:N], pz[4][0][64:96, 0:N], sinS[64:96, n0:n0 + N], ALU.mult, [pz[4][1], d_tab], [dk2])
                tt(DVE, k1[64:96, 0:N], k1[64:96, 0:N], k2[64:96, 0:N], ALU.add, [dk1, dk2], [dk1])
                bs_, dbs = banks[6], dbank[6]
                bs2, dbs2 = banks[7], dbank[7]
                for c in range(3):
                    act(sq[c][:, 0:N], pz[c][0][:, 0:N], AF.Square, [pz[c][1]], [d_sq[c]])
                mmg([(bs_[:, 0:N], ones_bf[:], sq[c][:, 0:N], c == 0, c == 1) for c in range(2)],
                    [d_const, d_sq[0], d_sq[1]], [dbs])
                mmg([(bs2[:, 0:N], ones_bf[:], sq[2][:, 0:N], True, True)], [d_const, d_sq[2]], [dbs2])
                r_, dr = T1()
                r2, dr2 = T1()
                tsc(DVE, r_[:, 0:N], bs_[:, 0:N], 1.0 / 256.0, EPS, ALU.mult, ALU.add, [dbs], [dr])
                tsc(DVE, r2[:, 0:N], bs2[:, 0:N], 1.0 / 128.0, EPS, ALU.mult, ALU.add, [dbs2], [dr2])
                act(r_[:, 0:N], r_[:, 0:N], AF.Ln, [dr], [dr])
                act(r2[:, 0:N], r2[:, 0:N], AF.Ln, [dr2], [dr2])
                act(r_[:, 0:N], r_[:, 0:N], AF.Exp, [dr], [dr], scale=-0.5)
                act(r2[:, 0:N], r2[:, 0:N], AF.Exp, [dr2], [dr2], scale=-0.5)
                for c in range(2):
                    stt(cqn[:, c, n0:n0 + N], pz[c][0][:, 0:N], pv[:, 32 + c:33 + c], r_[:, 0:N], ALU.mult, ALU.mult,
                        [pz[c][1], d_lw, dr], [d_cqn])
                stt(r2[:, 0:N], pz[2][0][:, 0:N], pv[:, 34:35], r2[:, 0:N], ALU.mult, ALU.mult, [pz[2][1], d_lw, dr2], [dr2])
                out_dma(ckvo[l][:, n0:n0 + N], r2[:, 0:N], [dr2])
                act(ckvall[:, cc0:cc0 + N], r2[:, 0:N], AF.Copy, [dr2], [d_ckvall])
                out_dma(kro[l][:, n0:n0 + N], k1[64:96, 0:N], [dk1])
                act(Kb[64:96, cc0:cc0 + N], k1[64:96, 0:N], AF.Copy, [dk1], [d_Kr])

            P.barrier()
            p1s.close()
            Qb = S("Qb", [96, TT], BF16, ph)
            Vb = S("Vb", [128, 41, 128], BF16, ph)
            PT = [S("PT%d" % i, [128, 512], BF16, ph) for i in range(4)]
            rsum = [S("rsum%d" % i, [128, 512], F32, ph) for i in range(2)]
            memset(POOL, Vb[:], 1.0, [d_V])
            Kbs = [Kb, S("Kb1", [96, KC], BF16, ph)]
            Qbs = [Qb, S("Qb1", [96, TT], BF16, ph)]
            d_Kns = [d_Kn, Dep()]
            d_Qs = [d_Q, Dep()]
            cp(DVE, Kbs[1][64:96, :], Kb[64:96, :], [d_Kr], [d_Kr])
            obank = [(banks[6], dbank[6]), (banks[7], dbank[7])]

            def kq_pieces(h):
                Kh, dKh, Qh, dQh = Kbs[h % 2], d_Kns[h % 2], Qbs[h % 2], d_Qs[h % 2]
                out = []
                c0 = 0
                while c0 < KC:
                    n = min(512, KC - c0)

                    def kp(c0=c0, n=n):
                        bk, db = nb()
                        mmg([(bk[0:64, 0:n], wukv[:, h * 128:h * 128 + 64], ckvall[:, c0:c0 + n], True, True)],
                            [d_lw, d_ckvall], [db])
                        cp(DVE, Kh[0:64, c0:c0 + n], bk[0:64, 0:n], [db], [dKh])
                    out.append(kp)
                    c0 += n
                for (n0, N, seg) in BLOCKS:
                    def qp(n0=n0, N=N):
                        qa, dqa = nb()
                        qs, dqs = nb()
                        mmg([(qa[0:96, 0:N], wuq[:, k, h * 96:(h + 1) * 96], cqn[:, k, n0:n0 + N], k == 0, k == 1) for k in range(2)],
                            [d_lw, d_cqn], [dqa])
                        mmg([(qs[0:96, 0:N], wuqs[:, k, h * 96:(h + 1) * 96], cqn[:, k, n0:n0 + N], k == 0, k == 1) for k in range(2)],
                            [d_lw, d_cqn], [dqs])
                        cp(DVE, Qh[0:64, n0:n0 + N], qa[0:64, 0:N], [dqa], [dQh])
                        k1, dk1 = T1()
                        k2, dk2 = T1()
                        tt(DVE, k1[64:96, 0:N], qa[64:96, 0:N], cosS[64:96, n0:n0 + N], ALU.mult, [dqa, d_tab], [dk1])
                        tt(DVE, k2[64:96, 0:N], qs[64:96, 0:N], sinS[64:96, n0:n0 + N], ALU.mult, [dqs, d_tab], [dk2])
                        tt(DVE, Qh[64:96, n0:n0 + N], k1[64:96, 0:N], k2[64:96, 0:N], ALU.add, [dk1, dk2], [dQh])
                    out.append(qp)
                return out

            for f_ in kq_pieces(0):
                f_()
            for h in range(8):
                Kh, dKh, Qh, dQh = Kbs[h % 2], d_Kns[h % 2], Qbs[h % 2], d_Qs[h % 2]
                pieces = kq_pieces(h + 1) if h + 1 < 8 else []
                if l == 0:
                    emit_conv_some(11 if h < 7 else 1000)
                for b8 in range(5):
                    bk, db = nb()
                    mmg([(bk[:, i * 64:(i + 1) * 64], ckvall[:, (b8 * 8 + i) * 128:(b8 * 8 + i + 1) * 128],
                          wukv[:, h * 128 + 64:h * 128 + 128], True, True) for i in range(8)],
                        [d_lw, d_ckvall], [db])
                    cp(DVE, Vb[:, b8 * 8:b8 * 8 + 8, 0:64], bk[:, :].rearrange("p (t d) -> p t d", t=8), [db], [d_V])
                bk, db = nb()
                mmg([(bk[0:16, 0:64], ckvall[:, TP + PAST:KC], wukv[:, h * 128 + 64:h * 128 + 128], True, True)],
                    [d_lw, d_ckvall], [db])
                cp(DVE, Vb[0:16, 40, 0:64], bk[0:16, 0:64], [db], [d_V])
                items = []
                for qb in range(8):
                    nk = 4 * qb + 4
                    for kt in range(nk):
                        j = kt - 4 * qb
                        cs = 128 * j if j > 0 else 0
                        items.append(dict(q0=qb * 512, nq=512, cs=cs, kcol=kt * 128, kn=128, vt=kt, diag=(j >= 0),
                                          first=(kt == 0), last=(kt == nk - 1), ob=qb % 2))
                for i in range(8):
                    items.append(dict(q0=TP, nq=TS, cs=0, kcol=TP + 128 * i, kn=128, vt=32 + i, diag=False,
                                      first=(i == 0), last=False, ob=0))
                items.append(dict(q0=TP, nq=TS, cs=0, kcol=TP + PAST, kn=TS, vt=40, diag=False, first=False, last=True, ob=0))
                LOOK = 3
                inflight = []
                norms = []
                pti = 0
                for ii in range(len(items) + LOOK + 9):
                    while norms and norms[0][0] <= ii:
                        norms.pop(0)[1]()
                    if pieces and ii >= 12 and ii % 5 == 0:
                        pieces.pop(0)()
                    if ii < len(items):
                        it = items[ii]
                        sb_, dsb = nb()
                        pt_, dpt = PT[pti % 4], d_PT[pti % 4]
                        pti += 1
                        kn, cs, nq, q0 = it["kn"], it["cs"], it["nq"], it["q0"]
                        mmg([(sb_[0:kn, cs:nq], Kh[0:96, it["kcol"]:it["kcol"] + kn], Qh[0:96, q0 + cs:q0 + nq], True, True)],
                            [dKh, d_Kr, dQh], [dsb])
                        act(pt_[0:kn, cs:nq], sb_[0:kn, cs:nq], AF.Exp, [dsb], [dpt], scale=SM_SCALE)
                        if it["diag"]:
                            memset(POOL, pt_[64:128, cs:cs + 64], 0.0, [dpt])
                        inflight.append((it, pt_, dpt))
                    if ii >= LOOK and inflight:
                        it, pt_, dpt = inflight.pop(0)
                        ob_, dob = obank[it["ob"]]
                        kn, cs, nq, q0 = it["kn"], it["cs"], it["nq"], it["q0"]
                        mmg([(ob_[0:128, cs:nq], Vb[0:kn, it["vt"], 0:128], pt_[0:kn, cs:nq], it["first"], it["last"])],
                            [d_V, dpt], [dob])
                        if it["last"]:
                            rs_, drs = rsum[it["ob"]], d_rsum[it["ob"]]
                            P.op(DVE, (lambda e, o=rs_[64:128, 0:nq], i_=ob_[64:128, 0:nq]: e.reciprocal(o, i_)), [dob], [drs])
                            p0 = (h % 2) * 64
                            tt(DVE, ocraw[p0:p0 + 64, h // 2, q0:q0 + nq], ob_[0:64, 0:nq], rs_[64:128, 0:nq], ALU.mult,
                               [dob, drs], [d_ocraw])
                while pieces:
                    pieces.pop(0)()
            if debug:
                t32 = tmp[0]
                for c in range(4):
                    for (n0, N, seg) in BLOCKS:
                        cp(DVE, t32[:, 0:N], ocraw[:, c, n0:n0 + N], [d_ocraw], [d_tmp[0]])
                        out_dma(dbg["ocraw"][l][:, c, n0:n0 + N], t32[:, 0:N], [d_tmp[0]])
            P.barrier()

        with contextlib.ExitStack() as ph:
            xres = [S("xres%d" % i, [128, 8, 512], F32, ph) for i in range(1)]
            xb3 = [S("xb3_%d" % i, [128, 8, 512], BF16, ph) for i in range(2)]
            vb = S("vb", [128, 4, 512], BF16, ph)
            uag = S("uag", [128, 4, 512], BF16, ph)
            oas = [S("oa%d" % i, [128, 4, 512], BF16, ph) for i in range(2)]
            d_oas = [Dep(), Dep()]
            obs = [S("ob%d" % i, [128, 4, 512], BF16, ph) for i in range(2)]
            ocbs = [S("ocb%d" % i, [128, 4, 512], BF16, ph) for i in range(2)]
            mq = S("mq", [128, 2, 512], BF16, ph)
            oms = [S("om%d" % i, [128, 2, 512], BF16, ph) for i in range(2)]
            mg = S("mg", [128, 8, 512], BF16, ph)
            bxs = [S("bx%d" % i, [128, 515], F32, ph) for i in range(2)]
            xcs = [S("xc%d" % i, [128, 512], F32, ph) for i in range(2)]
            xcbs = [S("xcb%d" % i, [128, 512], BF16, ph) for i in range(2)]
            sgbs = [S("sgb%d" % i, [128, 512], F32, ph) for i in range(2)]
            d_sgbs = [Dep(), Dep()]
            macc = [S("macc%d" % i, [128, 512], F32, ph) for i in range(2)]
            tmp = [S("t3_%d" % i, [128, 512], F32, ph) for i in range(4)]
            tb16 = [S("tb16_%d" % i, [128, 512], BF16, ph) for i in range(2)]
            gv = S("gv", [128, 4, 512], F32, ph)
            d_gv = Dep()
            tmn = [S("tmn%d" % i, [128, 512], F32, ph) for i in range(2)]
            d_tmn = [Dep(), Dep()]
            PT3 = [S("PT3_%d" % i, [128, 512], BF16, ph) for i in range(4)]
            stt_ = S("stat", [128, 20], F32, ph)
            d_xres = [Dep()]
            d_xb3 = [Dep(), Dep()]
            d_vb, d_uag, d_mq, d_mg = [Dep() for _ in range(4)]
            d_obs, d_ocbs, d_oms = [Dep(), Dep()], [Dep(), Dep()], [Dep(), Dep()]
            d_stat = Dep()
            d_bxs, d_xcs, d_xcbs = [Dep(), Dep()], [Dep(), Dep()], [Dep(), Dep()]
            d_macc = [Dep(), Dep()]
            d_tmp = [Dep() for _ in range(4)]
            d_tb16 = [Dep() for _ in range(2)]
            d_PT3 = [Dep() for _ in range(4)]
            ti = {"i": 0}

            def T3():
                i = ti["i"] % 4
                ti["i"] += 1
                return tmp[i], d_tmp[i]

            def silu2(dst, ddst, pz, dpz, N):
                act(dst[:, 0:N], pz[:, 0:N], AF.Tanh, [dpz], [ddst], scale=0.5)
                stt(dst[:, 0:N], dst[:, 0:N], 1.0, pz[:, 0:N], ALU.add, ALU.mult, [ddst, dpz], [ddst])

            def gelu2(dst, ddst, pz, dpz, m, n):
                xs, dxs = T3()
                act(xs[0:m, 0:n], pz[0:m, 0:n], AF.Copy, [dpz], [dxs])
                act(dst[0:m, 0:n], pz[0:m, 0:n], AF.Square, [dpz], [ddst])
                tsc(DVE, dst[0:m, 0:n], dst[0:m, 0:n], 0.044715, 1.0, ALU.mult, ALU.add, [ddst], [ddst])
                tt(DVE, dst[0:m, 0:n], dst[0:m, 0:n], xs[0:m, 0:n], ALU.mult, [ddst, dxs], [ddst])
                act(dst[0:m, 0:n], dst[0:m, 0:n], AF.Tanh, [ddst], [ddst], scale=GELU_C)
                stt(dst[0:m, 0:n], dst[0:m, 0:n], 1.0, xs[0:m, 0:n], ALU.add, ALU.mult, [ddst, dxs], [ddst])

            def p3_load(bj):
                n0_, N_, _ = BLOCKS[bj]
                rd = [] if l == 1 else [d_xbf_first if bj == 0 else d_xbf_rest]
                P.dma(ACT, xb3[bj % 2][:, :, 0:N_], xbf_d[l][:, :, n0_:n0_ + N_], s_x[2 + bj % 2], reads=rd, writes=[d_xb3[bj % 2]])

            def p3_load_res(bj):
                n0_, N_, _ = BLOCKS[bj]
                P.dma(ACT, xres[0][:, :, 0:N_], xsrc[:, :, n0_:n0_ + N_], s_x[4], reads=[d_xsrc], writes=[d_xres[0]])

            def zmm_b(key, bj):
                _, N_, _ = BLOCKS[bj]
                xbj, dxbj = xb3[bj % 2], d_xb3[bj % 2]
                wt_, dw, ri = ringA.get(key)
                bk, db = nb()
                mmg([(bk[:, 0:N_], wt_[:, k * 128:(k + 1) * 128], xbj[:, k, 0:N_], k == 0, k == 7) for k in range(8)],
                    [dw, dxbj], [db])
                ringA.rel(ri)
                return bk, db

            def au_ag(bj):
                _, N_, _ = BLOCKS[bj]
                for c in range(4):
                    pu, dpu = zmm_b(("W3", l, 2 * c), bj)
                    u2, du2 = T3()
                    act(u2[:, 0:N_], pu[:, 0:N_], AF.Gelu_apprx_tanh, [dpu], [du2])
                    pg, dpg = zmm_b(("W3", l, 2 * c + 1), bj)
                    s2, ds2 = T3()
                    silu2(s2, ds2, pg, dpg, N_)
                    stt(uag[:, c, 0:N_], s2[:, 0:N_], 0.5, u2[:, 0:N_], ALU.mult, ALU.mult, [ds2, du2], [d_uag])

            def v_pass1(bj):
                n0_, N_, _ = BLOCKS[bj]
                xbj, dxbj = xb3[bj % 2], d_xb3[bj % 2]
                memset(POOL, stt_[:, :], 1.0, [d_stat])
                for t_ in range((N_ + 127) // 128):
                    m = min(128, N_ - t_ * 128)
                    bk, db = nb()
                    mmg([(bk[0:m, :], xbj[:, k, t_ * 128:t_ * 128 + m], wav[:, k, :], k == 0, k == 7) for k in range(8)],
                        [dxbj, d_lw], [db])
                    act(gv[0:m, t_, :], bk[0:m, :], AF.Gelu_apprx_tanh, [db], [d_gv, d_stat], accum=stt_[0:m, t_:t_ + 1])
                    act(PT3[3][0:m, :], gv[0:m, t_, :], AF.Square, [d_gv], [d_PT3[3], d_stat], accum=stt_[0:m, 4 + t_:5 + t_])

            def lru_z(bj, c):
                pg, dpg = zmm_b(("W3", l, 8 + 2 * c), bj)
                px, dpx = zmm_b(("W3", l, 8 + 2 * c + 1), bj)
                return pg, dpg, px, dpx

            def cg_tile(bj, c):
                n0_, N_, _ = BLOCKS[bj]
                pc, dpc = zmm_b(("W3", l, 16 + c), bj)
                sg, dsg = T3()
                act(sg[:, 0:N_], pc[:, 0:N_], AF.Silu, [dpc], [dsg])
                tt(DVE, ocbs[bj % 2][:, c, 0:N_], ocraw[:, c, n0_:n0_ + N_], sg[:, 0:N_], ALU.mult, [d_ocraw, dsg], [d_ocbs[bj % 2]])

            def lru_front(bj, c, zc):
                _, N_, seg_ = BLOCKS[bj]
                pg, dpg, px, dpx = zc
                bx, d_bx, xc, d_xc, xcb, d_xcb = bxs[c % 2], d_bxs[c % 2], xcs[c % 2], d_xcs[c % 2], xcbs[c % 2], d_xcbs[c % 2]
                sg, dsg = sgbs[c % 2], d_sgbs[c % 2]
                act(sg[:, 0:N_], pg[:, 0:N_], AF.Silu, [dpg], [dsg])
                cp(POOL, bx[:, 0:3], carry[seg_][:, c, :], [d_carry[seg_]], [d_bx])
                act(bx[:, 3:3 + N_], px[:, 0:N_], AF.Copy, [dpx], [d_bx])
                cp(POOL, carry[seg_][:, c, :], bx[:, N_:N_ + 3], [d_bx], [d_carry[seg_]])
                tsc(DVE, xc[:, 0:N_], bx[:, 0:N_], pv[:, 4 * c:4 * c + 1], pv[:, 16 + c:17 + c], ALU.mult, ALU.add,
                    [d_bx, d_lw], [d_xc])
                for k in range(1, 3):
                    stt(xc[:, 0:N_], bx[:, k:k + N_], pv[:, 4 * c + k:4 * c + k + 1], xc[:, 0:N_], ALU.mult, ALU.add,
                        [d_bx, d_lw, d_xc], [d_xc])
                stt(xcb[:, 0:N_], bx[:, 3:3 + N_], pv[:, 4 * c + 3:4 * c + 4], xc[:, 0:N_], ALU.mult, ALU.add,
                    [d_bx, d_lw, d_xc], [d_xcb])
                stt(xc[:, 0:N_], bx[:, 3:3 + N_], pv[:, 4 * c + 3:4 * c + 4], xc[:, 0:N_], ALU.mult, ALU.add,
                    [d_bx, d_lw, d_xc], [d_xc])
                return sg, dsg

            def lru_step(bj, c, stl):
                n0_, N_, seg_ = BLOCKS[bj]
                frs = stl["frs"]
                if c == 0:
                    frs[0] = lru_front(bj, 0, lru_z(bj, 0))
                sg, dsg = frs[c]
                xc, d_xc, xcb, d_xcb = xcs[c % 2], d_xcs[c % 2], xcbs[c % 2], d_xcbs[c % 2]
                pr, dpr = nb()
                pi_, dpi = nb()
                mmg([(pr[:, 0:N_], wr[:, c, :], xcb[:, 0:N_], True, True)], [d_lw, d_xcb], [dpr])
                mmg([(pi_[:, 0:N_], wi[:, c, :], xcb[:, 0:N_], True, True)], [d_lw, d_xcb], [dpi])
                cg_tile(bj, c)
                ta, dta = T3()
                tc_, dtc = T3()
                td, dtd = T3()
                act(ta[:, 0:N_], pr[:, 0:N_], AF.Tanh, [dpr, d_lw], [dta], scale=0.5, bias=pvd[:, c:c + 1])
                act(tc_[:, 0:N_], pi_[:, 0:N_], AF.Tanh, [dpi, d_lw], [dtc], scale=0.5, bias=pvd[:, 4 + c:5 + c])
                act(td[:, 0:N_], ta[:, 0:N_], AF.Exp, [dta, d_lw], [dtd], scale=pvd[:, 12 + c:13 + c], bias=pvd[:, 12 + c:13 + c])
                act(ta[:, 0:N_], ta[:, 0:N_], AF.Exp, [dta, d_lw], [dta], scale=pvd[:, 8 + c:9 + c], bias=pvd[:, 8 + c:9 + c])
                act(td[:, 0:N_], td[:, 0:N_], AF.Ln, [dtd], [dtd], scale=-1.0, bias=1.0)
                act(td[:, 0:N_], td[:, 0:N_], AF.Exp, [dtd], [dtd], scale=0.5)
                stt(tc_[:, 0:N_], tc_[:, 0:N_], 1.0, xc[:, 0:N_], ALU.add, ALU.mult, [dtc, d_xc], [dtc])
                stt(tc_[:, 0:N_], tc_[:, 0:N_], 0.5, td[:, 0:N_], ALU.mult, ALU.mult, [dtc, dtd], [dtc])
                P.op(DVE, (lambda e, o=td[:, 0:N_], a=ta[:, 0:N_], b=tc_[:, 0:N_], i0=hst[seg_][:, c:c + 1]:
                           e.tensor_tensor_scan(out=o, data0=a, data1=b, initial=i0, op0=ALU.mult, op1=ALU.add)),
                     [dta, dtc, d_hst[seg_]], [dtd])
                cp(DVE, hst[seg_][:, c:c + 1], td[:, N_ - 1:N_], [dtd], [d_hst[seg_]])
                tt(DVE, obs[bj % 2][:, c, 0:N_], td[:, 0:N_], sg[:, 0:N_], ALU.mult, [dtd, dsg], [d_obs[bj % 2]])
                if c + 1 < 4:
                    frs[c + 1] = lru_front(bj, c + 1, lru_z(bj, c + 1))
                if c == 3 and (bj == 7 or seg_ == 1):
                    out_dma(lruh[l, seg_], hst[seg_][:], [d_hst[seg_]])
                    out_dma(lruc[l, seg_], carry[seg_][:], [d_carry[seg_]])

            def mq_tiles(bj):
                _, N_, _ = BLOCKS[bj]
                for c in range(2):
                    pm, dpm = zmm_b(("W3", l, 20 + c), bj)
                    act(mq[:, c, 0:N_], pm[:, 0:N_], AF.Copy, [dpm], [d_mq])

            def mem_pair(bj, hp):
                _, N_, seg_ = BLOCKS[bj]
                c = hp
                om_, dom_ = oms[bj % 2], d_oms[bj % 2]
                sbs = []
                for hh in range(2):
                    p0 = hh * 64
                    for mt in range(2):
                        sb_, dsb = nb()
                        mmg([(sb_[:, 0:N_], mkT[seg_][p0:p0 + 64, c, mt * 128:(mt + 1) * 128], mq[p0:p0 + 64, c, 0:N_], True, True)],
                            [d_mem, d_mq], [dsb])
                        sbs.append((sb_, dsb))
                for hh in range(2):
                    h = 2 * hp + hh
                    po, dpo = banks[6 + hh], dbank[6 + hh]
                    for mt in range(2):
                        sb_, dsb = sbs[2 * hh + mt]
                        pt_, dpt = PT3[2 * hh + mt], d_PT3[2 * hh + mt]
                        act(pt_[:, 0:N_], sb_[:, 0:N_], AF.Exp, [dsb], [dpt], scale=0.125)
                        mmg([(po[0:128, 0:N_], mvx[seg_][:, mt, h, :], pt_[:, 0:N_], mt == 0, mt == 1)], [d_mem, dpt], [dpo])
                rss = []
                for hh in range(2):
                    po, dpo = banks[6 + hh], dbank[6 + hh]
                    rs3, d_rs3 = T3()
                    P.op(DVE, (lambda e, o=rs3[64:128, 0:N_], i_=po[64:128, 0:N_]: e.reciprocal(o, i_)), [dpo], [d_rs3])
                    rss.append((rs3, d_rs3))
                for hh in range(2):
                    po, dpo = banks[6 + hh], dbank[6 + hh]
                    rs3, d_rs3 = rss[hh]
                    tt(DVE, om_[hh * 64:hh * 64 + 64, c, 0:N_], po[0:64, 0:N_], rs3[64:128, 0:N_], ALU.mult, [dpo, d_rs3], [dom_])

            def spatial_part(bj):
                n0, N, seg = BLOCKS[bj]
                ntile = (N + 127) // 128
                oa, d_oa = oas[bj % 2], d_oas[bj % 2]
                for g in range(4):
                    bk, db = nb()
                    mms = []
                    for t_ in range(ntile):
                        m = min(128, N - t_ * 128)
                        mms.append((bk[:, t_ * 128:t_ * 128 + m], vb[0:m, t_, g * 128:(g + 1) * 128], wst[0:m, g, 0:m], True, False))
                        mms.append((bk[:, t_ * 128:t_ * 128 + m], ones_bf[0:1, 0:128], bsrow[0:1, g * 128:g * 128 + m], False, True))
                    mmg(mms, [d_vb, d_lw, d_const], [db])
                    tt(DVE, oa[:, g, 0:N], bk[:, 0:N], uag[:, g, 0:N], ALU.mult, [db, d_uag], [d_oa])

            def v_rest_spatial(bj, part=None):
                n0, N, seg = BLOCKS[bj]
                ntile = (N + 127) // 128
                oa, d_oa = oas[bj % 2], d_oas[bj % 2]
                if part == 1:
                    return spatial_part(bj)
                mt_ = [min(128, N - t_ * 128) for t_ in range(ntile)]
                mm_ = mt_[0]
                nt = ntile
                tsc(DVE, stt_[0:mm_, 8:8 + nt], stt_[0:mm_, 0:nt], 1.0 / 512.0, None, ALU.mult, None, [d_stat], [d_stat])
                tt(DVE, stt_[0:mm_, 12:12 + nt], stt_[0:mm_, 8:8 + nt], stt_[0:mm_, 8:8 + nt], ALU.mult, [d_stat], [d_stat])
                stt(stt_[0:mm_, 12:12 + nt], stt_[0:mm_, 4:4 + nt], 1.0 / 512.0, stt_[0:mm_, 12:12 + nt], ALU.mult, ALU.subtract,
                    [d_stat], [d_stat])
                tsc(DVE, stt_[0:mm_, 12:12 + nt], stt_[0:mm_, 12:12 + nt], EPS, None, ALU.add, None, [d_stat], [d_stat])
                act(stt_[0:mm_, 12:12 + nt], stt_[0:mm_, 12:12 + nt], AF.Ln, [d_stat], [d_stat])
                act(stt_[0:mm_, 12:12 + nt], stt_[0:mm_, 12:12 + nt], AF.Exp, [d_stat], [d_stat], scale=-0.5)
                stt(stt_[0:mm_, 16:16 + nt], stt_[0:mm_, 8:8 + nt], -1.0, stt_[0:mm_, 12:12 + nt], ALU.mult, ALU.mult, [d_stat], [d_stat])
                d_gvt = [Dep() for _ in range(ntile)]
                for t_ in range(ntile):
                    m = mt_[t_]
                    act(gv[0:m, t_, :], gv[0:m, t_, :], AF.Identity, [d_gv, d_stat], [d_gvt[t_]],
                        scale=stt_[0:m, 12 + t_:13 + t_], bias=stt_[0:m, 16 + t_:17 + t_])
                for t_ in range(ntile):
                    m = mt_[t_]
                    tt(DVE, gv[0:m, t_, :], gv[0:m, t_, :], gbc[0:m, :], ALU.mult, [d_gvt[t_], d_lw], [d_gvt[t_]])
                for t_ in range(ntile):
                    m = mt_[t_]
                    d_gv_t = d_gvt[t_]
                    if seg == 1:
                        tt(DVE, gv[0:m, t_, :], gv[0:m, t_, :], bbc[0:m, :], ALU.add, [d_gv_t, d_lw], [d_gv_t, d_gv])
                        out_dma(sgv[l], gv[0:m, t_, :], [d_gv_t, d_gv])
                        act(vb[0:m, t_, :], gv[0:m, t_, :], AF.Copy, [d_gv_t], [d_vb])
                    else:
                        tt(DVE, vb[0:m, t_, :], gv[0:m, t_, :], bbc[0:m, :], ALU.add, [d_gv_t, d_lw], [d_vb, d_gv])
                if part == 0:
                    return
                spatial_part(bj)

            p3_load(0)
            v_pass1(0)
            au_ag(0)
            stl0 = {"zs": {}, "frs": {}}
            for c in range(4):
                lru_step(0, c, stl0)
            mq_tiles(0)
            mem_pair(0, 0)
            mem_pair(0, 1)
            v_rest_spatial(0)

            pending_ln = []
            late_cast = []
            d_yblk = Dep("yblk")
            for bi, (n0, N, seg) in enumerate(BLOCKS):
                ntile = (N + 127) // 128
                xr_, dxr = xres[0], d_xres[0]
                nxt = bi + 1 if bi + 1 < len(BLOCKS) else None
                if nxt is not None:
                    p3_load(nxt)
                ob, d_ob, ocb, d_ocb, om, d_om = obs[bi % 2], d_obs[bi % 2], ocbs[bi % 2], d_ocbs[bi % 2], oms[bi % 2], d_oms[bi % 2]
                oa, d_oa = oas[bi % 2], d_oas[bi % 2]
                if debug:
                    srcs = [(oa, d_oa, 4, 0), (ob, d_ob, 4, 4), (ocb, d_ocb, 4, 8), (om, d_om, 2, 12)]
                    for (tb_, dtb, nch, k0) in srcs:
                        for c in range(nch):
                            t32, dt32 = T3()
                            cp(DVE, t32[:, 0:N], tb_[:, c, 0:N], [dtb], [dt32])
                            out_dma(dbg["br"][l][:, k0 + c, n0:n0 + N], t32[:, 0:N], [dt32])
                branches = [(oa, d_oa, 0, 4), (ob, d_ob, 4, 4), (ocb, d_ocb, 8, 4), (om, d_om, 12, 2)]
                stl = {"zs": {}, "frs": {}}
                for j in range(8):
                    wb_, dwb, rib = ringB.get(("WBR", l, j))
                    ma, dma_ = macc[j % 2], d_macc[j % 2]
                    for br in range(4):
                        pgt, dpgt = zmm_b(("W3", l, 22 + j * 4 + br), bi)
                        tg, dtg = T3()
                        act(tg[:, 0:N], pgt[:, 0:N], AF.Tanh, [dpgt], [dtg], scale=0.5)
                        src, dsrc, k0, nk = branches[br]
                        py, dpy = nb()
                        mmg([(py[:, 0:N], wb_[:, (k0 + k) * 128:(k0 + k + 1) * 128], src[:, k, 0:N], k == 0, k == nk - 1)
                             for k in range(nk)], [dwb, dsrc], [dpy])
                        if br == 0:
                            stt(ma[:, 0:N], tg[:, 0:N], 1.0, py[:, 0:N], ALU.add, ALU.mult, [dtg, dpy], [dma_])
                        else:
                            stt(tg[:, 0:N], tg[:, 0:N], 1.0, py[:, 0:N], ALU.add, ALU.mult, [dtg, dpy], [dtg])
                            tt(POOL, ma[:, 0:N], ma[:, 0:N], tg[:, 0:N], ALU.add, [dma_, dtg], [dma_])
                    ringB.rel(rib)
                    act(mg[:, j, 0:N], ma[:, 0:N], AF.Copy, [dma_], [d_mg])
                    if debug:
                        out_dma(dbg["mg"][l][:, j, n0:n0 + N], ma[:, 0:N], [dma_])
                    if nxt is not None:
                        if j < 4:
                            lru_step(nxt, j, stl)
                        elif j == 4:
                            mq_tiles(nxt)
                            mem_pair(nxt, 0)
                        elif j == 5:
                            mem_pair(nxt, 1)
                            v_pass1(nxt)
                        elif j == 6:
                            au_ag(nxt)
                            v_rest_spatial(nxt, part=0)
                        elif j == 7:
                            v_rest_spatial(nxt, part=1)
                    if pending_ln and j < 3:
                        pending_ln.pop(0)()
                    if l == 0 and j == 3:
                        emit_conv_some(11 if bi + 1 < len(BLOCKS) else 1000, conv_order_l1)
                    if j == 5:
                        p3_load_res(bi)
                    if j == 6 and late_cast:
                        late_cast.pop(0)()
                act(xr_[:, :, 0:N], xr_[:, :, 0:N], AF.Copy, [dxr], [dxr], scale=ALPHA)
                s1, ds1 = banks[6], dbank[6]
                s2b, ds2b = banks[7], dbank[7]
                pend = None
                for j in range(8):
                    wo_, dwo, rio = ringA.get(("WOUT", l, j))
                    pyo, dpyo = nb()
                    mmg([(pyo[:, 0:N], wo_[:, k * 128:(k + 1) * 128], mg[:, k, 0:N], k == 0, k == 7) for k in range(8)],
                        [dwo, d_mg], [dpyo])
                    ringA.rel(rio)
                    stt(xr_[:, j, 0:N], pyo[:, 0:N], 0.5, xr_[:, j, 0:N], ALU.mult, ALU.add, [dpyo, dxr], [dxr])
                    ta_, dta_ = tb16[0], d_tb16[0]
                    tq_, dtq_ = tb16[1], d_tb16[1]
                    if pend is not None:
                        pend()
                    act(ta_[:, 0:N], xr_[:, j, 0:N], AF.Copy, [dxr], [dta_])
                    act(tq_[:, 0:N], xr_[:, j, 0:N], AF.Square, [dxr], [dtq_])

                    def stats(j=j, ta_=ta_, dta_=dta_, tq_=tq_, dtq_=dtq_):
                        mmg([(s1[:, 0:N], ones_bf[:], ta_[:, 0:N], j == 0, j == 7)], [d_const, dta_], [ds1])
                        mmg([(s2b[:, 0:N], ones_bf[:], tq_[:, 0:N], j == 0, j == 7)], [d_const, dtq_], [ds2b])
                    pend = stats
                pend()
                if debug:
                    out_dma(dbg["t"][l][:, :, n0:n0 + N], xr_[:, :, 0:N], [dxr])
                tm, dtm = tmn[0], d_tmn[0]
                tn, dtn = tmn[1], d_tmn[1]
                tsc(DVE, tm[:, 0:N], s1[:, 0:N], 1.0 / 1024.0, None, ALU.mult, None, [ds1], [dtm])
                tt(DVE, tn[:, 0:N], tm[:, 0:N], tm[:, 0:N], ALU.mult, [dtm], [dtn])
                stt(tn[:, 0:N], s2b[:, 0:N], 1.0 / 1024.0, tn[:, 0:N], ALU.mult, ALU.subtract, [ds2b, dtn], [dtn])
                tsc(DVE, tn[:, 0:N], tn[:, 0:N], EPS, None, ALU.add, None, [dtn], [dtn])
                act(tn[:, 0:N], tn[:, 0:N], AF.Ln, [dtn], [dtn])
                act(tn[:, 0:N], tn[:, 0:N], AF.Exp, [dtn], [dtn], scale=-0.5)
                stt(tm[:, 0:N], tm[:, 0:N], -1.0, tn[:, 0:N], ALU.mult, ALU.mult, [dtm, dtn], [dtm])

                def ln_b1(N=N, xr_=xr_, dxr=dxr, tn=tn, dtn=dtn):
                    tt(DVE, xr_[:, :, 0:N], xr_[:, :, 0:N], tn[:, 0:N].unsqueeze(1).to_broadcast([128, 8, N]), ALU.mult,
                       [dxr, dtn], [dxr])

                def ln_b2(N=N, xr_=xr_, dxr=dxr, tm=tm, dtm=dtm):
                    tt(DVE, xr_[:, :, 0:N], xr_[:, :, 0:N], tm[:, 0:N].unsqueeze(1).to_broadcast([128, 8, N]), ALU.add,
                       [dxr, dtm], [dxr])

                def ln_c(N=N, n0=n0, xr_=xr_, dxr=dxr):
                    for j in range(8):
                        tsc(POOL, xr_[:, j, 0:N], xr_[:, j, 0:N], pv[:, 35 + j:36 + j], pv[:, 43 + j:44 + j], ALU.mult, ALU.add,
                            [dxr, d_lw], [dxr])
                    s_ = s_out[oi["i"] % 4]
                    oi["i"] += 1
                    P.dma(POOL, ydst[:, :, n0:n0 + N], xr_[:, :, 0:N], s_, reads=[dxr], writes=[d_yblk])

                def cast_late(N=N, n0=n0):
                    P.dma(POOL, xbf_d[l + 1][:, :, n0:n0 + N], xres_d[:, :, n0:n0 + N], s_out[oi["i"] % 4], reads=[d_yblk])
                    oi["i"] += 1

                if nxt is not None:
                    pending_ln.extend([ln_b1, ln_b2, ln_c])
                    if l + 1 < NL:
                        late_cast.append(cast_late)
                else:
                    ln_b1()
                    ln_b2()
                    ln_c()
                    if l + 1 < NL:
                        late_cast.append(cast_late)
                    while late_cast:
                        late_cast.pop(0)()
            P.barrier()

    if debug:
        out_dma(dbg["x1"], xres_d, [])
    for s in s_out:
        if P.dsem_val[s] > 0:
            P.q[POOL].append((lambda s=s, v=P.dsem_val[s]: nc.gpsimd.wait_ge(s, v)))
    assert not conv_pending(), conv_pending()
    if sched is None:
        st.close()
        return [ringA.rec, ringB.rec]
    assert ringA.consumed == len(ringA.sched) and ringB.consumed == len(ringB.sched)
    P.run()
    st.close()
    return nc


def _fm(a):
    T, F = a.shape
    return np.ascontiguousarray(a.T.reshape(F // 128, 128, T).transpose(1, 0, 2))


def _wtile(w):
    n = w.shape[1]
    t = np.zeros((128, 8, 128), np.float32)
    t[:, :, :n] = w.reshape(8, 128, n).transpose(1, 0, 2)
    return t.reshape(128, 1024)


def _prep_shared(inp):
    f32 = np.float32
    w_in = inp["w_in"]
    sw = np.concatenate([np.arange(16, 32), np.arange(0, 16)])
    W1 = np.zeros((NL, 5, 128, 1024), f32)
    WAV = np.zeros((NL, 128, 4096), f32)
    W3 = np.zeros((NL, N_W3, 128, 1024), f32)
    WBR = np.zeros((NL, 8, 128, 1792), f32)
    WOUT = np.zeros((NL, 8, 128, 1024), f32)
    WUQ = np.zeros((NL, 128, 1536), f32)
    WUQS = np.zeros((NL, 128, 1536), f32)
    WUKV = np.zeros((NL, 128, 1024), f32)
    MWK = np.zeros((NL, 128, 2048), f32)
    MWV = np.zeros((NL, 128, 2048), f32)
    WR = np.zeros((NL, 128, 512), f32)
    WI = np.zeros((NL, 128, 512), f32)
    WST = np.zeros((NL, 128, 512), f32)
    PV = np.zeros((NL, 128, NPV), f32)
    GLN = np.zeros((NL, 2, 512), f32)
    BS = np.zeros((NL, 1, 512), f32)
    cols3 = w3_tile_cols()
    for l in range(NL):
        w = w_in[l]
        W1[l, 0] = _wtile(w[:, 2560:2688])
        W1[l, 1] = _wtile(w[:, 2688:2816])
        W1[l, 2] = _wtile(w[:, 2816:2944])
        kr = np.zeros((1024, 128), f32)
        kr[:, 64:96] = w[:, 2944:2976]
        W1[l, 3] = _wtile(kr)
        krs = np.zeros((1024, 128), f32)
        krs[:, 64:96] = w[:, 2944 + sw]
        W1[l, 4] = _wtile(krs)
        WAV[l] = w[:, 512:1024].reshape(8, 128, 512).transpose(1, 0, 2).reshape(128, 4096)
        for i, cc in enumerate(cols3):
            W3[l, i] = _wtile(w[:, cc])
        wbr = inp["w_br"][l]
        for j in range(8):
            WBR[l, j] = wbr[:, j * 128:(j + 1) * 128].reshape(14, 128, 128).transpose(1, 0, 2).reshape(128, 1792)
            WOUT[l, j] = _wtile(inp["w_out"][l][:, j * 128:(j + 1) * 128])
        wuq = inp["mla_w_uq"][l]
        WUQ[l] = wuq.reshape(2, 128, 768).transpose(1, 0, 2).reshape(128, 1536)
        wuqs = np.zeros_like(wuq)
        for h in range(8):
            wuqs[:, h * 96 + 64:h * 96 + 96] = wuq[:, h * 96 + 64 + sw]
        WUQS[l] = wuqs.reshape(2, 128, 768).transpose(1, 0, 2).reshape(128, 1536)
        WUKV[l] = inp["mla_w_ukv"][l]
        MWK[l] = inp["mem_w_k"][l].reshape(8, 128, 256).transpose(1, 0, 2).reshape(128, 2048)
        MWV[l] = inp["mem_w_v"][l].reshape(8, 128, 256).transpose(1, 0, 2).reshape(128, 2048)
        for c in range(4):
            for hh in range(2):
                hb = 2 * c + hh
                WR[l, hh * 64:(hh + 1) * 64, c * 128 + hh * 64:c * 128 + (hh + 1) * 64] = inp["lru_w_r"][l][hb]
                WI[l, hh * 64:(hh + 1) * 64, c * 128 + hh * 64:c * 128 + (hh + 1) * 64] = inp["lru_w_i"][l][hb]
        WST[l] = inp["gmlp_ws"][l].transpose(2, 0, 1).reshape(128, 512)
        fmv = lambda v: v.reshape(-1, 128).T
        PV[l, :, 0:16] = inp["lru_conv_w"][l].reshape(4, 4, 128).transpose(2, 1, 0).reshape(128, 16)
        PV[l, :, 16:20] = fmv(inp["lru_conv_b"][l])
        PV[l, :, 20:24] = fmv(inp["lru_b_r"][l])
        PV[l, :, 24:28] = fmv(inp["lru_b_i"][l])
        PV[l, :, 28:32] = fmv(inp["lru_lambda"][l])
        PV[l, :, 32:34] = fmv(inp["mla_q_norm"][l])
        PV[l, :, 34:35] = fmv(inp["mla_kv_norm"][l])
        PV[l, :, 35:43] = fmv(inp["ln_g"][l])
        PV[l, :, 43:51] = fmv(inp["ln_b"][l])
        GLN[l, 0] = inp["gmlp_ln_g"][l]
        GLN[l, 1] = inp["gmlp_ln_b"][l]
        BS[l, 0] = inp["gmlp_bs"][l].reshape(512)
    pos = np.concatenate([np.arange(TP), PAST + np.arange(TS)]).astype(np.float32)
    freq = (np.float32(10000.0) ** (-np.arange(16, dtype=np.float32) / np.float32(16))).astype(np.float32)
    ang = pos[None, :] * freq[:, None]
    cosT = np.concatenate([np.cos(ang), np.cos(ang)], 0).astype(f32)
    sinT = np.concatenate([-np.sin(ang), np.sin(ang)], 0).astype(f32)
    maskT = (np.arange(128)[:, None] <= np.arange(128)[None, :]).astype(f32)
    return dict(W1=W1, WAV=WAV, W3=W3, WBR=WBR, WOUT=WOUT, WUQ=WUQ, WUQS=WUQS, WUKV=WUKV, MWK=MWK, MWV=MWV,
                WR=WR, WI=WI, WST=WST, PV=PV, GLN=GLN, BS=BS, cosT=cosT, sinT=sinT, maskT=maskT)


def _prep_core(inp, c):
    b = c % 4
    f32 = np.float32
    xtok = np.concatenate([inp["x_prompt"][b], inp["x_sample"][c]], 0)
    m = {"xT": _fm(xtok)}
    m["memT"] = _fm(inp["mem_prompt"][b])
    cmk = inp["cache_mem_k"][:, c].reshape(NL, 256, 256)
    m["cmkT"] = np.ascontiguousarray(cmk.transpose(0, 2, 1).reshape(NL, 2, 128, 256).transpose(0, 2, 1, 3))
    cmvv = inp["cache_mem_v"][:, c].reshape(NL, 256, 256)
    m["cmv"] = np.ascontiguousarray(cmvv.reshape(NL, 2, 128, 256).transpose(0, 2, 1, 3))
    m["cckvT"] = np.ascontiguousarray(inp["cache_mla_ckv"][:, c].transpose(0, 2, 1))
    m["ckrT"] = np.ascontiguousarray(inp["cache_mla_krope"][:, c].transpose(0, 2, 1))
    m["shin"] = np.ascontiguousarray(inp["state_lru_h"][:, c].reshape(NL, 4, 128).transpose(0, 2, 1))
    m["scin"] = np.ascontiguousarray(inp["state_lru_conv"][:, c].reshape(NL, 3, 4, 128).transpose(0, 3, 2, 1))
    return {k: np.ascontiguousarray(v, dtype=f32) for k, v in m.items()}


_CACHE = {}


def _run(inp, debug=False):
    key = "dbg" if debug else "prog"
    if key not in _CACHE:
        rec = build_program(debug=debug, sched=None)
        _CACHE[key] = build_program(debug=debug, sched=rec)
    nc = _CACHE[key]
    shared = _prep_shared(inp)
    in_maps = []
    for c in range(8):
        m = dict(shared)
        m.update(_prep_core(inp, c))
        in_maps.append(m)
    res = run_bass_kernel_spmd(nc, in_maps, core_ids=list(range(8)))
    return res.results


def _tok(a):
    p, c, t = a.shape
    return np.ascontiguousarray(a.transpose(2, 1, 0).reshape(t, c * p))


def kernel(**inputs):
    inp = {k: np.asarray(v, dtype=np.float32) for k, v in inputs.items()}
    r = _run(inp)
    f32 = np.float32
    y_prompt = np.stack([_tok(r[b]["yT"][:, :, :TP]) for b in range(4)]).astype(f32)
    y_sample = np.stack([_tok(r[c]["yT"][:, :, TP:]) for c in range(8)]).astype(f32)
    p_ckv = np.stack([np.stack([r[b]["ckvo"][l][:, :TP].T for b in range(4)]) for l in range(NL)]).astype(f32)
    p_kr = np.stack([np.stack([r[b]["kro"][l][:, :TP].T for b in range(4)]) for l in range(NL)]).astype(f32)
    p_mk = np.stack([np.stack([_tok(r[b]["pmk"][l]).reshape(256, 4, 64) for b in range(4)]) for l in range(NL)]).astype(f32)
    p_mv = np.stack([np.stack([r[b]["pmv"][l].transpose(1, 0, 2).reshape(256, 4, 64) for b in range(4)])
                     for l in range(NL)]).astype(f32)
    p_h = np.stack([np.stack([r[b]["lruh"][l, 0].T.reshape(512) for b in range(4)]) for l in range(NL)]).astype(f32)
    p_conv = np.stack([np.stack([r[b]["lruc"][l, 0].transpose(2, 1, 0).reshape(3, 512) for b in range(4)])
                       for l in range(NL)]).astype(f32)
    s_ckv = np.stack([np.stack([r[c]["ckvo"][l][:, TP:].T for c in range(8)]) for l in range(NL)]).astype(f32)
    s_kr = np.stack([np.stack([r[c]["kro"][l][:, TP:].T for c in range(8)]) for l in range(NL)]).astype(f32)
    s_h = np.stack([np.stack([r[c]["lruh"][l, 1].T.reshape(512) for c in range(8)]) for l in range(NL)]).astype(f32)
    s_conv = np.stack([np.stack([r[c]["lruc"][l, 1].transpose(2, 1, 0).reshape(3, 512) for c in range(8)])
                       for l in range(NL)]).astype(f32)
    s_v = np.stack([np.stack([r[c]["sgv"][l] for c in range(8)]) for l in range(NL)]).astype(f32)
    return (y_prompt, y_sample, p_ckv, p_kr, p_mk, p_mv, p_h, p_conv, s_ckv, s_kr, s_h, s_conv, s_v)
```

```python
import contextlib
import numpy as np
import concourse.bass as bass
import concourse.mybir as mybir
from concourse.bass_utils import run_bass_kernel_spmd

F32 = mybir.dt.float32
BF16 = mybir.dt.bfloat16
AF = mybir.ActivationFunctionType
ALU = mybir.AluOpType
PE, ACT, DVE, POOL, SP = "pe", "act", "dve", "pool", "sp"

NL = 2
D = 1024
TP = 4096
TS = 16
TT = TP + TS
PAST = 1024
KC = TP + PAST + TS
NPV = 51
D_IN = 7840
SM_SCALE = 96.0 ** -0.5
ALPHA = (2.0 * NL) ** 0.25
EPS = 1e-6
GELU_C = 0.7978845608028654
BLOCKS = [(i * 512, 512, 0) for i in range(8)] + [(TP, TS, 1)]
NSLOT_A = 6
NSLOT_B = 2
SEM_LIMIT = 30000


class Dep:
    __slots__ = ("w", "r", "name")

    def __init__(self, name=""):
        self.w = None
        self.r = {}
        self.name = name


class Prog:
    def __init__(self, nc):
        self.nc = nc
        self.q = {e: [] for e in (PE, ACT, DVE, POOL, SP)}
        self.eng = {PE: nc.tensor, ACT: nc.scalar, DVE: nc.vector, POOL: nc.gpsimd, SP: nc.sync}
        self.csem = {}
        self.cnt = {}
        self.nsem = 0
        for e in (PE, ACT, DVE, POOL):
            self._new_csem(e)
        self.seen = {e: {} for e in self.q}
        self.dsem_val = {}
        self.snap = {}

    def _new_csem(self, e):
        self.nsem += 1
        self.csem[e] = self.nc.alloc_semaphore("c%s%d" % (e, self.nsem))
        self.cnt[e] = 0

    def dma_sem(self, name):
        s = self.nc.alloc_semaphore(name)
        self.dsem_val[s] = 0
        return s

    def _need(self, engine, ev, waits):
        if ev is None:
            return
        sem, val, src = ev
        if src == engine and engine == PE:
            return
        if src == "dma":
            val = self.dsem_val[sem]
        if self.seen[engine].get(sem, 0) >= val:
            return
        if waits.get(sem, 0) < val:
            waits[sem] = val

    def _waits(self, engine, reads, writes):
        waits = {}
        for d in reads:
            self._need(engine, d.w, waits)
        for d in writes:
            self._need(engine, d.w, waits)
            for e2, ev in d.r.items():
                if e2 == engine:
                    continue
                self._need(engine, ev, waits)
        q = self.q[engine]
        eng = self.eng[engine]
        for sem, val in waits.items():
            q.append((lambda eng=eng, sem=sem, val=val: eng.wait_ge(sem, val)))
            self.seen[engine][sem] = val
        for sem, val in waits.items():
            for s2, v2 in self.snap.get((sem, val), {}).items():
                if self.seen[engine].get(s2, 0) < v2:
                    self.seen[engine][s2] = v2

    def op(self, engine, fn, reads=(), writes=()):
        self._waits(engine, reads, writes)
        if self.cnt[engine] >= SEM_LIMIT:
            self._new_csem(engine)
        self.cnt[engine] += 1
        val = self.cnt[engine]
        sem = self.csem[engine]
        eng = self.eng[engine]
        self.q[engine].append((lambda eng=eng, fn=fn, sem=sem: fn(eng).then_inc(sem, 1)))
        ev = (sem, val, engine)
        self.snap[(sem, val)] = dict(self.seen[engine])
        for d in reads:
            d.r[engine] = ev
        for d in writes:
            d.w = ev
            d.r = {}
        return ev

    def dma(self, engine, out, in_, sem, reads=(), writes=()):
        self._waits(engine, reads, writes)
        self.dsem_val[sem] += 16
        val = self.dsem_val[sem]
        eng = self.eng[engine]
        self.q[engine].append(
            (lambda eng=eng, out=out, in_=in_, sem=sem: eng.dma_start(out=out, in_=in_).then_inc(sem, 16)))
        ev = (sem, val, "dma")
        for d in reads:
            d.r["dma:%d" % id(sem)] = ev
        for d in writes:
            d.w = ev
            d.r = {}
        return ev

    def barrier(self):
        targets = [(self.csem[e], self.cnt[e]) for e in (PE, ACT, DVE, POOL) if self.cnt[e] > 0]
        targets += [(s, v) for s, v in self.dsem_val.items() if v > 0]
        for e in (PE, ACT, DVE, POOL, SP):
            eng = self.eng[e]
            for sem, val in targets:
                if self.seen[e].get(sem, 0) >= val:
                    continue
                self.q[e].append((lambda eng=eng, sem=sem, val=val: eng.wait_ge(sem, val)))
                self.seen[e][sem] = val

    def run(self):
        nc = self.nc
        with nc.Block() as block:
            @block.tensor
            def _(e):
                for f in self.q[PE]:
                    f()

            @block.scalar
            def _(e):
                for f in self.q[ACT]:
                    f()

            @block.vector
            def _(e):
                for f in self.q[DVE]:
                    f()

            @block.gpsimd
            def _(e):
                for f in self.q[POOL]:
                    f()

            @block.sync
            def _(e):
                for f in self.q[SP]:
                    f()


def w3_tile_cols():
    tiles = []
    for c in range(4):
        tiles.append(np.arange(0 + c * 128, 0 + (c + 1) * 128))
        tiles.append(np.arange(1024 + c * 128, 1024 + (c + 1) * 128))
    for c in range(4):
        tiles.append(np.arange(2048 + c * 128, 2048 + (c + 1) * 128))
        tiles.append(np.arange(1536 + c * 128, 1536 + (c + 1) * 128))
    for c in range(4):
        tiles.append(np.arange(2976 + c * 128, 2976 + (c + 1) * 128))
    for c in range(2):
        tiles.append(np.arange(3488 + c * 128, 3488 + (c + 1) * 128))
    for j in range(8):
        for br in range(4):
            tiles.append(np.arange(3744 + br * 1024 + j * 128, 3744 + br * 1024 + (j + 1) * 128))
    return tiles


N_W3 = 54


def build_program(debug=False, sched=None):
    nc = bass.Bass("TRN2", target_bir_lowering=False)
    P = Prog(nc)

    def din(name, shape):
        return nc.dram_tensor(name, list(shape), F32, kind="ExternalInput").ap()

    def dout(name, shape):
        return nc.dram_tensor(name, list(shape), F32, kind="ExternalOutput").ap()

    def dscr(name, shape, dt):
        return nc.dram_tensor(name, list(shape), dt).ap()

    xT = din("xT", [128, 8, TT])
    cosT = din("cosT", [32, TT])
    sinT = din("sinT", [32, TT])
    maskT = din("maskT", [128, 128])
    memT = din("memT", [128, 8, 256])
    cmkT = din("cmkT", [NL, 128, 2, 256])
    cmv = din("cmv", [NL, 128, 2, 256])
    cckvT = din("cckvT", [NL, 128, PAST])
    ckrT = din("ckrT", [NL, 32, PAST])
    shin = din("shin", [NL, 128, 4])
    scin = din("scin", [NL, 128, 4, 3])
    W1 = din("W1", [NL, 5, 128, 1024])
    WAV = din("WAV", [NL, 128, 4096])
    W3 = din("W3", [NL, N_W3, 128, 1024])
    WBR = din("WBR", [NL, 8, 128, 1792])
    WOUT = din("WOUT", [NL, 8, 128, 1024])
    WUQ = din("WUQ", [NL, 128, 1536])
    WUQS = din("WUQS", [NL, 128, 1536])
    WUKV = din("WUKV", [NL, 128, 1024])
    MWK = din("MWK", [NL, 128, 2048])
    MWV = din("MWV", [NL, 128, 2048])
    WR = din("WR", [NL, 128, 512])
    WI = din("WI", [NL, 128, 512])
    WST = din("WST", [NL, 128, 512])
    PVin = din("PV", [NL, 128, NPV])
    GLN = din("GLN", [NL, 2, 512])
    BSin = din("BS", [NL, 1, 512])

    yT = dout("yT", [128, 8, TT])
    ckvo = dout("ckvo", [NL, 128, TT])
    kro = dout("kro", [NL, 32, TT])
    pmk = dout("pmk", [NL, 128, 2, 256])
    pmv = dout("pmv", [NL, 128, 2, 256])
    lruh = dout("lruh", [NL, 2, 128, 4])
    lruc = dout("lruc", [NL, 2, 128, 4, 3])
    sgv = dout("sgv", [NL, 16, 512])
    dbg = {}
    if debug:
        dbg["ocraw"] = dout("dbg_ocraw", [NL, 128, 4, TT])
        dbg["br"] = dout("dbg_br", [NL, 128, 14, TT])
        dbg["x1"] = dout("dbg_x1", [128, 8, TT])
        dbg["mg"] = dout("dbg_mg", [NL, 128, 8, TT])
        dbg["t"] = dout("dbg_t", [NL, 128, 8, TT])

    xres_d = dscr("xres_d", [128, 8, TT], F32)
    xbf_d = [dscr("xbf0_d", [128, 8, TT], BF16), dscr("xbf1_d", [128, 8, TT], BF16)]
    W1b = dscr("W1b", [NL, 5, 128, 1024], BF16)
    WAVb = dscr("WAVb", [NL, 128, 4096], BF16)
    W3b = dscr("W3b", [NL, N_W3, 128, 1024], BF16)
    WBRb = dscr("WBRb", [NL, 8, 128, 1792], BF16)
    WOUTb = dscr("WOUTb", [NL, 8, 128, 1024], BF16)
    WUQb = dscr("WUQb", [NL, 128, 1536], BF16)
    WUQSb = dscr("WUQSb", [NL, 128, 1536], BF16)
    WUKVb = dscr("WUKVb", [NL, 128, 1024], BF16)
    MWKb = dscr("MWKb", [NL, 128, 2048], BF16)
    MWVb = dscr("MWVb", [NL, 128, 2048], BF16)
    WRb = dscr("WRb", [NL, 128, 512], BF16)
    WIb = dscr("WIb", [NL, 128, 512], BF16)

    st = contextlib.ExitStack()

    uid = {"i": 0}

    def S(name, shape, dt=F32, stack=None):
        uid["i"] += 1
        t = (stack or st).enter_context(nc.sbuf_tensor("%s_%d" % (name, uid["i"]), list(shape), dt))
        return t

    def act(out, in_, func, reads, writes, bias=None, scale=None, accum=None):
        kw = {}
        if bias is not None:
            kw["bias"] = bias
        if scale is not None:
            kw["scale"] = scale
        if accum is not None:
            kw["accum_out"] = accum
        P.op(ACT, lambda e: e.activation(out=out, in_=in_, func=func, **kw), reads, writes)

    def tt(eng, out, in0, in1, op, reads, writes):
        P.op(eng, lambda e: e.tensor_tensor(out=out, in0=in0, in1=in1, op=op), reads, writes)

    def tsc(eng, out, in0, s1, s2, op0, op1, reads, writes):
        if s2 is None:
            P.op(eng, lambda e: e.tensor_scalar(out=out, in0=in0, scalar1=s1, scalar2=None, op0=op0), reads, writes)
        else:
            P.op(eng, lambda e: e.tensor_scalar(out=out, in0=in0, scalar1=s1, scalar2=s2, op0=op0, op1=op1), reads, writes)

    def stt(out, in0, scalar, in1, op0, op1, reads, writes, accum=None):
        if accum is None:
            P.op(DVE, lambda e: e.scalar_tensor_tensor(out=out, in0=in0, scalar=scalar, in1=in1, op0=op0, op1=op1), reads, writes)
        else:
            P.op(DVE, lambda e: e.scalar_tensor_tensor(out=out, in0=in0, scalar=scalar, in1=in1, op0=op0, op1=op1,
                                                       accum_out=accum), reads, writes)

    def cp(eng, out, in_, reads, writes):
        P.op(eng, lambda e: e.tensor_copy(out, in_), reads, writes)

    def memset(eng, ap, val, writes):
        P.op(eng, lambda e: e.memset(ap, val), (), writes)

    def mmg(mms, reads, writes):
        def fn(e, mms=mms):
            r = None
            for (o, l, rh, s0, s1) in mms:
                r = e.matmul(o, l, rh, start=s0, stop=s1)
            return r
        P.op(PE, fn, reads, writes)

    banks = [st.enter_context(nc.psum_tensor("bank%d" % i, [128, 512], F32)) for i in range(8)]
    dbank = [Dep("bank%d" % i) for i in range(8)]
    rr = {"i": 0}

    def nb():
        i = rr["i"] % 6
        rr["i"] += 1
        return banks[i], dbank[i]

    ones_bf = S("ones_bf", [128, 128], BF16)
    ones32 = S("ones32", [128, 128])
    maskS = S("maskS", [128, 128])
    d_const = Dep("const")
    ocraw = S("ocraw", [128, 4, TT], BF16)
    d_ocraw = Dep("ocraw")
    wav = S("wav", [128, 8, 512], BF16)
    wuq = S("wuq", [128, 2, 768], BF16)
    wuqs = S("wuqs", [128, 2, 768], BF16)
    wukv = S("wukv", [128, 1024], BF16)
    wr = S("wr", [128, 4, 128], BF16)
    wi = S("wi", [128, 4, 128], BF16)
    wst = S("wst", [128, 4, 128], BF16)
    gbc = S("gbc", [128, 512])
    bbc = S("bbc", [128, 512])
    bsrow = S("bsrow", [1, 512], BF16)
    pv = S("pv", [128, NPV])
    pvd = S("pvd", [128, 16])
    mkT = [S("mkT%d" % s, [128, 2, 256], BF16) for s in range(2)]
    mvx = [S("mvx%d" % s, [128, 2, 4, 128], BF16) for s in range(2)]
    hst = [S("hst%d" % s, [128, 4]) for s in range(2)]
    carry = [S("carry%d" % s, [128, 4, 3]) for s in range(2)]
    d_lw = Dep("layer_weights")
    d_mem = Dep("memkv")
    d_hst = [Dep("hst0"), Dep("hst1")]
    d_carry = [Dep("carry0"), Dep("carry1")]

    s_misc = P.dma_sem("s_misc")
    s_misc_sp = P.dma_sem("s_misc_sp")
    s_xst = P.dma_sem("s_xst")
    s_x = [P.dma_sem("s_x%d" % i) for i in range(5)]
    s_out = [P.dma_sem("s_out%d" % i) for i in range(4)]
    oi = {"i": 0}

    def out_dma(dst, src, reads):
        s = s_out[oi["i"] % 4]
        oi["i"] += 1
        P.dma(POOL, dst, src, s, reads=reads)

    s_conv = {}
    d_conv = {}
    conv_lists = {}
    for l in range(NL):
        g0 = [(W1b[l, i], W1[l, i]) for i in range(5)]
        g0 += [(WAVb[l][:, i * 1024:(i + 1) * 1024], WAV[l][:, i * 1024:(i + 1) * 1024]) for i in range(4)]
        g0 += [(WUQb[l], WUQ[l]), (WUQSb[l], WUQS[l]), (WUKVb[l], WUKV[l]), (WRb[l], WR[l]), (WIb[l], WI[l]),
               (MWKb[l], MWK[l]), (MWVb[l], MWV[l])]
        g1 = [(W3b[l, i], W3[l, i]) for i in range(22)]
        g2 = [(W3b[l, i], W3[l, i]) for i in range(22, N_W3)]
        g2 += [(WBRb[l, j], WBR[l, j]) for j in range(8)]
        g2 += [(WOUTb[l, j], WOUT[l, j]) for j in range(8)]
        for gi, g in enumerate((g0, g1, g2)):
            s_conv[(l, gi)] = P.dma_sem("s_conv%d_%d" % (l, gi))
            d_conv[(l, gi)] = Dep("conv%d_%d" % (l, gi))
            conv_lists[(l, gi)] = list(g)

    def emit_conv(l, gi, n=None):
        lst = conv_lists[(l, gi)]
        k = len(lst) if n is None else min(n, len(lst))
        for _ in range(k):
            dst, src = lst.pop(0)
            P.dma(POOL, dst, src, s_conv[(l, gi)])
        sem = s_conv[(l, gi)]
        d_conv[(l, gi)].w = (sem, P.dsem_val[sem], "dma")

    def conv_pending():
        return [(k, len(v)) for k, v in conv_lists.items() if v]

    def tile_src(key):
        kind, l_, i_ = key
        if kind == "W1":
            return W1b[l_, i_], 1024, d_conv[(l_, 0)]
        if kind == "W3":
            return W3b[l_, i_], 1024, d_conv[(l_, 1 if i_ < 22 else 2)]
        if kind == "WBR":
            return WBRb[l_, i_], 1792, d_conv[(l_, 2)]
        return WOUTb[l_, i_], 1024, d_conv[(l_, 2)]

    class Ring:
        def __init__(self, name, nslot, width, sched_):
            self.buf = [S("%s%d" % (name, i), [128, width], BF16) for i in range(nslot)]
            self.dep = [Dep() for _ in range(nslot)]
            self.sem = [P.dma_sem("s_%s%d" % (name, i)) for i in range(nslot)]
            self.n = nslot
            self.sched = sched_
            self.rec = []
            self.issued = 0
            self.consumed = 0
            self.low = 0
            self.done = set()

        def _issue(self, upto):
            if self.sched is None:
                return
            while self.issued < min(upto, len(self.sched)):
                i = self.issued
                sl = i % self.n
                ap, ncol, dc = tile_src(self.sched[i])
                if dc.w is None or conv_lists[[k for k, v in d_conv.items() if v is dc][0]]:
                    break
                P.dma(SP, self.buf[sl][:, 0:ncol], ap, self.sem[sl], reads=[dc], writes=[self.dep[sl]])
                self.issued += 1

        def get(self, key):
            i = self.consumed
            self.rec.append(key)
            if self.sched is not None:
                assert self.sched[i] == key, (i, self.sched[i], key)
                assert i < self.low + self.n
                self._issue(self.low + self.n)
                assert self.issued > i
            self.consumed += 1
            return self.buf[i % self.n], self.dep[i % self.n], i

        def rel(self, i):
            self.done.add(i)
            while self.low in self.done:
                self.done.discard(self.low)
                self.low += 1
            self._issue(self.low + self.n)

    ringA = Ring("rA", NSLOT_A, 1024, None if sched is None else sched[0])
    ringB = Ring("rB", NSLOT_B, 1792, None if sched is None else sched[1])

    memset(POOL, ones_bf[:], 1.0, [d_const])
    memset(POOL, ones32[:], 1.0, [d_const])
    P.dma(POOL, maskS[:], maskT, s_misc, writes=[d_const])
    emit_conv(0, 0)
    s_conv[(0, 9)] = P.dma_sem("s_xc")
    d_conv[(0, 9)] = Dep("xbf")
    conv_lists[(0, 9)] = [(xbf_d[0][:, :, n0_:n0_ + N_], xT[:, :, n0_:n0_ + N_]) for (n0_, N_, _sg) in BLOCKS]
    d_xbf_first = d_xbf_rest = d_conv[(0, 9)]
    conv_order = [(0, 9), (0, 1), (0, 2)]
    conv_order_l1 = [(1, 0), (1, 1), (1, 2)]

    def emit_conv_some(budget, order=None):
        for key in (order or conv_order):
            if budget <= 0:
                break
            n_ = min(budget, len(conv_lists[key]))
            if n_ > 0:
                emit_conv(key[0], key[1], n_)
                budget -= n_

    for l in range(NL):
        xsrc = xT if l == 0 else xres_d
        d_xsrc = Dep("xsrc")
        ydst = xres_d if l == 0 else yT

        with contextlib.ExitStack() as ph:
            memTb = S("memTb", [128, 8, 256], BF16, ph)
            d_memTb = Dep("memTb")
            P.dma(POOL, memTb[:], memT, s_misc, writes=[d_memTb])
            mwk = S("mwk", [128, 8, 256], BF16, ph)
            mwv = S("mwv", [128, 8, 256], BF16, ph)
            mk32 = S("mk32", [128, 2, 256], F32, ph)
            mv32 = S("mv32", [128, 2, 256], F32, ph)
            cm32 = S("cm32", [128, 2, 256], F32, ph)
            cv32 = S("cv32", [128, 2, 256], F32, ph)
            wst32 = S("wst32", [128, 4, 128], F32, ph)
            bsrow32 = S("bsrow32", [1, 512], F32, ph)
            d_p0 = Dep("p0")
            dcv = d_conv[(l, 0)]
            P.dma(SP, wav[:], WAVb[l].rearrange("p (k n) -> p k n", k=8), s_misc_sp, reads=[dcv], writes=[d_lw])
            P.dma(SP, wuq[:], WUQb[l].rearrange("p (k n) -> p k n", k=2), s_misc_sp, reads=[dcv], writes=[d_lw])
            P.dma(SP, wuqs[:], WUQSb[l].rearrange("p (k n) -> p k n", k=2), s_misc_sp, reads=[dcv], writes=[d_lw])
            P.dma(SP, wukv[:], WUKVb[l], s_misc_sp, reads=[dcv], writes=[d_lw])
            P.dma(SP, wr[:], WRb[l].rearrange("p (k n) -> p k n", k=4), s_misc_sp, reads=[dcv], writes=[d_lw])
            P.dma(SP, wi[:], WIb[l].rearrange("p (k n) -> p k n", k=4), s_misc_sp, reads=[dcv], writes=[d_lw])
            P.dma(SP, mwk[:], MWKb[l].rearrange("p (k n) -> p k n", k=8), s_misc_sp, reads=[dcv], writes=[d_p0])
            P.dma(SP, mwv[:], MWVb[l].rearrange("p (k n) -> p k n", k=8), s_misc_sp, reads=[dcv], writes=[d_p0])
            P.dma(SP, wst32[:], WST[l].rearrange("p (k n) -> p k n", k=4), s_misc_sp, writes=[d_lw])
            P.dma(SP, pv[:], PVin[l], s_misc_sp, writes=[d_lw])
            P.dma(SP, gbc[:], GLN[l][0:1, :].partition_broadcast(128), s_misc_sp, writes=[d_lw])
            P.dma(SP, bbc[:], GLN[l][1:2, :].partition_broadcast(128), s_misc_sp, writes=[d_lw])
            P.dma(SP, bsrow32[:], BSin[l], s_misc_sp, writes=[d_lw])
            P.dma(SP, cm32[:], cmkT[l], s_misc_sp, writes=[d_p0])
            P.dma(SP, cv32[:], cmv[l], s_misc_sp, writes=[d_p0])
            P.dma(SP, hst[1][:], shin[l], s_misc_sp, writes=[d_hst[1]])
            P.dma(SP, carry[1][:], scin[l], s_misc_sp, writes=[d_carry[1]])
            memset(POOL, hst[0][:], 0.0, [d_hst[0]])
            memset(POOL, carry[0][:], 0.0, [d_carry[0]])
            tt(DVE, wst[:], wst32[:], maskS[:].unsqueeze(1).to_broadcast([128, 4, 128]), ALU.mult,
               [d_lw, d_const], [d_lw])
            cp(DVE, bsrow[:], bsrow32[:], [d_lw], [d_lw])
            tsc(DVE, pvd[:, 0:4], pv[:, 20:24], 0.5, None, ALU.mult, None, [d_lw], [d_lw])
            tsc(DVE, pvd[:, 4:8], pv[:, 24:28], 0.5, None, ALU.mult, None, [d_lw], [d_lw])
            act(pvd[:, 12:16], pv[:, 28:32], AF.Exp, [d_lw], [d_lw], scale=-1.0)
            act(pvd[:, 12:16], pvd[:, 12:16], AF.Ln, [d_lw], [d_lw], bias=1.0)
            tsc(DVE, pvd[:, 8:12], pvd[:, 12:16], -4.0, None, ALU.mult, None, [d_lw], [d_lw])
            tsc(DVE, pvd[:, 12:16], pvd[:, 12:16], -8.0, None, ALU.mult, None, [d_lw], [d_lw])
            memset(POOL, mvx[0][:], 1.0, [d_mem])
            memset(POOL, mvx[1][:], 1.0, [d_mem])
            for c in range(2):
                bk, db = nb()
                mmg([(bk[:, 0:256], mwk[:, k, c * 128:(c + 1) * 128], memTb[:, k, :], k == 0, k == 7) for k in range(8)],
                    [d_p0, d_memTb], [db])
                act(mk32[:, c, :], bk[:, 0:256], AF.Copy, [db], [d_p0])
            cp(DVE, mkT[0][:], mk32[:], [d_p0], [d_mem])
            out_dma(pmk[l], mk32[:], [d_p0])
            for mt in range(2):
                bk, db = nb()
                mmg([(bk[:, 0:256], memTb[:, k, mt * 128:(mt + 1) * 128], mwv[:, k, :], k == 0, k == 7) for k in range(8)],
                    [d_p0, d_memTb], [db])
                act(mv32[:, mt, :], bk[:, 0:256], AF.Copy, [db], [d_p0])
            out_dma(pmv[l], mv32[:], [d_p0])
            for mt in range(2):
                cp(DVE, mvx[0][:, mt, :, 0:64], mv32[:, mt, :].rearrange("p (h d) -> p h d", h=4), [d_p0], [d_mem])
                cp(DVE, mvx[1][:, mt, :, 0:64], cv32[:, mt, :].rearrange("p (h d) -> p h d", h=4), [d_p0], [d_mem])
            cp(DVE, mkT[1][:], cm32[:], [d_p0], [d_mem])
            P.barrier()

        with contextlib.ExitStack() as ph:
            cosS = S("cosS", [128, TT], F32, ph)
            sinS = S("sinS", [128, TT], F32, ph)
            d_tab = Dep("tab")
            P.dma(POOL, cosS[64:96, :], cosT, s_misc, writes=[d_tab])
            P.dma(POOL, sinS[64:96, :], sinT, s_misc, writes=[d_tab])
            cqn = S("cqn", [128, 2, TT], BF16, ph)
            ckvall = S("ckvall", [128, KC], BF16, ph)
            Kb = S("Kb", [96, KC], BF16, ph)
            tmp = [S("t1_%d" % i, [128, 512], F32, ph) for i in range(4)]
            p1s = contextlib.ExitStack()
            xb = [S("xb1_%d" % i, [128, 8, 512], BF16, p1s) for i in range(2)]
            sq = [S("sq%d" % i, [128, 512], BF16, p1s) for i in range(3)]
            xst = S("xst", [128, 8, 512], F32, p1s) if l == 0 else None
            d_xst = Dep()
            d_cqn, d_ckvall, d_Kn, d_Kr, d_Q, d_V = Dep(), Dep(), Dep(), Dep(), Dep(), Dep()
            d_xb = [Dep(), Dep()]
            d_PT = [Dep() for _ in range(4)]
            d_tmp = [Dep() for _ in range(4)]
            d_sq = [Dep(), Dep(), Dep()]
            d_rsum = [Dep(), Dep()]
            ti = {"i": 0}

            def T1():
                i = ti["i"] % 4
                ti["i"] += 1
                return tmp[i], d_tmp[i]

            P.dma(POOL, ckvall[:, TP:TP + PAST], cckvT[l], s_misc, writes=[d_ckvall])
            P.dma(POOL, Kb[64:96, TP:TP + PAST], ckrT[l], s_misc, writes=[d_Kr])
            def p1_load(bj):
                n0_, N_, _ = BLOCKS[bj]
                if l == 0:
                    P.dma(ACT, xst[:, :, 0:N_], xT[:, :, n0_:n0_ + N_], s_xst, writes=[d_xst])
                else:
                    P.dma(ACT, xb[bj % 2][:, :, 0:N_], xbf_d[l][:, :, n0_:n0_ + N_], s_x[bj % 2], writes=[d_xb[bj % 2]])

            def p1_cast(bj):
                _, N_, _ = BLOCKS[bj]
                act(xb[bj % 2][:, 0:4, 0:N_], xst[:, 0:4, 0:N_], AF.Copy, [d_xst], [d_xb[bj % 2]])
                cp(DVE, xb[bj % 2][:, 4:8, 0:N_], xst[:, 4:8, 0:N_], [d_xst], [d_xb[bj % 2]])

            p1_load(0)
            for bi, (n0, N, seg) in enumerate(BLOCKS):
                cc0 = n0 if seg == 0 else TP + PAST
                x_b, dx = xb[bi % 2], d_xb[bi % 2]
                if l == 0:
                    p1_cast(bi)
                if bi + 1 < len(BLOCKS):
                    p1_load(bi + 1)
                pz = []
                for i in range(5):
                    wt_, dw, ri = ringA.get(("W1", l, i))
                    bk, db = nb()
                    mmg([(bk[:, 0:N], wt_[:, k * 128:(k + 1) * 128], x_b[:, k, 0:N], k == 0, k == 7) for k in range(8)],
                        [dw, dx], [db])
                    ringA.rel(ri)
                    pz.append((bk, db))
                k1, dk1 = T1()
                k2, dk2 = T1()
                tt(DVE, k1[64:96, 0:N], pz[3][0][64:96, 0:N], cosS[64:96, n0:n0 + N], ALU.mult, [pz[3][1], d_tab], [dk1])
                tt(DVE, k2[64:96, 0:N], pz[4][0][64:96, 0:N], sinS[64:96, n0:n0 + N], ALU.mult, [pz[4][1], d_tab], [dk2])
                tt(DVE, k1[64:96, 0:N], k1[64:96, 0:N], k2[64:96, 0:N], ALU.add, [dk1, dk2], [dk1])
                bs_, dbs = banks[6], dbank[6]
                bs2, dbs2 = banks[7], dbank[7]
                for c in range(3):
                    act(sq[c][:, 0:N], pz[c][0][:, 0:N], AF.Square, [pz[c][1]], [d_sq[c]])
                mmg([(bs_[:, 0:N], ones_bf[:], sq[c][:, 0:N], c == 0, c == 1) for c in range(2)],
                    [d_const, d_sq[0], d_sq[1]], [dbs])
                mmg([(bs2[:, 0:N], ones_bf[:], sq[2][:, 0:N], True, True)], [d_const, d_sq[2]], [dbs2])
                r_, dr = T1()
                r2, dr2 = T1()
                tsc(DVE, r_[:, 0:N], bs_[:, 0:N], 1.0 / 256.0, EPS, ALU.mult, ALU.add, [dbs], [dr])
                tsc(DVE, r2[:, 0:N], bs2[:, 0:N], 1.0 / 128.0, EPS, ALU.mult, ALU.add, [dbs2], [dr2])
                act(r_[:, 0:N], r_[:, 0:N], AF.Ln, [dr], [dr])
                act(r2[:, 0:N], r2[:, 0:N], AF.Ln, [dr2], [dr2])
                act(r_[:, 0:N], r_[:, 0:N], AF.Exp, [dr], [dr], scale=-0.5)
                act(r2[:, 0:N], r2[:, 0:N], AF.Exp, [dr2], [dr2], scale=-0.5)
                for c in range(2):
                    stt(cqn[:, c, n0:n0 + N], pz[c][0][:, 0:N], pv[:, 32 + c:33 + c], r_[:, 0:N], ALU.mult, ALU.mult,
                        [pz[c][1], d_lw, dr], [d_cqn])
                stt(r2[:, 0:N], pz[2][0][:, 0:N], pv[:, 34:35], r2[:, 0:N], ALU.mult, ALU.mult, [pz[2][1], d_lw, dr2], [dr2])
                out_dma(ckvo[l][:, n0:n0 + N], r2[:, 0:N], [dr2])
                act(ckvall[:, cc0:cc0 + N], r2[:, 0:N], AF.Copy, [dr2], [d_ckvall])
                out_dma(kro[l][:, n0:n0 + N], k1[64:96, 0:N], [dk1])
                act(Kb[64:96, cc0:cc0 + N], k1[64:96, 0:N], AF.Copy, [dk1], [d_Kr])

            P.barrier()
            p1s.close()
            Qb = S("Qb", [96, TT], BF16, ph)
            Vb = S("Vb", [128, 41, 128], BF16, ph)
            PT = [S("PT%d" % i, [128, 512], BF16, ph) for i in range(4)]
            rsum = [S("rsum%d" % i, [128, 512], F32, ph) for i in range(2)]
            memset(POOL, Vb[:], 1.0, [d_V])
            Kbs = [Kb, S("Kb1", [96, KC], BF16, ph)]
            Qbs = [Qb, S("Qb1", [96, TT], BF16, ph)]
            d_Kns = [d_Kn, Dep()]
            d_Qs = [d_Q, Dep()]
            cp(DVE, Kbs[1][64:96, :], Kb[64:96, :], [d_Kr], [d_Kr])
            obank = [(banks[6], dbank[6]), (banks[7], dbank[7])]

            def kq_pieces(h):
                Kh, dKh, Qh, dQh = Kbs[h % 2], d_Kns[h % 2], Qbs[h % 2], d_Qs[h % 2]
                out = []
                c0 = 0
                while c0 < KC:
                    n = min(512, KC - c0)

                    def kp(c0=c0, n=n):
                        bk, db = nb()
                        mmg([(bk[0:64, 0:n], wukv[:, h * 128:h * 128 + 64], ckvall[:, c0:c0 + n], True, True)],
                            [d_lw, d_ckvall], [db])
                        cp(DVE, Kh[0:64, c0:c0 + n], bk[0:64, 0:n], [db], [dKh])
                    out.append(kp)
                    c0 += n
                for (n0, N, seg) in BLOCKS:
                    def qp(n0=n0, N=N):
                        qa, dqa = nb()
                        qs, dqs = nb()
                        mmg([(qa[0:96, 0:N], wuq[:, k, h * 96:(h + 1) * 96], cqn[:, k, n0:n0 + N], k == 0, k == 1) for k in range(2)],
                            [d_lw, d_cqn], [dqa])
                        mmg([(qs[0:96, 0:N], wuqs[:, k, h * 96:(h + 1) * 96], cqn[:, k, n0:n0 + N], k == 0, k == 1) for k in range(2)],
                            [d_lw, d_cqn], [dqs])
                        cp(DVE, Qh[0:64, n0:n0 + N], qa[0:64, 0:N], [dqa], [dQh])
                        k1, dk1 = T1()
                        k2, dk2 = T1()
                        tt(DVE, k1[64:96, 0:N], qa[64:96, 0:N], cosS[64:96, n0:n0 + N], ALU.mult, [dqa, d_tab], [dk1])
                        tt(DVE, k2[64:96, 0:N], qs[64:96, 0:N], sinS[64:96, n0:n0 + N], ALU.mult, [dqs, d_tab], [dk2])
                        tt(DVE, Qh[64:96, n0:n0 + N], k1[64:96, 0:N], k2[64:96, 0:N], ALU.add, [dk1, dk2], [dQh])
                    out.append(qp)
                return out

            for f_ in kq_pieces(0):
                f_()
            for h in range(8):
                Kh, dKh, Qh, dQh = Kbs[h % 2], d_Kns[h % 2], Qbs[h % 2], d_Qs[h % 2]
                pieces = kq_pieces(h + 1) if h + 1 < 8 else []
                if l == 0:
                    emit_conv_some(11 if h < 7 else 1000)
                for b8 in range(5):
                    bk, db = nb()
                    mmg([(bk[:, i * 64:(i + 1) * 64], ckvall[:, (b8 * 8 + i) * 128:(b8 * 8 + i + 1) * 128],
                          wukv[:, h * 128 + 64:h * 128 + 128], True, True) for i in range(8)],
                        [d_lw, d_ckvall], [db])
                    cp(DVE, Vb[:, b8 * 8:b8 * 8 + 8, 0:64], bk[:, :].rearrange("p (t d) -> p t d", t=8), [db], [d_V])
                bk, db = nb()
                mmg([(bk[0:16, 0:64], ckvall[:, TP + PAST:KC], wukv[:, h * 128 + 64:h * 128 + 128], True, True)],
                    [d_lw, d_ckvall], [db])
                cp(DVE, Vb[0:16, 40, 0:64], bk[0:16, 0:64], [db], [d_V])
                items = []
                for qb in range(8):
                    nk = 4 * qb + 4
                    for kt in range(nk):
                        j = kt - 4 * qb
                        cs = 128 * j if j > 0 else 0
                        items.append(dict(q0=qb * 512, nq=512, cs=cs, kcol=kt * 128, kn=128, vt=kt, diag=(j >= 0),
                                          first=(kt == 0), last=(kt == nk - 1), ob=qb % 2))
                for i in range(8):
                    items.append(dict(q0=TP, nq=TS, cs=0, kcol=TP + 128 * i, kn=128, vt=32 + i, diag=False,
                                      first=(i == 0), last=False, ob=0))
                items.append(dict(q0=TP, nq=TS, cs=0, kcol=TP + PAST, kn=TS, vt=40, diag=False, first=False, last=True, ob=0))
                LOOK = 3
                inflight = []
                norms = []
                pti = 0
                for ii in range(len(items) + LOOK + 9):
                    while norms and norms[0][0] <= ii:
                        norms.pop(0)[1]()
                    if pieces and ii >= 12 and ii % 5 == 0:
                        pieces.pop(0)()
                    if ii < len(items):
                        it = items[ii]
                        sb_, dsb = nb()
                        pt_, dpt = PT[pti % 4], d_PT[pti % 4]
                        pti += 1
                        kn, cs, nq, q0 = it["kn"], it["cs"], it["nq"], it["q0"]
                        mmg([(sb_[0:kn, cs:nq], Kh[0:96, it["kcol"]:it["kcol"] + kn], Qh[0:96, q0 + cs:q0 + nq], True, True)],
                            [dKh, d_Kr, dQh], [dsb])
                        act(pt_[0:kn, cs:nq], sb_[0:kn, cs:nq], AF.Exp, [dsb], [dpt], scale=SM_SCALE)
                        if it["diag"]:
                            memset(POOL, pt_[64:128, cs:cs + 64], 0.0, [dpt])
                        inflight.append((it, pt_, dpt))
                    if ii >= LOOK and inflight:
                        it, pt_, dpt = inflight.pop(0)
                        ob_, dob = obank[it["ob"]]
                        kn, cs, nq, q0 = it["kn"], it["cs"], it["nq"], it["q0"]
                        mmg([(ob_[0:128, cs:nq], Vb[0:kn, it["vt"], 0:128], pt_[0:kn, cs:nq], it["first"], it["last"])],
                            [d_V, dpt], [dob])
                        if it["last"]:
                            rs_, drs = rsum[it["ob"]], d_rsum[it["ob"]]
                            P.op(DVE, (lambda e, o=rs_[64:128, 0:nq], i_=ob_[64:128, 0:nq]: e.reciprocal(o, i_)), [dob], [drs])
                            p0 = (h % 2) * 64
                            tt(DVE, ocraw[p0:p0 + 64, h // 2, q0:q0 + nq], ob_[0:64, 0:nq], rs_[64:128, 0:nq], ALU.mult,
                               [dob, drs], [d_ocraw])
                while pieces:
                    pieces.pop(0)()
            if debug:
                t32 = tmp[0]
                for c in range(4):
                    for (n0, N, seg) in BLOCKS:
                        cp(DVE, t32[:, 0:N], ocraw[:, c, n0:n0 + N], [d_ocraw], [d_tmp[0]])
                        out_dma(dbg["ocraw"][l][:, c, n0:n0 + N], t32[:, 0:N], [d_tmp[0]])
            P.barrier()

        with contextlib.ExitStack() as ph:
            xres = [S("xres%d" % i, [128, 8, 512], F32, ph) for i in range(1)]
            xb3 = [S("xb3_%d" % i, [128, 8, 512], BF16, ph) for i in range(2)]
            vb = S("vb", [128, 4, 512], BF16, ph)
            uag = S("uag", [128, 4, 512], BF16, ph)
            oas = [S("oa%d" % i, [128, 4, 512], BF16, ph) for i in range(2)]
            d_oas = [Dep(), Dep()]
            obs = [S("ob%d" % i, [128, 4, 512], BF16, ph) for i in range(2)]
            ocbs = [S("ocb%d" % i, [128, 4, 512], BF16, ph) for i in range(2)]
            mq = S("mq", [128, 2, 512], BF16, ph)
            oms = [S("om%d" % i, [128, 2, 512], BF16, ph) for i in range(2)]
            mg = S("mg", [128, 8, 512], BF16, ph)
            bxs = [S("bx%d" % i, [128, 515], F32, ph) for i in range(2)]
            xcs = [S("xc%d" % i, [128, 512], F32, ph) for i in range(2)]
            xcbs = [S("xcb%d" % i, [128, 512], BF16, ph) for i in range(2)]
            sgbs = [S("sgb%d" % i, [128, 512], F32, ph) for i in range(2)]
            d_sgbs = [Dep(), Dep()]
            macc = [S("macc%d" % i, [128, 512], F32, ph) for i in range(2)]
            tmp = [S("t3_%d" % i, [128, 512], F32, ph) for i in range(5)]
            tb16 = [S("tb16_%d" % i, [128, 512], BF16, ph) for i in range(2)]
            gv = S("gv", [128, 4, 512], F32, ph)
            d_gv = Dep()
            tmn = [S("tmn%d" % i, [128, 512], F32, ph) for i in range(2)]
            d_tmn = [Dep(), Dep()]
            PT3 = [S("PT3_%d" % i, [128, 512], BF16, ph) for i in range(4)]
            stt_ = S("stat", [128, 20], F32, ph)
            d_xres = [Dep()]
            d_xb3 = [Dep(), Dep()]
            d_vb, d_uag, d_mq, d_mg = [Dep() for _ in range(4)]
            d_obs, d_ocbs, d_oms = [Dep(), Dep()], [Dep(), Dep()], [Dep(), Dep()]
            d_stat = Dep()
            d_bxs, d_xcs, d_xcbs = [Dep(), Dep()], [Dep(), Dep()], [Dep(), Dep()]
            d_macc = [Dep(), Dep()]
            d_tmp = [Dep() for _ in range(5)]
            d_tb16 = [Dep() for _ in range(2)]
            d_PT3 = [Dep() for _ in range(4)]
            ti = {"i": 0}

            def T3():
                i = ti["i"] % 5
                ti["i"] += 1
                return tmp[i], d_tmp[i]

            def silu2(dst, ddst, pz, dpz, N):
                act(dst[:, 0:N], pz[:, 0:N], AF.Tanh, [dpz], [ddst], scale=0.5)
                stt(dst[:, 0:N], dst[:, 0:N], 1.0, pz[:, 0:N], ALU.add, ALU.mult, [ddst, dpz], [ddst])

            def gelu2(dst, ddst, pz, dpz, m, n):
                xs, dxs = T3()
                act(xs[0:m, 0:n], pz[0:m, 0:n], AF.Copy, [dpz], [dxs])
                act(dst[0:m, 0:n], pz[0:m, 0:n], AF.Square, [dpz], [ddst])
                tsc(DVE, dst[0:m, 0:n], dst[0:m, 0:n], 0.044715, 1.0, ALU.mult, ALU.add, [ddst], [ddst])
                tt(DVE, dst[0:m, 0:n], dst[0:m, 0:n], xs[0:m, 0:n], ALU.mult, [ddst, dxs], [ddst])
                act(dst[0:m, 0:n], dst[0:m, 0:n], AF.Tanh, [ddst], [ddst], scale=GELU_C)
                stt(dst[0:m, 0:n], dst[0:m, 0:n], 1.0, xs[0:m, 0:n], ALU.add, ALU.mult, [ddst, dxs], [ddst])

            def p3_load(bj):
                n0_, N_, _ = BLOCKS[bj]
                rd = [] if l == 1 else [d_xbf_first if bj == 0 else d_xbf_rest]
                P.dma(ACT, xb3[bj % 2][:, :, 0:N_], xbf_d[l][:, :, n0_:n0_ + N_], s_x[2 + bj % 2], reads=rd, writes=[d_xb3[bj % 2]])

            def p3_load_res(bj):
                n0_, N_, _ = BLOCKS[bj]
                P.dma(ACT, xres[0][:, :, 0:N_], xsrc[:, :, n0_:n0_ + N_], s_x[4], reads=[d_xsrc], writes=[d_xres[0]])

            def zmm_b(key, bj):
                _, N_, _ = BLOCKS[bj]
                xbj, dxbj = xb3[bj % 2], d_xb3[bj % 2]
                wt_, dw, ri = ringA.get(key)
                bk, db = nb()
                mmg([(bk[:, 0:N_], wt_[:, k * 128:(k + 1) * 128], xbj[:, k, 0:N_], k == 0, k == 7) for k in range(8)],
                    [dw, dxbj], [db])
                ringA.rel(ri)
                return bk, db

            def au_ag(bj):
                _, N_, _ = BLOCKS[bj]
                for c in range(4):
                    pu, dpu = zmm_b(("W3", l, 2 * c), bj)
                    u2, du2 = T3()
                    act(u2[:, 0:N_], pu[:, 0:N_], AF.Gelu_apprx_tanh, [dpu], [du2])
                    pg, dpg = zmm_b(("W3", l, 2 * c + 1), bj)
                    s2, ds2 = T3()
                    silu2(s2, ds2, pg, dpg, N_)
                    stt(uag[:, c, 0:N_], s2[:, 0:N_], 0.5, u2[:, 0:N_], ALU.mult, ALU.mult, [ds2, du2], [d_uag])

            def v_pass1(bj):
                n0_, N_, _ = BLOCKS[bj]
                xbj, dxbj = xb3[bj % 2], d_xb3[bj % 2]
                memset(POOL, stt_[:, :], 1.0, [d_stat])
                for t_ in range((N_ + 127) // 128):
                    m = min(128, N_ - t_ * 128)
                    bk, db = nb()
                    mmg([(bk[0:m, :], xbj[:, k, t_ * 128:t_ * 128 + m], wav[:, k, :], k == 0, k == 7) for k in range(8)],
                        [dxbj, d_lw], [db])
                    act(gv[0:m, t_, :], bk[0:m, :], AF.Gelu_apprx_tanh, [db], [d_gv, d_stat], accum=stt_[0:m, t_:t_ + 1])
                    act(PT3[3][0:m, :], gv[0:m, t_, :], AF.Square, [d_gv], [d_PT3[3], d_stat], accum=stt_[0:m, 4 + t_:5 + t_])

            def lru_z(bj, c):
                pg, dpg = zmm_b(("W3", l, 8 + 2 * c), bj)
                px, dpx = zmm_b(("W3", l, 8 + 2 * c + 1), bj)
                return pg, dpg, px, dpx

            def cg_tile(bj, c):
                n0_, N_, _ = BLOCKS[bj]
                pc, dpc = zmm_b(("W3", l, 16 + c), bj)
                sg, dsg = T3()
                act(sg[:, 0:N_], pc[:, 0:N_], AF.Silu, [dpc], [dsg])
                tt(DVE, ocbs[bj % 2][:, c, 0:N_], ocraw[:, c, n0_:n0_ + N_], sg[:, 0:N_], ALU.mult, [d_ocraw, dsg], [d_ocbs[bj % 2]])

            def lru_front(bj, c, zc):
                _, N_, seg_ = BLOCKS[bj]
                pg, dpg, px, dpx = zc
                bx, d_bx, xc, d_xc, xcb, d_xcb = bxs[c % 2], d_bxs[c % 2], xcs[c % 2], d_xcs[c % 2], xcbs[c % 2], d_xcbs[c % 2]
                sg, dsg = sgbs[c % 2], d_sgbs[c % 2]
                act(sg[:, 0:N_], pg[:, 0:N_], AF.Silu, [dpg], [dsg])
                cp(POOL, bx[:, 0:3], carry[seg_][:, c, :], [d_carry[seg_]], [d_bx])
                act(bx[:, 3:3 + N_], px[:, 0:N_], AF.Copy, [dpx], [d_bx])
                cp(POOL, carry[seg_][:, c, :], bx[:, N_:N_ + 3], [d_bx], [d_carry[seg_]])
                tsc(DVE, xc[:, 0:N_], bx[:, 0:N_], pv[:, 4 * c:4 * c + 1], pv[:, 16 + c:17 + c], ALU.mult, ALU.add,
                    [d_bx, d_lw], [d_xc])
                for k in range(1, 3):
                    stt(xc[:, 0:N_], bx[:, k:k + N_], pv[:, 4 * c + k:4 * c + k + 1], xc[:, 0:N_], ALU.mult, ALU.add,
                        [d_bx, d_lw, d_xc], [d_xc])
                stt(xcb[:, 0:N_], bx[:, 3:3 + N_], pv[:, 4 * c + 3:4 * c + 4], xc[:, 0:N_], ALU.mult, ALU.add,
                    [d_bx, d_lw, d_xc], [d_xcb])
                stt(xc[:, 0:N_], bx[:, 3:3 + N_], pv[:, 4 * c + 3:4 * c + 4], xc[:, 0:N_], ALU.mult, ALU.add,
                    [d_bx, d_lw, d_xc], [d_xc])
                return sg, dsg

            def lru_step(bj, c, stl):
                n0_, N_, seg_ = BLOCKS[bj]
                frs = stl["frs"]
                if c == 0:
                    frs[0] = lru_front(bj, 0, lru_z(bj, 0))
                sg, dsg = frs[c]
                xc, d_xc, xcb, d_xcb = xcs[c % 2], d_xcs[c % 2], xcbs[c % 2], d_xcbs[c % 2]
                pr, dpr = nb()
                pi_, dpi = nb()
                mmg([(pr[:, 0:N_], wr[:, c, :], xcb[:, 0:N_], True, True)], [d_lw, d_xcb], [dpr])
                mmg([(pi_[:, 0:N_], wi[:, c, :], xcb[:, 0:N_], True, True)], [d_lw, d_xcb], [dpi])
                cg_tile(bj, c)
                ta, dta = T3()
                tc_, dtc = T3()
                td, dtd = T3()
                act(ta[:, 0:N_], pr[:, 0:N_], AF.Tanh, [dpr, d_lw], [dta], scale=0.5, bias=pvd[:, c:c + 1])
                act(tc_[:, 0:N_], pi_[:, 0:N_], AF.Tanh, [dpi, d_lw], [dtc], scale=0.5, bias=pvd[:, 4 + c:5 + c])
                act(td[:, 0:N_], ta[:, 0:N_], AF.Exp, [dta, d_lw], [dtd], scale=pvd[:, 12 + c:13 + c], bias=pvd[:, 12 + c:13 + c])
                act(ta[:, 0:N_], ta[:, 0:N_], AF.Exp, [dta, d_lw], [dta], scale=pvd[:, 8 + c:9 + c], bias=pvd[:, 8 + c:9 + c])
                act(td[:, 0:N_], td[:, 0:N_], AF.Ln, [dtd], [dtd], scale=-1.0, bias=1.0)
                act(td[:, 0:N_], td[:, 0:N_], AF.Exp, [dtd], [dtd], scale=0.5)
                stt(tc_[:, 0:N_], tc_[:, 0:N_], 1.0, xc[:, 0:N_], ALU.add, ALU.mult, [dtc, d_xc], [dtc])
                stt(tc_[:, 0:N_], tc_[:, 0:N_], 0.5, td[:, 0:N_], ALU.mult, ALU.mult, [dtc, dtd], [dtc])
                P.op(DVE, (lambda e, o=td[:, 0:N_], a=ta[:, 0:N_], b=tc_[:, 0:N_], i0=hst[seg_][:, c:c + 1]:
                           e.tensor_tensor_scan(out=o, data0=a, data1=b, initial=i0, op0=ALU.mult, op1=ALU.add)),
                     [dta, dtc, d_hst[seg_]], [dtd])
                cp(DVE, hst[seg_][:, c:c + 1], td[:, N_ - 1:N_], [dtd], [d_hst[seg_]])
                tt(DVE, obs[bj % 2][:, c, 0:N_], td[:, 0:N_], sg[:, 0:N_], ALU.mult, [dtd, dsg], [d_obs[bj % 2]])
                if c + 1 < 4:
                    frs[c + 1] = lru_front(bj, c + 1, lru_z(bj, c + 1))
                if c == 3 and (bj == 7 or seg_ == 1):
                    out_dma(lruh[l, seg_], hst[seg_][:], [d_hst[seg_]])
                    out_dma(lruc[l, seg_], carry[seg_][:], [d_carry[seg_]])

            def mq_tiles(bj):
                _, N_, _ = BLOCKS[bj]
                for c in range(2):
                    pm, dpm = zmm_b(("W3", l, 20 + c), bj)
                    act(mq[:, c, 0:N_], pm[:, 0:N_], AF.Copy, [dpm], [d_mq])

            def mem_pair(bj, hp):
                _, N_, seg_ = BLOCKS[bj]
                c = hp
                om_, dom_ = oms[bj % 2], d_oms[bj % 2]
                sbs = []
                for hh in range(2):
                    p0 = hh * 64
                    for mt in range(2):
                        sb_, dsb = nb()
                        mmg([(sb_[:, 0:N_], mkT[seg_][p0:p0 + 64, c, mt * 128:(mt + 1) * 128], mq[p0:p0 + 64, c, 0:N_], True, True)],
                            [d_mem, d_mq], [dsb])
                        sbs.append((sb_, dsb))
                for hh in range(2):
                    h = 2 * hp + hh
                    po, dpo = banks[6 + hh], dbank[6 + hh]
                    for mt in range(2):
                        sb_, dsb = sbs[2 * hh + mt]
                        pt_, dpt = PT3[2 * hh + mt], d_PT3[2 * hh + mt]
                        act(pt_[:, 0:N_], sb_[:, 0:N_], AF.Exp, [dsb], [dpt], scale=0.125)
                        mmg([(po[0:128, 0:N_], mvx[seg_][:, mt, h, :], pt_[:, 0:N_], mt == 0, mt == 1)], [d_mem, dpt], [dpo])
                rss = []
                for hh in range(2):
                    po, dpo = banks[6 + hh], dbank[6 + hh]
                    rs3, d_rs3 = T3()
                    P.op(DVE, (lambda e, o=rs3[64:128, 0:N_], i_=po[64:128, 0:N_]: e.reciprocal(o, i_)), [dpo], [d_rs3])
                    rss.append((rs3, d_rs3))
                for hh in range(2):
                    po, dpo = banks[6 + hh], dbank[6 + hh]
                    rs3, d_rs3 = rss[hh]
                    tt(DVE, om_[hh * 64:hh * 64 + 64, c, 0:N_], po[0:64, 0:N_], rs3[64:128, 0:N_], ALU.mult, [dpo, d_rs3], [dom_])

            def spatial_part(bj):
                n0, N, seg = BLOCKS[bj]
                ntile = (N + 127) // 128
                oa, d_oa = oas[bj % 2], d_oas[bj % 2]
                for g in range(4):
                    bk, db = nb()
                    mms = []
                    for t_ in range(ntile):
                        m = min(128, N - t_ * 128)
                        mms.append((bk[:, t_ * 128:t_ * 128 + m], vb[0:m, t_, g * 128:(g + 1) * 128], wst[0:m, g, 0:m], True, False))
                        mms.append((bk[:, t_ * 128:t_ * 128 + m], ones_bf[0:1, 0:128], bsrow[0:1, g * 128:g * 128 + m], False, True))
                    mmg(mms, [d_vb, d_lw, d_const], [db])
                    tt(DVE, oa[:, g, 0:N], bk[:, 0:N], uag[:, g, 0:N], ALU.mult, [db, d_uag], [d_oa])

            def v_rest_spatial(bj, part=None):
                n0, N, seg = BLOCKS[bj]
                ntile = (N + 127) // 128
                oa, d_oa = oas[bj % 2], d_oas[bj % 2]
                if part == 1:
                    return spatial_part(bj)
                mt_ = [min(128, N - t_ * 128) for t_ in range(ntile)]
                mm_ = mt_[0]
                nt = ntile
                tsc(DVE, stt_[0:mm_, 8:8 + nt], stt_[0:mm_, 0:nt], 1.0 / 512.0, None, ALU.mult, None, [d_stat], [d_stat])
                tt(DVE, stt_[0:mm_, 12:12 + nt], stt_[0:mm_, 8:8 + nt], stt_[0:mm_, 8:8 + nt], ALU.mult, [d_stat], [d_stat])
                stt(stt_[0:mm_, 12:12 + nt], stt_[0:mm_, 4:4 + nt], 1.0 / 512.0, stt_[0:mm_, 12:12 + nt], ALU.mult, ALU.subtract,
                    [d_stat], [d_stat])
                tsc(DVE, stt_[0:mm_, 12:12 + nt], stt_[0:mm_, 12:12 + nt], EPS, None, ALU.add, None, [d_stat], [d_stat])
                act(stt_[0:mm_, 12:12 + nt], stt_[0:mm_, 12:12 + nt], AF.Ln, [d_stat], [d_stat])
                act(stt_[0:mm_, 12:12 + nt], stt_[0:mm_, 12:12 + nt], AF.Exp, [d_stat], [d_stat], scale=-0.5)
                stt(stt_[0:mm_, 16:16 + nt], stt_[0:mm_, 8:8 + nt], -1.0, stt_[0:mm_, 12:12 + nt], ALU.mult, ALU.mult, [d_stat], [d_stat])
                d_gvt = [Dep() for _ in range(ntile)]
                for t_ in range(ntile):
                    m = mt_[t_]
                    act(gv[0:m, t_, :], gv[0:m, t_, :], AF.Identity, [d_gv, d_stat], [d_gvt[t_]],
                        scale=stt_[0:m, 12 + t_:13 + t_], bias=stt_[0:m, 16 + t_:17 + t_])
                for t_ in range(ntile):
                    m = mt_[t_]
                    tt(DVE, gv[0:m, t_, :], gv[0:m, t_, :], gbc[0:m, :], ALU.mult, [d_gvt[t_], d_lw], [d_gvt[t_]])
                for t_ in range(ntile):
                    m = mt_[t_]
                    d_gv_t = d_gvt[t_]
                    if seg == 1:
                        tt(DVE, gv[0:m, t_, :], gv[0:m, t_, :], bbc[0:m, :], ALU.add, [d_gv_t, d_lw], [d_gv_t, d_gv])
                        out_dma(sgv[l], gv[0:m, t_, :], [d_gv_t, d_gv])
                        act(vb[0:m, t_, :], gv[0:m, t_, :], AF.Copy, [d_gv_t], [d_vb])
                    else:
                        tt(DVE, vb[0:m, t_, :], gv[0:m, t_, :], bbc[0:m, :], ALU.add, [d_gv_t, d_lw], [d_vb, d_gv])
                if part == 0:
                    return
                spatial_part(bj)

            p3_load(0)
            v_pass1(0)
            au_ag(0)
            stl0 = {"zs": {}, "frs": {}}
            for c in range(4):
                lru_step(0, c, stl0)
            mq_tiles(0)
            mem_pair(0, 0)
            mem_pair(0, 1)
            v_rest_spatial(0)

            pending_ln = []
            late_cast = []
            d_yblk = Dep("yblk")
            for bi, (n0, N, seg) in enumerate(BLOCKS):
                ntile = (N + 127) // 128
                xr_, dxr = xres[0], d_xres[0]
                nxt = bi + 1 if bi + 1 < len(BLOCKS) else None
                if nxt is not None:
                    p3_load(nxt)
                ob, d_ob, ocb, d_ocb, om, d_om = obs[bi % 2], d_obs[bi % 2], ocbs[bi % 2], d_ocbs[bi % 2], oms[bi % 2], d_oms[bi % 2]
                oa, d_oa = oas[bi % 2], d_oas[bi % 2]
                if debug:
                    srcs = [(oa, d_oa, 4, 0), (ob, d_ob, 4, 4), (ocb, d_ocb, 4, 8), (om, d_om, 2, 12)]
                    for (tb_, dtb, nch, k0) in srcs:
                        for c in range(nch):
                            t32, dt32 = T3()
                            cp(DVE, t32[:, 0:N], tb_[:, c, 0:N], [dtb], [dt32])
                            out_dma(dbg["br"][l][:, k0 + c, n0:n0 + N], t32[:, 0:N], [dt32])
                branches = [(oa, d_oa, 0, 4), (ob, d_ob, 4, 4), (ocb, d_ocb, 8, 4), (om, d_om, 12, 2)]
                stl = {"zs": {}, "frs": {}}
                for j in range(8):
                    wb_, dwb, rib = ringB.get(("WBR", l, j))
                    ma, dma_ = macc[j % 2], d_macc[j % 2]
                    for br in range(4):
                        pgt, dpgt = zmm_b(("W3", l, 22 + j * 4 + br), bi)
                        tg, dtg = T3()
                        act(tg[:, 0:N], pgt[:, 0:N], AF.Tanh, [dpgt], [dtg], scale=0.5)
                        src, dsrc, k0, nk = branches[br]
                        py, dpy = nb()
                        mmg([(py[:, 0:N], wb_[:, (k0 + k) * 128:(k0 + k + 1) * 128], src[:, k, 0:N], k == 0, k == nk - 1)
                             for k in range(nk)], [dwb, dsrc], [dpy])
                        if br == 0:
                            stt(ma[:, 0:N], tg[:, 0:N], 1.0, py[:, 0:N], ALU.add, ALU.mult, [dtg, dpy], [dma_])
                        else:
                            stt(tg[:, 0:N], tg[:, 0:N], 1.0, py[:, 0:N], ALU.add, ALU.mult, [dtg, dpy], [dtg])
                            tt(POOL, ma[:, 0:N], ma[:, 0:N], tg[:, 0:N], ALU.add, [dma_, dtg], [dma_])
                    ringB.rel(rib)
                    act(mg[:, j, 0:N], ma[:, 0:N], AF.Copy, [dma_], [d_mg])
                    if debug:
                        out_dma(dbg["mg"][l][:, j, n0:n0 + N], ma[:, 0:N], [dma_])
                    if nxt is not None:
                        if j < 4:
                            lru_step(nxt, j, stl)
                        elif j == 4:
                            mq_tiles(nxt)
                            mem_pair(nxt, 0)
                        elif j == 5:
                            mem_pair(nxt, 1)
                            v_pass1(nxt)
                        elif j == 6:
                            au_ag(nxt)
                            v_rest_spatial(nxt, part=0)
                        elif j == 7:
                            v_rest_spatial(nxt, part=1)
                    if pending_ln and j < 3:
                        pending_ln.pop(0)()
                    if l == 0 and j == 3:
                        emit_conv_some(11 if bi + 1 < len(BLOCKS) else 1000, conv_order_l1)
                    if j == 5:
                        p3_load_res(bi)
                    if j == 6 and late_cast:
                        late_cast.pop(0)()
                act(xr_[:, :, 0:N], xr_[:, :, 0:N], AF.Copy, [dxr], [dxr], scale=ALPHA)
                s1, ds1 = banks[6], dbank[6]
                s2b, ds2b = banks[7], dbank[7]
                pend = None
                for j in range(8):
                    wo_, dwo, rio = ringA.get(("WOUT", l, j))
                    pyo, dpyo = nb()
                    mmg([(pyo[:, 0:N], wo_[:, k * 128:(k + 1) * 128], mg[:, k, 0:N], k == 0, k == 7) for k in range(8)],
                        [dwo, d_mg], [dpyo])
                    ringA.rel(rio)
                    stt(xr_[:, j, 0:N], pyo[:, 0:N], 0.5, xr_[:, j, 0:N], ALU.mult, ALU.add, [dpyo, dxr], [dxr])
                    ta_, dta_ = tb16[0], d_tb16[0]
                    tq_, dtq_ = tb16[1], d_tb16[1]
                    if pend is not None:
                        pend()
                    act(ta_[:, 0:N], xr_[:, j, 0:N], AF.Copy, [dxr], [dta_])
                    act(tq_[:, 0:N], xr_[:, j, 0:N], AF.Square, [dxr], [dtq_])

                    def stats(j=j, ta_=ta_, dta_=dta_, tq_=tq_, dtq_=dtq_):
                        mmg([(s1[:, 0:N], ones_bf[:], ta_[:, 0:N], j == 0, j == 7)], [d_const, dta_], [ds1])
                        mmg([(s2b[:, 0:N], ones_bf[:], tq_[:, 0:N], j == 0, j == 7)], [d_const, dtq_], [ds2b])
                    pend = stats
                pend()
                if debug:
                    out_dma(dbg["t"][l][:, :, n0:n0 + N], xr_[:, :, 0:N], [dxr])
                tm, dtm = tmn[0], d_tmn[0]
                tn, dtn = tmn[1], d_tmn[1]
                tsc(DVE, tm[:, 0:N], s1[:, 0:N], 1.0 / 1024.0, None, ALU.mult, None, [ds1], [dtm])
                tt(DVE, tn[:, 0:N], tm[:, 0:N], tm[:, 0:N], ALU.mult, [dtm], [dtn])
                stt(tn[:, 0:N], s2b[:, 0:N], 1.0 / 1024.0, tn[:, 0:N], ALU.mult, ALU.subtract, [ds2b, dtn], [dtn])
                tsc(DVE, tn[:, 0:N], tn[:, 0:N], EPS, None, ALU.add, None, [dtn], [dtn])
                act(tn[:, 0:N], tn[:, 0:N], AF.Ln, [dtn], [dtn])
                act(tn[:, 0:N], tn[:, 0:N], AF.Exp, [dtn], [dtn], scale=-0.5)
                stt(tm[:, 0:N], tm[:, 0:N], -1.0, tn[:, 0:N], ALU.mult, ALU.mult, [dtm, dtn], [dtm])

                def ln_b1(N=N, xr_=xr_, dxr=dxr, tn=tn, dtn=dtn):
                    tt(DVE, xr_[:, :, 0:N], xr_[:, :, 0:N], tn[:, 0:N].unsqueeze(1).to_broadcast([128, 8, N]), ALU.mult,
                       [dxr, dtn], [dxr])

                def ln_b2(N=N, xr_=xr_, dxr=dxr, tm=tm, dtm=dtm):
                    tt(DVE, xr_[:, :, 0:N], xr_[:, :, 0:N], tm[:, 0:N].unsqueeze(1).to_broadcast([128, 8, N]), ALU.add,
                       [dxr, dtm], [dxr])

                def ln_c(N=N, n0=n0, xr_=xr_, dxr=dxr):
                    for j in range(8):
                        tsc(POOL, xr_[:, j, 0:N], xr_[:, j, 0:N], pv[:, 35 + j:36 + j], pv[:, 43 + j:44 + j], ALU.mult, ALU.add,
                            [dxr, d_lw], [dxr])
                    s_ = s_out[oi["i"] % 4]
                    oi["i"] += 1
                    P.dma(POOL, ydst[:, :, n0:n0 + N], xr_[:, :, 0:N], s_, reads=[dxr], writes=[d_yblk])

                def cast_late(N=N, n0=n0):
                    P.dma(POOL, xbf_d[l + 1][:, :, n0:n0 + N], xres_d[:, :, n0:n0 + N], s_out[oi["i"] % 4], reads=[d_yblk])
                    oi["i"] += 1

                if nxt is not None:
                    pending_ln.extend([ln_b1, ln_b2, ln_c])
                    if l + 1 < NL:
                        late_cast.append(cast_late)
                else:
                    ln_b1()
                    ln_b2()
                    ln_c()
                    if l + 1 < NL:
                        late_cast.append(cast_late)
                    while late_cast:
                        late_cast.pop(0)()
            P.barrier()

    if debug:
        out_dma(dbg["x1"], xres_d, [])
    for s in s_out:
        if P.dsem_val[s] > 0:
            P.q[POOL].append((lambda s=s, v=P.dsem_val[s]: nc.gpsimd.wait_ge(s, v)))
    assert not conv_pending(), conv_pending()
    if sched is None:
        st.close()
        return [ringA.rec, ringB.rec]
    assert ringA.consumed == len(ringA.sched) and ringB.consumed == len(ringB.sched)
    P.run()
    st.close()
    return nc


def _fm(a):
    T, F = a.shape
    return np.ascontiguousarray(a.T.reshape(F // 128, 128, T).transpose(1, 0, 2))


def _wtile(w):
    n = w.shape[1]
    t = np.zeros((128, 8, 128), np.float32)
    t[:, :, :n] = w.reshape(8, 128, n).transpose(1, 0, 2)
    return t.reshape(128, 1024)


def _prep_shared(inp):
    f32 = np.float32
    w_in = inp["w_in"]
    sw = np.concatenate([np.arange(16, 32), np.arange(0, 16)])
    W1 = np.zeros((NL, 5, 128, 1024), f32)
    WAV = np.zeros((NL, 128, 4096), f32)
    W3 = np.zeros((NL, N_W3, 128, 1024), f32)
    WBR = np.zeros((NL, 8, 128, 1792), f32)
    WOUT = np.zeros((NL, 8, 128, 1024), f32)
    WUQ = np.zeros((NL, 128, 1536), f32)
    WUQS = np.zeros((NL, 128, 1536), f32)
    WUKV = np.zeros((NL, 128, 1024), f32)
    MWK = np.zeros((NL, 128, 2048), f32)
    MWV = np.zeros((NL, 128, 2048), f32)
    WR = np.zeros((NL, 128, 512), f32)
    WI = np.zeros((NL, 128, 512), f32)
    WST = np.zeros((NL, 128, 512), f32)
    PV = np.zeros((NL, 128, NPV), f32)
    GLN = np.zeros((NL, 2, 512), f32)
    BS = np.zeros((NL, 1, 512), f32)
    cols3 = w3_tile_cols()
    for l in range(NL):
        w = w_in[l]
        W1[l, 0] = _wtile(w[:, 2560:2688])
        W1[l, 1] = _wtile(w[:, 2688:2816])
        W1[l, 2] = _wtile(w[:, 2816:2944])
        kr = np.zeros((1024, 128), f32)
        kr[:, 64:96] = w[:, 2944:2976]
        W1[l, 3] = _wtile(kr)
        krs = np.zeros((1024, 128), f32)
        krs[:, 64:96] = w[:, 2944 + sw]
        W1[l, 4] = _wtile(krs)
        WAV[l] = w[:, 512:1024].reshape(8, 128, 512).transpose(1, 0, 2).reshape(128, 4096)
        for i, cc in enumerate(cols3):
            W3[l, i] = _wtile(w[:, cc])
        wbr = inp["w_br"][l]
        for j in range(8):
            WBR[l, j] = wbr[:, j * 128:(j + 1) * 128].reshape(14, 128, 128).transpose(1, 0, 2).reshape(128, 1792)
            WOUT[l, j] = _wtile(inp["w_out"][l][:, j * 128:(j + 1) * 128])
        wuq = inp["mla_w_uq"][l]
        WUQ[l] = wuq.reshape(2, 128, 768).transpose(1, 0, 2).reshape(128, 1536)
        wuqs = np.zeros_like(wuq)
        for h in range(8):
            wuqs[:, h * 96 + 64:h * 96 + 96] = wuq[:, h * 96 + 64 + sw]
        WUQS[l] = wuqs.reshape(2, 128, 768).transpose(1, 0, 2).reshape(128, 1536)
        WUKV[l] = inp["mla_w_ukv"][l]
        MWK[l] = inp["mem_w_k"][l].reshape(8, 128, 256).transpose(1, 0, 2).reshape(128, 2048)
        MWV[l] = inp["mem_w_v"][l].reshape(8, 128, 256).transpose(1, 0, 2).reshape(128, 2048)
        for c in range(4):
            for hh in range(2):
                hb = 2 * c + hh
                WR[l, hh * 64:(hh + 1) * 64, c * 128 + hh * 64:c * 128 + (hh + 1) * 64] = inp["lru_w_r"][l][hb]
                WI[l, hh * 64:(hh + 1) * 64, c * 128 + hh * 64:c * 128 + (hh + 1) * 64] = inp["lru_w_i"][l][hb]
        WST[l] = inp["gmlp_ws"][l].transpose(2, 0, 1).reshape(128, 512)
        fmv = lambda v: v.reshape(-1, 128).T
        PV[l, :, 0:16] = inp["lru_conv_w"][l].reshape(4, 4, 128).transpose(2, 1, 0).reshape(128, 16)
        PV[l, :, 16:20] = fmv(inp["lru_conv_b"][l])
        PV[l, :, 20:24] = fmv(inp["lru_b_r"][l])
        PV[l, :, 24:28] = fmv(inp["lru_b_i"][l])
        PV[l, :, 28:32] = fmv(inp["lru_lambda"][l])
        PV[l, :, 32:34] = fmv(inp["mla_q_norm"][l])
        PV[l, :, 34:35] = fmv(inp["mla_kv_norm"][l])
        PV[l, :, 35:43] = fmv(inp["ln_g"][l])
        PV[l, :, 43:51] = fmv(inp["ln_b"][l])
        GLN[l, 0] = inp["gmlp_ln_g"][l]
        GLN[l, 1] = inp["gmlp_ln_b"][l]
        BS[l, 0] = inp["gmlp_bs"][l].reshape(512)
    pos = np.concatenate([np.arange(TP), PAST + np.arange(TS)]).astype(np.float32)
    freq = (np.float32(10000.0) ** (-np.arange(16, dtype=np.float32) / np.float32(16))).astype(np.float32)
    ang = pos[None, :] * freq[:, None]
    cosT = np.concatenate([np.cos(ang), np.cos(ang)], 0).astype(f32)
    sinT = np.concatenate([-np.sin(ang), np.sin(ang)], 0).astype(f32)
    maskT = (np.arange(128)[:, None] <= np.arange(128)[None, :]).astype(f32)
    return dict(W1=W1, WAV=WAV, W3=W3, WBR=WBR, WOUT=WOUT, WUQ=WUQ, WUQS=WUQS, WUKV=WUKV, MWK=MWK, MWV=MWV,
                WR=WR, WI=WI, WST=WST, PV=PV, GLN=GLN, BS=BS, cosT=cosT, sinT=sinT, maskT=maskT)


def _prep_core(inp, c):
    b = c % 4
    f32 = np.float32
    xtok = np.concatenate([inp["x_prompt"][b], inp["x_sample"][c]], 0)
    m = {"xT": _fm(xtok)}
    m["memT"] = _fm(inp["mem_prompt"][b])
    cmk = inp["cache_mem_k"][:, c].reshape(NL, 256, 256)
    m["cmkT"] = np.ascontiguousarray(cmk.transpose(0, 2, 1).reshape(NL, 2, 128, 256).transpose(0, 2, 1, 3))
    cmvv = inp["cache_mem_v"][:, c].reshape(NL, 256, 256)
    m["cmv"] = np.ascontiguousarray(cmvv.reshape(NL, 2, 128, 256).transpose(0, 2, 1, 3))
    m["cckvT"] = np.ascontiguousarray(inp["cache_mla_ckv"][:, c].transpose(0, 2, 1))
    m["ckrT"] = np.ascontiguousarray(inp["cache_mla_krope"][:, c].transpose(0, 2, 1))
    m["shin"] = np.ascontiguousarray(inp["state_lru_h"][:, c].reshape(NL, 4, 128).transpose(0, 2, 1))
    m["scin"] = np.ascontiguousarray(inp["state_lru_conv"][:, c].reshape(NL, 3, 4, 128).transpose(0, 3, 2, 1))
    return {k: np.ascontiguousarray(v, dtype=f32) for k, v in m.items()}


_CACHE = {}


def _run(inp, debug=False):
    key = "dbg" if debug else "prog"
    if key not in _CACHE:
        rec = build_program(debug=debug, sched=None)
        _CACHE[key] = build_program(debug=debug, sched=rec)
    nc = _CACHE[key]
    shared = _prep_shared(inp)
    in_maps = []
    for c in range(8):
        m = dict(shared)
        m.update(_prep_core(inp, c))
        in_maps.append(m)
    res = run_bass_kernel_spmd(nc, in_maps, core_ids=list(range(8)))
    return res.results


def _tok(a):
    p, c, t = a.shape
    return np.ascontiguousarray(a.transpose(2, 1, 0).reshape(t, c * p))


def kernel(**inputs):
    inp = {k: np.asarray(v, dtype=np.float32) for k, v in inputs.items()}
    r = _run(inp)
    f32 = np.float32
    y_prompt = np.stack([_tok(r[b]["yT"][:, :, :TP]) for b in range(4)]).astype(f32)
    y_sample = np.stack([_tok(r[c]["yT"][:, :, TP:]) for c in range(8)]).astype(f32)
    p_ckv = np.stack([np.stack([r[b]["ckvo"][l][:, :TP].T for b in range(4)]) for l in range(NL)]).astype(f32)
    p_kr = np.stack([np.stack([r[b]["kro"][l][:, :TP].T for b in range(4)]) for l in range(NL)]).astype(f32)
    p_mk = np.stack([np.stack([_tok(r[b]["pmk"][l]).reshape(256, 4, 64) for b in range(4)]) for l in range(NL)]).astype(f32)
    p_mv = np.stack([np.stack([r[b]["pmv"][l].transpose(1, 0, 2).reshape(256, 4, 64) for b in range(4)])
                     for l in range(NL)]).astype(f32)
    p_h = np.stack([np.stack([r[b]["lruh"][l, 0].T.reshape(512) for b in range(4)]) for l in range(NL)]).astype(f32)
    p_conv = np.stack([np.stack([r[b]["lruc"][l, 0].transpose(2, 1, 0).reshape(3, 512) for b in range(4)])
                       for l in range(NL)]).astype(f32)
    s_ckv = np.stack([np.stack([r[c]["ckvo"][l][:, TP:].T for c in range(8)]) for l in range(NL)]).astype(f32)
    s_kr = np.stack([np.stack([r[c]["kro"][l][:, TP:].T for c in range(8)]) for l in range(NL)]).astype(f32)
    s_h = np.stack([np.stack([r[c]["lruh"][l, 1].T.reshape(512) for c in range(8)]) for l in range(NL)]).astype(f32)
    s_conv = np.stack([np.stack([r[c]["lruc"][l, 1].transpose(2, 1, 0).reshape(3, 512) for c in range(8)])
                       for l in range(NL)]).astype(f32)
    s_v = np.stack([np.stack([r[c]["sgv"][l] for c in range(8)]) for l in range(NL)]).astype(f32)
    return (y_prompt, y_sample, p_ckv, p_kr, p_mk, p_mv, p_h, p_conv, s_ckv, s_kr, s_h, s_conv, s_v)
```
